# Optimizing a Trainium2 kernel written in Bass

```python
import jax, jax.numpy as jnp
from jax import lax
import numpy as np

D_MODEL = 1024
BATCH = 8
SEQ = 4096
DEPTH = 4
DEC_BATCH = 32
DEC_SEQ = 64
PAST_LEN = 4096

CHUNK = 64
N_MIXERS = 2
N_CONV = len([i for i in range(DEPTH) if i % N_MIXERS == 0])
N_MLSTM = len([i for i in range(DEPTH) if i % N_MIXERS == 1])
CONV_WIDTH = D_MODEL
CONV_K = 31
M_WIDTH = 2 * D_MODEL
M_HEADS = 4
M_HEAD_DIM = M_WIDTH // M_HEADS
M_CONV_K = 4
EPS = 1e-6

kernel_name = 'hybrid_conv_mlstm_stream_step'


def rms_norm(x, g):
    xf = x.astype(jnp.float32)
    y = xf * lax.rsqrt(jnp.mean(xf * xf, axis=-1, keepdims=True) + EPS)
    return (y * g.astype(jnp.float32)).astype(x.dtype)


def layer_norm(x, g, b):
    xf = x.astype(jnp.float32)
    mu = jnp.mean(xf, axis=-1, keepdims=True)
    var = jnp.mean(jnp.square(xf - mu), axis=-1, keepdims=True)
    y = (xf - mu) * lax.rsqrt(var + EPS)
    return (y * g.astype(jnp.float32) + b.astype(jnp.float32)).astype(x.dtype)


def head_norm(h, g):
    mu = jnp.mean(h, axis=-1, keepdims=True)
    var = jnp.mean(jnp.square(h - mu), axis=-1, keepdims=True)
    return (h - mu) * lax.rsqrt(var + EPS) * g.astype(jnp.float32)


def causal_dwconv(x, buf, w, b):
    xp = jnp.concatenate([buf.astype(x.dtype), x], axis=1)
    y = lax.conv_general_dilated(xp, w[:, None, :].astype(x.dtype), window_strides=(1,),
                                 padding='VALID', dimension_numbers=('NWC', 'WIO', 'NWC'),
                                 feature_group_count=x.shape[-1])
    return y + b, xp[:, xp.shape[1] - (w.shape[0] - 1):]


def conv_branch(h, buf, w_in, w_dw, b_dw, ln_g, ln_b, w_out):
    u = jnp.einsum('btd,de->bte', h, w_in)
    a, a_gate, z = jnp.split(u, 3, axis=-1)
    g = a * jax.nn.sigmoid(a_gate)
    y, new_buf = causal_dwconv(g, buf, w_dw, b_dw)
    y = layer_norm(y, ln_g, ln_b)
    y = jax.nn.silu(y) * jax.nn.silu(z)
    return jnp.einsum('bte,ed->btd', y, w_out), new_buf


def mlstm_chunk(carry, inp):
    C, n, m = carry
    q, k, v, ig, lf = inp
    q = q.astype(jnp.float32)
    k = k.astype(jnp.float32) * (M_HEAD_DIM ** -0.5)
    v = v.astype(jnp.float32)
    L = q.shape[1]
    b = jnp.cumsum(lf, axis=1).transpose(0, 2, 1)
    igt = ig.transpose(0, 2, 1)
    a = b + m[..., None]
    causal = jnp.tril(jnp.ones((L, L), dtype=bool))
    logw = jnp.where(causal, b[..., :, None] - b[..., None, :] + igt[..., None, :], -jnp.inf)
    mt = jnp.maximum(a, jnp.max(logw, axis=-1))
    w_inter = jnp.exp(a - mt)
    w_intra = jnp.exp(logw - mt[..., None])
    s = jnp.einsum('bthd,bshd->bhts', q, k) * w_intra
    num = w_inter[..., None] * jnp.einsum('bhvd,bthd->bhtv', C, q) + jnp.einsum('bhts,bshv->bhtv', s, v)
    den = w_inter * jnp.einsum('bhd,bthd->bht', n, q) + jnp.sum(s, axis=-1)
    den = jnp.maximum(jnp.abs(den), jnp.exp(-mt))
    hh = (num / den[..., None]).transpose(0, 2, 1, 3)
    m_new = mt[..., -1]
    decay = jnp.exp(b[..., -1] + m - m_new)
    wk = jnp.exp(b[..., -1:] - b + igt - m_new[..., None])
    C_new = decay[..., None, None] * C + jnp.einsum('bhs,bshv,bshd->bhvd', wk, v, k)
    n_new = decay[..., None] * n + jnp.einsum('bhs,bshd->bhd', wk, k)
    return (C_new, n_new, m_new), hh


def mlstm_recurrence(q, k, v, ig, lf, C, n, m):
    B, T, H, DH = q.shape
    L = CHUNK if T % CHUNK == 0 else T
    nc = T // L

    def to_chunks(t):
        return jnp.moveaxis(t.reshape((B, nc, L) + t.shape[2:]), 1, 0)

    (C, n, m), hs = lax.scan(mlstm_chunk, (C, n, m),
                             (to_chunks(q), to_chunks(k), to_chunks(v), to_chunks(ig), to_chunks(lf)))
    return jnp.moveaxis(hs, 0, 1).reshape(B, T, H, DH), C, n, m


def mlstm_branch(h, conv_buf, C, n, m, w_in, w_conv, b_conv, w_q, w_k, w_v, w_gate, b_gate,
                 gn_g, skip, w_out):
    B, T, _ = h.shape
    u = jnp.einsum('btd,de->bte', h, w_in)
    xm, z, o = jnp.split(u, 3, axis=-1)
    xc, new_buf = causal_dwconv(xm, conv_buf, w_conv, b_conv)
    xc = jax.nn.silu(xc)
    xch = xc.reshape(B, T, M_HEADS, M_HEAD_DIM)
    xmh = xm.reshape(B, T, M_HEADS, M_HEAD_DIM)
    q = jnp.einsum('bthd,hde->bthe', xch, w_q)
    k = jnp.einsum('bthd,hde->bthe', xch, w_k)
    v = jnp.einsum('bthd,hde->bthe', xmh, w_v)
    gpre = (jnp.einsum('bthd,hdg->btg', q, w_gate[0]) + jnp.einsum('bthd,hdg->btg', k, w_gate[1])
            + jnp.einsum('bthd,hdg->btg', v, w_gate[2]) + b_gate).astype(jnp.float32)
    ig, fg = jnp.split(gpre, 2, axis=-1)
    lf = jax.nn.log_sigmoid(fg)
    hh, C, n, m = mlstm_recurrence(q, k, v, ig, lf, C, n, m)
    hh = jax.nn.sigmoid(o.astype(jnp.float32)).reshape(B, T, M_HEADS, M_HEAD_DIM) * hh
    hn = head_norm(hh, gn_g.reshape(M_HEADS, M_HEAD_DIM)).reshape(B, T, M_WIDTH).astype(h.dtype)
    y = (hn + skip * xc) * jax.nn.silu(z)
    return jnp.einsum('bte,ed->btd', y, w_out), new_buf, C, n, m


def run_trunk(x, c, conv_buf, mconv_buf, C0, n0, m0, P):
    new_conv, new_mconv, new_C, new_n, new_m = [], [], [], [], []
    for i in range(DEPTH):
        mod = jnp.einsum('bd,de->be', c, P['ada_w'][i]) + P['ada_b'][i]
        shift, scale, gate = jnp.split(mod[:, None, :], 3, axis=-1)
        h = rms_norm(x, P['norm_g'][i]) * (1 + scale) + shift
        j = i // N_MIXERS
        if i % N_MIXERS == 0:
            out, nb = conv_branch(h, conv_buf[j], P['cv_w_in'][j], P['cv_w_dw'][j], P['cv_b_dw'][j],
                                  P['cv_ln_g'][j], P['cv_ln_b'][j], P['cv_w_out'][j])
            new_conv.append(nb)
        else:
            out, nb, C, n, m = mlstm_branch(
                h, mconv_buf[j], C0[j].astype(jnp.float32), n0[j].astype(jnp.float32),
                m0[j].astype(jnp.float32), P['ml_w_in'][j], P['ml_w_conv'][j], P['ml_b_conv'][j],
                P['ml_w_q'][j], P['ml_w_k'][j], P['ml_w_v'][j], P['ml_w_gate'][j], P['ml_b_gate'][j],
                P['ml_gn_g'][j], P['ml_skip'][j], P['ml_w_out'][j])
            new_mconv.append(nb)
            new_C.append(C)
            new_n.append(n)
            new_m.append(m)
        x = x + gate * out
    y = rms_norm(x, P['final_g'])
    return y, jnp.stack(new_conv), jnp.stack(new_mconv), jnp.stack(new_C), jnp.stack(new_n), jnp.stack(new_m)


def setup_inputs(seed: int = 0) -> dict:
    key = jax.random.key(seed)
    ks = iter(jax.random.split(key, 40))
    nrm = lambda shape, s: jax.random.normal(next(ks), shape, jnp.float32) * s
    H, DH = M_HEADS, M_HEAD_DIM
    b_gate_i = nrm((N_MLSTM, H), 0.1)
    b_gate_f = jnp.linspace(3.0, 6.0, H, dtype=jnp.float32)[None, :] + nrm((N_MLSTM, H), 0.1)
    return {
        'x_prompt': nrm((BATCH, SEQ, D_MODEL), 1.0),
        'x_sample': nrm((DEC_BATCH, DEC_SEQ, D_MODEL), 1.0),
        'c_prompt': nrm((BATCH, D_MODEL), 1.0),
        'c_sample': nrm((DEC_BATCH, D_MODEL), 1.0),
        'state_conv': nrm((N_CONV, DEC_BATCH, CONV_K - 1, CONV_WIDTH), 0.5),
        'state_mconv': nrm((N_MLSTM, DEC_BATCH, M_CONV_K - 1, M_WIDTH), 0.5),
        'state_C': nrm((N_MLSTM, DEC_BATCH, H, DH, DH), 0.1),
        'state_n': nrm((N_MLSTM, DEC_BATCH, H, DH), 0.1),
        'state_m': nrm((N_MLSTM, DEC_BATCH, H), 0.5),
        'norm_g': 1.0 + nrm((DEPTH, D_MODEL), 0.01),
        'ada_w': nrm((DEPTH, D_MODEL, 3 * D_MODEL), 0.2 * D_MODEL ** -0.5),
        'ada_b': nrm((DEPTH, 3 * D_MODEL), 0.02),
        'cv_w_in': nrm((N_CONV, D_MODEL, 3 * CONV_WIDTH), D_MODEL ** -0.5),
        'cv_w_dw': nrm((N_CONV, CONV_K, CONV_WIDTH), CONV_K ** -0.5),
        'cv_b_dw': nrm((N_CONV, CONV_WIDTH), 0.01),
        'cv_ln_g': 1.0 + nrm((N_CONV, CONV_WIDTH), 0.01),
        'cv_ln_b': nrm((N_CONV, CONV_WIDTH), 0.01),
        'cv_w_out': nrm((N_CONV, CONV_WIDTH, D_MODEL), CONV_WIDTH ** -0.5),
        'ml_w_in': nrm((N_MLSTM, D_MODEL, 3 * M_WIDTH), D_MODEL ** -0.5),
        'ml_w_conv': nrm((N_MLSTM, M_CONV_K, M_WIDTH), M_CONV_K ** -0.5),
        'ml_b_conv': nrm((N_MLSTM, M_WIDTH), 0.01),
        'ml_w_q': nrm((N_MLSTM, H, DH, DH), DH ** -0.5),
        'ml_w_k': nrm((N_MLSTM, H, DH, DH), DH ** -0.5),
        'ml_w_v': nrm((N_MLSTM, H, DH, DH), DH ** -0.5),
        'ml_w_gate': nrm((N_MLSTM, 3, H, DH, 2 * H), (3 * M_WIDTH) ** -0.5),
        'ml_b_gate': jnp.concatenate([b_gate_i, b_gate_f], axis=-1),
        'ml_gn_g': 1.0 + nrm((N_MLSTM, M_WIDTH), 0.01),
        'ml_skip': 1.0 + nrm((N_MLSTM, M_WIDTH), 0.01),
        'ml_w_out': nrm((N_MLSTM, M_WIDTH, D_MODEL), M_WIDTH ** -0.5),
        'final_g': 1.0 + nrm((D_MODEL,), 0.01),
    }


def reference(x_prompt, x_sample, c_prompt, c_sample, state_conv, state_mconv, state_C, state_n, state_m,
              norm_g, ada_w, ada_b, cv_w_in, cv_w_dw, cv_b_dw, cv_ln_g, cv_ln_b, cv_w_out,
              ml_w_in, ml_w_conv, ml_b_conv, ml_w_q, ml_w_k, ml_w_v, ml_w_gate, ml_b_gate,
              ml_gn_g, ml_skip, ml_w_out, final_g):
    P = dict(norm_g=norm_g, ada_w=ada_w, ada_b=ada_b, cv_w_in=cv_w_in, cv_w_dw=cv_w_dw, cv_b_dw=cv_b_dw,
             cv_ln_g=cv_ln_g, cv_ln_b=cv_ln_b, cv_w_out=cv_w_out, ml_w_in=ml_w_in, ml_w_conv=ml_w_conv,
             ml_b_conv=ml_b_conv, ml_w_q=ml_w_q, ml_w_k=ml_w_k, ml_w_v=ml_w_v, ml_w_gate=ml_w_gate,
             ml_b_gate=ml_b_gate, ml_gn_g=ml_gn_g, ml_skip=ml_skip, ml_w_out=ml_w_out, final_g=final_g)
    Bp = x_prompt.shape[0]
    z_conv = jnp.zeros((N_CONV, Bp, CONV_K - 1, CONV_WIDTH), x_prompt.dtype)
    z_mconv = jnp.zeros((N_MLSTM, Bp, M_CONV_K - 1, M_WIDTH), x_prompt.dtype)
    z_C = jnp.zeros((N_MLSTM, Bp, M_HEADS, M_HEAD_DIM, M_HEAD_DIM), jnp.float32)
    z_n = jnp.zeros((N_MLSTM, Bp, M_HEADS, M_HEAD_DIM), jnp.float32)
    z_m = jnp.zeros((N_MLSTM, Bp, M_HEADS), jnp.float32)
    y_prompt, p_conv, p_mconv, p_C, p_n, p_m = run_trunk(x_prompt, c_prompt, z_conv, z_mconv, z_C, z_n, z_m, P)
    y_sample, s_conv, s_mconv, s_C, s_n, s_m = run_trunk(x_sample, c_sample, state_conv, state_mconv,
                                                         state_C, state_n, state_m, P)
    return (y_prompt, y_sample, p_conv, p_mconv, p_C, p_n, p_m, s_conv, s_mconv, s_C, s_n, s_m)
```

```python
import contextlib
import os
import numpy as np
import concourse.bass as bass
import concourse.mybir as mybir
from concourse.bass_utils import run_bass_kernel_spmd

F32 = mybir.dt.float32
BF16 = mybir.dt.bfloat16
AF = mybir.ActivationFunctionType
ALU = mybir.AluOpType

D = 1024
TP = 4096
NS = 4
LS = 64
NT = TP + NS * LS
NSEQ = 5
DEPTH = 4
EPS = 1e-6
CK = 31
MW = 2048
H = 4
DH = 512
MK = 4
KSCALE = DH ** -0.5
NEG = -30000.0

_SM = {}
_o = 0
for _n, _w in [("norm_g", 32), ("ada_b", 96), ("cv_w_dw", 2 * 8 * 31), ("cv_b_dw", 16), ("cv_ln_g", 16),
               ("cv_ln_b", 16), ("ml_w_conv", 2 * 16 * 4), ("ml_b_conv", 32), ("ml_gn_g", 32), ("ml_skip", 32),
               ("final_g", 8), ("b_gate", 2)]:
    _SM[_n] = _o
    _o += _w
NSMALL = _o
C_ID = 0
C_ONES = 128
C_MASK = 256
C_SELROW = 384
C_SELF = 896
NCONST = 900
NCONSTB = 256


class Sched:
    ENG = ["pe", "act", "dve", "pool", "sp"]

    def __init__(self, nc):
        self.nc = nc
        self.prog = {e: [] for e in self.ENG}
        self.cnt = {e: 0 for e in self.ENG}
        self.seen = {e: {} for e in self.ENG}
        self.lastw = {}
        self.rd = {}
        self.dcnt = {}

    def op(self, eng, fn, reads=(), writes=(), dma=None):
        psr = [r for r in reads if isinstance(r, tuple) and len(r) == 2 and r[0] == "ps"]
        if psr:
            writes = list(writes) + [r for r in psr if r not in writes]
        deps = {}

        def add(ev):
            if ev is None:
                return
            k, v = ev
            if k == "pe" and eng == "pe":
                return
            if deps.get(k, 0) < v:
                deps[k] = v

        for r in reads:
            add(self.lastw.get(r))
        for w in writes:
            add(self.lastw.get(w))
            for ev in self.rd.get(w, {}).items():
                add(ev)
        waits = []
        for k, v in deps.items():
            if self.seen[eng].get(k, 0) >= v:
                continue
            self.seen[eng][k] = v
            waits.append((k, v))
        if dma is None:
            self.cnt[eng] += 1
            ev = (eng, self.cnt[eng])
        else:
            self.dcnt[dma] = self.dcnt.get(dma, 0) + 16
            ev = (dma, self.dcnt[dma])
        self.prog[eng].append((fn, waits, ev))
        for w in writes:
            self.lastw[w] = ev
            self.rd[w] = {}
        for r in reads:
            d = self.rd.setdefault(r, {})
            if d.get(ev[0], 0) < ev[1]:
                d[ev[0]] = ev[1]
        return ev

    def barrier(self):
        tot = dict(self.cnt)
        tot.update(self.dcnt)
        for e in self.ENG:
            waits = []
            for k, v in tot.items():
                if v <= 0 or k == e:
                    continue
                if self.seen[e].get(k, 0) >= v:
                    continue
                self.seen[e][k] = v
                waits.append((k, v))
            if waits:
                self.prog[e].append((None, waits, None))
        self.lastw = {}
        self.rd = {}

    def emit(self):
        nc = self.nc
        semnames = set(self.ENG) | set(self.dcnt.keys())
        with contextlib.ExitStack() as st:
            sems = {k: st.enter_context(nc.semaphore("s_" + k)) for k in sorted(semnames)}
            block = st.enter_context(nc.Block())
            fin = []
            for k in sorted(semnames):
                v = self.cnt[k] if k in self.cnt else self.dcnt[k]
                if v > 0 and k != "sp":
                    fin.append((k, v))

            def run(engname, eng, extra=None):
                for fn, waits, ev in self.prog[engname]:
                    for k, v in waits:
                        eng.wait_ge(sems[k], v)
                    if fn is None:
                        continue
                    ins = fn(eng)
                    ins.then_inc(sems[ev[0]], 16 if ev[0] not in self.cnt else 1)
                if extra:
                    for k, v in extra:
                        eng.wait_ge(sems[k], v)

            @block.tensor
            def _(e):
                run("pe", e)

            @block.scalar
            def _(e):
                run("act", e)

            @block.vector
            def _(e):
                run("dve", e)

            @block.gpsimd
            def _(e):
                run("pool", e)

            @block.sync
            def _(e):
                run("sp", e, fin)


class Arena:
    def __init__(self, tens, cap_bytes):
        self.t = tens
        self.cap = cap_bytes
        self.off = 0
        self.marks = []

    def alloc(self, free_shape, dt, parts=128):
        n = int(np.prod(free_shape))
        nb = n * (4 if dt == F32 else 2)
        nb = (nb + 63) // 64 * 64
        assert self.off + nb <= self.cap, ("SBUF arena overflow", self.off, nb, self.cap)
        o32 = self.off // 4
        ap = self.t[:, o32:o32 + nb // 4]
        if dt != F32:
            ap = ap.bitcast(dt)
        ap = ap[:, 0:n]
        if len(free_shape) == 2:
            ap = ap.rearrange("p (a b) -> p a b", a=free_shape[0])
        elif len(free_shape) == 3:
            ap = ap.rearrange("p (a b c) -> p a b c", a=free_shape[0], b=free_shape[1])
        elif len(free_shape) == 4:
            ap = ap.rearrange("p (a b c d) -> p a b c d", a=free_shape[0], b=free_shape[1], c=free_shape[2])
        if parts != 128:
            ap = ap[0:parts]
        self.off += nb
        return ap

    def mark(self):
        self.marks.append(self.off)

    def release(self):
        if os.environ.get("KVERB"):
            print("arena high-water at release:", self.off, "of", self.cap)
        self.off = self.marks.pop()


def seq_gcol(s):
    return 0 if s == 0 else TP + (s - 1) * LS


def make_blocks(nb):
    blocks = []
    for b in range(TP // nb):
        blocks.append((b * nb, [dict(seq=0, c0=0, L=nb, t0=b * nb, first=(b == 0), last=(b == TP // nb - 1))]))
    per = nb // LS
    for b in range(NS * LS // nb):
        segs = []
        for i in range(per):
            segs.append(dict(seq=1 + b * per + i, c0=i * LS, L=LS, t0=0, first=True, last=True))
        blocks.append((TP + b * nb, segs))
    return blocks


def build(nlayers=DEPTH):
    nc = bass.Bass("TRN2", target_bir_lowering=False)

    def din(name, shape, dt=F32):
        return nc.dram_tensor(name, list(shape), dt, kind="ExternalInput").ap()

    def dout(name, shape, dt=F32):
        return nc.dram_tensor(name, list(shape), dt, kind="ExternalOutput").ap()

    def dint(name, shape, dt=F32):
        return nc.dram_tensor(name, list(shape), dt, kind="Internal").ap()

    xT = din("xT", [D, NT])
    cT = din("cT", [D, NSEQ])
    sconvT = din("sconvT", [2, NS, D, CK - 1])
    smconvT = din("smconvT", [2, NS, MW, MK - 1])
    sCT = din("sCT", [2, NS, H, DH, DH])
    snT = din("snT", [2, 128, NS * H * 4])
    smT = din("smT", [2, H, NS])
    smallp_d = din("smallp", [128, NSMALL])
    constp_d = din("constp", [128, NCONST])
    ada_w = din("ada_w", [DEPTH, D, 3 * D])
    cv_w_in = din("cv_w_in", [2, D, 3 * D])
    cv_w_out = din("cv_w_out", [2, D, D])
    ml_w_in = din("ml_w_in", [2, D, 3 * MW])
    ml_w_q = din("ml_w_q", [2, H, DH, DH])
    ml_w_k = din("ml_w_k", [2, H, DH, DH])
    ml_w_v = din("ml_w_v", [2, H, DH, DH])
    wgate_d = din("wgate", [2, 128, 3 * 16 * 8])
    ml_w_out = din("ml_w_out", [2, MW, D])

    yT = dout("yT", [D, NT])
    oconvT = dout("oconvT", [2, NSEQ, D, CK - 1])
    omconvT = dout("omconvT", [2, NSEQ, MW, MK - 1])
    oCT = dout("oCT", [2, NSEQ, H, DH, DH])
    onT = dout("onT", [2, 128, NSEQ * H * 4])
    omT = dout("omT", [2, H, NSEQ])

    xres = dint("xres", [D, NT])
    XM = dint("XMs", [MW, NT], BF16)
    ZZ = dint("ZZs", [MW, NT], BF16)
    OO = dint("OOs", [NT, MW], BF16)
    YY = dint("YYs", [MW, NT], BF16)

    xT_v = xT.rearrange("(c p) t -> p c t", p=128)
    xres_v = xres.rearrange("(c p) t -> p c t", p=128)
    yT_v = yT.rearrange("(c p) t -> p c t", p=128)
    XM_v = XM.rearrange("(c p) t -> p c t", p=128)
    ZZ_v = ZZ.rearrange("(c p) t -> p c t", p=128)
    YY_v = YY.rearrange("(c p) t -> p c t", p=128)

    CAP = int(os.environ.get('KCAP', '207')) * 1024
    with contextlib.ExitStack() as st:
        arena_t = st.enter_context(nc.sbuf_tensor("arena", [128, CAP // 4], F32))
        pst = [st.enter_context(nc.psum_tensor("ps%d" % i, [128, 512], F32)) for i in range(8)]
        S = Sched(nc)
        A = Arena(arena_t, CAP)
        psi = [0]

        psw_ = [6]

        def PS():
            i = psi[0] % psw_[0]
            psi[0] += 1
            return pst[i], ("ps", i)

        def PSX(i):
            return pst[6 + i], ("ps", 6 + i)

        def act(out, in_, func, reads, writes, bias=None, scale=None, eng="act"):
            kw = {}
            if bias is not None:
                kw["bias"] = bias
            if scale is not None:
                kw["scale"] = scale
            S.op(eng, lambda e: e.activation(out=out, in_=in_, func=func, **kw), reads, writes)

        def tt(out, in0, in1, op, reads, writes, eng="dve"):
            S.op(eng, lambda e: e.tensor_tensor(out=out, in0=in0, in1=in1, op=op), reads, writes)

        def ts(out, in0, s1, s2, op0, op1, reads, writes, eng="dve"):
            if op1 is None:
                S.op(eng, lambda e: e.tensor_scalar(out=out, in0=in0, scalar1=s1, scalar2=None, op0=op0), reads, writes)
            else:
                S.op(eng, lambda e: e.tensor_scalar(out=out, in0=in0, scalar1=s1, scalar2=s2, op0=op0, op1=op1), reads, writes)

        def stt(out, in0, scalar, in1, op0, op1, reads, writes):
            S.op("dve", lambda e: e.scalar_tensor_tensor(out=out, in0=in0, scalar=scalar, in1=in1, op0=op0, op1=op1), reads, writes)

        def cp(out, in_, reads, writes, eng="dve"):
            S.op(eng, lambda e: e.tensor_copy(out=out, in_=in_), reads, writes)

        def memset(ap, val, writes, eng="dve"):
            S.op(eng, lambda e: e.memset(ap, val), (), writes)

        def recip(out, in_, reads, writes):
            S.op("dve", lambda e: e.reciprocal(out=out, in_=in_), reads, writes)

        def mm(ps_ap, lhsT, rhs, start, stop, reads, pskey):
            S.op("pe", lambda e: e.matmul(ps_ap, lhsT=lhsT, rhs=rhs, start=start, stop=stop), reads, [pskey])

        def tr(ps_ap, in_, ident, reads, pskey):
            S.op("pe", lambda e: e.transpose(out=ps_ap, in_=in_, identity=ident), reads, [pskey])

        def dma(out, in_, reads, writes, sem, eng="sp"):
            S.op(eng, lambda e: e.dma_start(out=out, in_=in_), reads, writes, dma=sem)

        smallp = A.alloc([NSMALL], F32)
        constp = A.alloc([NCONST], F32)
        constb = A.alloc([NCONSTB], BF16)
        modt = A.alloc([DEPTH, 24, NSEQ], F32)
        Amod = A.alloc([DEPTH, 8, NSEQ], F32)
        dma(smallp, smallp_d, (), ["smallp"], "ld_smallp")
        dma(constp, constp_d, (), ["constp"], "ld_constp")
        cp(constb, constp[:, 0:NCONSTB], ["constp"], ["constb"])
        ident_f = constp[:, C_ID:C_ID + 128]
        ident_b = constb[:, C_ID:C_ID + 128]
        ones_b = constb[:, C_ONES:C_ONES + 128]
        ones_f = constp[:, C_ONES:C_ONES + 128]
        maskT = constp[:, C_MASK:C_MASK + 128]
        CONST = ["constp", "constb", "smallp"]

        def sm(name, off, n=1):
            o = _SM[name] + off
            return smallp[:, o:o + n]

        A.mark()
        cTf = A.alloc([8, NSEQ], BF16)
        dma(cTf, cT.rearrange("(c p) s -> p c s", p=128), (), ["cTf"], "ld_cT", eng="pool")
        adw = [A.alloc([8, 3 * D], BF16) for _ in range(2)]
        for l in range(nlayers):
            w = adw[l % 2]
            wk = ("adw", l % 2)
            dma(w, ada_w[l].rearrange("(c p) e -> p c e", p=128), (), [wk], "ld_adw%d" % (l % 2), eng="pool")
            ps, pk = PS()
            for e in range(24):
                for c in range(8):
                    mm(ps[:, e * NSEQ:(e + 1) * NSEQ], w[:, c, e * 128:(e + 1) * 128], cTf[:, c, :], c == 0, c == 7,
                       [wk, "cTf"], pk)
            psv = ps[:, 0:24 * NSEQ].rearrange("p (e s) -> p e s", s=NSEQ)
            for s in range(NSEQ):
                tt(modt[:, l, :, s], psv[:, :, s], sm("ada_b", l * 24, 24), ALU.add, [pk, "smallp"], ["modt"])
            for s in range(NSEQ):
                stt(Amod[:, l, :, s], modt[:, l, 8:16, s], 1.0, sm("norm_g", l * 8, 8), ALU.add, ALU.mult,
                    ["modt", "smallp"], ["Amod"])
        S.barrier()
        A.release()

        def load_x(xt, xkey, src_v, g0, nb):
            dma(xt[:, :, 0:nb], src_v[:, :, g0:g0 + nb], (), [xkey], "ld_" + xkey[0] + str(xkey[1]))

        def norm_h(l, xt, xkey, ht, xsq, rstd, tmp, segs, nb, tmpkey="tmp", xsq_w=(), h_w=lambda c: []):
            act(xsq[:, :, 0:nb], xt[:, :, 0:nb], AF.Square, [xkey], ["xsq"] + list(xsq_w))
            ps, pk = PS()
            for c in range(8):
                mm(ps[:, 0:nb], ones_b, xsq[:, c, 0:nb], c == 0, c == 7, ["xsq", "constb"], pk)
            act(rstd[:, 0:nb], ps[:, 0:nb], AF.Sqrt, [pk], ["rstd"], bias=EPS, scale=1.0 / D)
            recip(rstd[:, 0:nb], rstd[:, 0:nb], ["rstd"], ["rstd"])
            for sg in segs:
                c0, L, s = sg["c0"], sg["L"], sg["seq"]
                for c in range(8):
                    tk = (tmpkey, c % 2)
                    tt(tmp[:, c % 2, 0:L], xt[:, c, c0:c0 + L], rstd[:, c0:c0 + L], ALU.mult, [xkey, "rstd"], [tk])
                    act(ht[:, c, c0:c0 + L], tmp[:, c % 2, 0:L], AF.Identity, [tk, "Amod", "modt"], [("h", c)] + h_w(c),
                        bias=modt[:, l, c, s:s + 1], scale=Amod[:, l, c, s:s + 1])

        HK = [("h", c) for c in range(8)]

        def conv_layer(l, src_v):
            j = l // 2
            NB = 256
            blocks = make_blocks(NB)
            A.mark()
            w_in = A.alloc([8, 3 * D], BF16)
            w_out = A.alloc([8, D], BF16)
            diag = A.alloc([8, CK, 128], BF16)
            dma(w_in, cv_w_in[j].rearrange("(c p) e -> p c e", p=128), (), ["w_in"], "ld_w_in", eng="pool")
            dma(w_out, cv_w_out[j].rearrange("(c p) e -> p c e", p=128), (), ["w_out"], "ld_w_out", eng="pool")
            for c in range(8):
                for k in range(CK):
                    ts(diag[:, c, k, :], ident_f, sm("cv_w_dw", (j * 8 + c) * CK + k), None, ALU.mult, None,
                       CONST, ["diag"], eng="dve")
            xts = [A.alloc([8, NB], F32) for _ in range(2)]
            ht = A.alloc([8, NB], BF16)
            xsq = A.alloc([8, NB], BF16)
            rstd = A.alloc([NB], F32)
            HL = CK - 1
            gb = [A.alloc([8, 4 * (HL + LS)], BF16) for _ in range(2)]
            gtail = A.alloc([8, 4, HL], F32)
            hst = A.alloc([8, 4, HL], F32)
            yf = A.alloc([8, NB], F32)
            ybf = A.alloc([2, NB], BF16)
            ysq = A.alloc([2, NB], BF16)
            zz = xsq
            yo = ht
            mean = A.alloc([NB], F32)
            rs2 = A.alloc([NB], F32)
            m2 = A.alloc([NB], F32)
            t1 = A.alloc([2, NB], F32)
            t2 = A.alloc([2, NB], F32)
            s3 = A.alloc([2, NB], F32)
            tmp = t1
            sig = t2
            load_x(xts[0], ("x", 0), src_v, blocks[0][0], NB)
            for bi, (g0, segs) in enumerate(blocks):
                xt, xk = xts[bi % 2], ("x", bi % 2)
                g, gk = gb[bi % 2], ("gb", bi % 2)
                gp, gpk = gb[(bi + 1) % 2], ("gb", (bi + 1) % 2)
                if bi + 1 < len(blocks):
                    load_x(xts[(bi + 1) % 2], ("x", (bi + 1) % 2), src_v if True else None, blocks[bi + 1][0], NB)
                norm_h(l, xt, xk, ht, xsq, rstd, tmp, segs, NB, tmpkey="t1", xsq_w=[("zz", c) for c in range(8)],
                       h_w=lambda c: [("yo", c)])
                for si, sg in enumerate(segs):
                    if sg["first"]:
                        if sg["seq"] == 0:
                            memset(g[:, :, 0:HL], 0.0, [(gk, "h")], eng="pool")
                        else:
                            dma(hst[:, :, si, :], sconvT[j, sg["seq"] - 1].rearrange("(c p) k -> p c k", p=128), (),
                                [("hst", si)], "ld_hst%d" % si)
                            cp(g[:, :, si * (HL + LS):si * (HL + LS) + HL], hst[:, :, si, :], [("hst", si)], [(gk, "h")], eng="pool")
                    else:
                        cp(g[:, :, 0:HL], gp[:, :, NB:NB + HL], [gpk, (gpk, "h")], [(gk, "h")], eng="pool")
                for cc in range(8):
                    ps_g, pkg = PS()
                    for c in range(8):
                        mm(ps_g[:, 0:NB], w_in[:, c, (8 + cc) * 128:(9 + cc) * 128], ht[:, c, :], c == 0, c == 7,
                           HK + ["w_in"], pkg)
                    sk = ("t2", cc % 2)
                    act(sig[:, cc % 2, :], ps_g[:, 0:NB], AF.Sigmoid, [pkg], [sk])
                    ps_a, pka = PS()
                    for c in range(8):
                        mm(ps_a[:, 0:NB], w_in[:, c, cc * 128:(cc + 1) * 128], ht[:, c, :], c == 0, c == 7,
                           HK + ["w_in"], pka)
                    for si, sg in enumerate(segs):
                        c0, L = sg["c0"], sg["L"]
                        go = si * (HL + L)
                        tt(g[:, cc, go + HL:go + HL + L], ps_a[:, c0:c0 + L], sig[:, cc % 2, c0:c0 + L], ALU.mult,
                           [pka, sk], [gk])
                        if sg["last"]:
                            tt(gtail[:, cc, si, :], ps_a[:, c0 + L - HL:c0 + L], sig[:, cc % 2, c0 + L - HL:c0 + L],
                               ALU.mult, [pka, sk], [("gtail", si)])
                for si, sg in enumerate(segs):
                    if sg["last"]:
                        dma(oconvT[j, sg["seq"]].rearrange("(c p) k -> p c k", p=128), gtail[:, :, si, :],
                            [("gtail", si)], (), "st_gtail%d" % si)
                ps_m, pkm = PSX(0)
                ps_q, pkq = PSX(1)
                for cc in range(8):
                    ps_c, pkc = PS()
                    for si, sg in enumerate(segs):
                        c0, L = sg["c0"], sg["L"]
                        for k in range(CK):
                            mm(ps_c[:, c0:c0 + L], diag[:, cc, k, :],
                               g[:, cc, si * (HL + L) + k:si * (HL + L) + k + L], k == 0, k == CK - 1,
                               [gk, (gk, "h"), "diag"], pkc)
                    k2 = cc % 2
                    act(yf[:, cc, :], ps_c[:, 0:NB], AF.Identity, [pkc, "smallp"], [("yf", cc)],
                        bias=sm("cv_b_dw", j * 8 + cc))
                    cp(ybf[:, k2, :], yf[:, cc, :], [("yf", cc)], [("ybf", k2)], eng="pool")
                    act(ysq[:, k2, :], yf[:, cc, :], AF.Square, [("yf", cc)], [("ysq", k2)])
                    mm(ps_m[:, 0:NB], ones_b, ybf[:, k2, :], cc == 0, cc == 7, [("ybf", k2), "constb"], pkm)
                    mm(ps_q[:, 0:NB], ones_b, ysq[:, k2, :], cc == 0, cc == 7, [("ysq", k2), "constb"], pkq)
                for cc in range(8):
                    ps_z, pkz = PS()
                    for c in range(8):
                        mm(ps_z[:, 0:NB], w_in[:, c, (16 + cc) * 128:(17 + cc) * 128], ht[:, c, :], c == 0, c == 7,
                           HK + ["w_in"], pkz)
                    act(zz[:, cc, :], ps_z[:, 0:NB], AF.Silu, [pkz], [("zz", cc), "xsq"])
                ts(mean, ps_m[:, 0:NB], 1.0 / D, None, ALU.mult, None, [pkm], ["mean"])
                tt(m2, mean, mean, ALU.mult, ["mean"], ["m2"])
                stt(rs2, ps_q[:, 0:NB], 1.0 / D, m2, ALU.mult, ALU.subtract, [pkq, "m2"], ["rs2"])
                act(rs2, rs2, AF.Sqrt, ["rs2"], ["rs2"], bias=EPS, scale=1.0)
                recip(rs2, rs2, ["rs2"], ["rs2"])
                for cc in range(8):
                    k2 = cc % 2
                    tt(t1[:, k2, :], yf[:, cc, :], mean, ALU.subtract, [("yf", cc), "mean"], [("t1", k2)], eng="pool")
                    tt(t2[:, k2, :], t1[:, k2, :], rs2, ALU.mult, [("t1", k2), "rs2"], [("t2", k2)])
                    act(s3[:, k2, :], t2[:, k2, :], AF.Silu, [("t2", k2), "smallp"], [("s3", k2)],
                        bias=sm("cv_ln_b", j * 8 + cc), scale=sm("cv_ln_g", j * 8 + cc))
                    tt(yo[:, cc, :], s3[:, k2, :], zz[:, cc, :], ALU.mult, [("s3", k2), ("zz", cc)], [("yo", cc), ("h", cc)])
                YOK = [("yo", c) for c in range(8)]
                for dc in range(8):
                    ps_o, pko = PS()
                    for c in range(8):
                        mm(ps_o[:, 0:NB], w_out[:, c, dc * 128:(dc + 1) * 128], yo[:, c, :], c == 0, c == 7,
                           YOK + ["w_out"], pko)
                    for sg in segs:
                        c0, L, s = sg["c0"], sg["L"], sg["seq"]
                        stt(xt[:, dc, c0:c0 + L], ps_o[:, c0:c0 + L], modt[:, l, 16 + dc, s:s + 1],
                            xt[:, dc, c0:c0 + L], ALU.mult, ALU.add, [pko, "modt", xk], [xk])
                dma(xres_v[:, :, g0:g0 + NB], xt[:, :, 0:NB], [xk], (), "st_x%d" % (bi % 2))
            S.barrier()
            A.release()

        def mlstm_A(l, src_v):
            j = l // 2
            NB = 256
            blocks = make_blocks(NB)
            A.mark()
            w_in = A.alloc([8, 3 * MW], BF16)
            dma(w_in, ml_w_in[j].rearrange("(c p) e -> p c e", p=128), (), ["w_in"], "ld_w_in", eng="pool")
            xts = [A.alloc([8, NB], F32) for _ in range(2)]
            ht = A.alloc([8, NB], BF16)
            xsq = A.alloc([8, NB], BF16)
            rstd = A.alloc([NB], F32)
            tmp = A.alloc([2, NB], F32)
            xm = [A.alloc([16, NB], BF16) for _ in range(2)]
            zt = [A.alloc([16, NB], BF16) for _ in range(2)]
            ot = [A.alloc([2, MW], BF16) for _ in range(2)]
            xtail = A.alloc([16, 4, MK - 1], F32)
            load_x(xts[0], ("x", 0), src_v, blocks[0][0], NB)
            for bi, (g0, segs) in enumerate(blocks):
                b2 = bi % 2
                xt, xk = xts[b2], ("x", b2)
                if bi + 1 < len(blocks):
                    load_x(xts[(bi + 1) % 2], ("x", (bi + 1) % 2), src_v, blocks[bi + 1][0], NB)
                norm_h(l, xt, xk, ht, xsq, rstd, tmp, segs, NB)
                for cc in range(16):
                    ps, pk = PS()
                    for c in range(8):
                        mm(ps[:, 0:NB], w_in[:, c, cc * 128:(cc + 1) * 128], ht[:, c, :], c == 0, c == 7,
                           HK + ["w_in"], pk)
                    if cc % 2 == 0:
                        act(xm[b2][:, cc, :], ps[:, 0:NB], AF.Copy, [pk], [("xm", b2, cc)])
                    else:
                        cp(xm[b2][:, cc, :], ps[:, 0:NB], [pk], [("xm", b2, cc)])
                    for si, sg in enumerate(segs):
                        if sg["last"]:
                            c0, L = sg["c0"], sg["L"]
                            cp(xtail[:, cc, si, :], ps[:, c0 + L - (MK - 1):c0 + L], [pk], [("xtail", si)])
                for si, sg in enumerate(segs):
                    if sg["last"]:
                        dma(omconvT[j, sg["seq"]].rearrange("(c p) k -> p c k", p=128), xtail[:, :, si, :],
                            [("xtail", si)], (), "st_xtail%d" % si)
                dma(XM_v[:, :, g0:g0 + NB], xm[b2], [("xm", b2, cc) for cc in range(16)], (), "st_xm%d" % b2)
                for cc in range(16):
                    ps, pk = PS()
                    for c in range(8):
                        mm(ps[:, 0:NB], w_in[:, c, MW + cc * 128:MW + (cc + 1) * 128], ht[:, c, :], c == 0, c == 7,
                           HK + ["w_in"], pk)
                    act(zt[b2][:, cc, :], ps[:, 0:NB], AF.Silu, [pk], [("zt", b2, cc)])
                dma(ZZ_v[:, :, g0:g0 + NB], zt[b2], [("zt", b2, cc) for cc in range(16)], (), "st_zt%d" % b2)
                for tti in range(NB // 128):
                    for oc in range(4):
                        ps, pk = PS()
                        for c in range(8):
                            mm(ps[:, :], ht[:, c, tti * 128:(tti + 1) * 128],
                               w_in[:, c, 2 * MW + oc * 512:2 * MW + (oc + 1) * 512], c == 0, c == 7,
                               HK + ["w_in"], pk)
                        act(ot[b2][:, tti, oc * 512:(oc + 1) * 512], ps[:, :], AF.Sigmoid, [pk], [("ot", b2)])
                dma(OO[g0:g0 + NB, :].rearrange("(a p) e -> p a e", p=128), ot[b2], [("ot", b2)], (), "st_ot%d" % b2)
            S.barrier()
            A.release()

        def mlstm_B(l):
            j = l // 2
            NB = 128
            blocks = make_blocks(NB)
            A.mark()
            psw_[0] = 8
            wq = A.alloc([H, 4, DH], BF16)
            wk_ = A.alloc([H, 4, DH], BF16)
            wv = A.alloc([H, 4, DH], BF16)
            wg = A.alloc([3, 16, 8], BF16)
            diag = A.alloc([16, MK, 128], BF16)
            for wt_, src, nm in ((wq, ml_w_q, "wq"), (wk_, ml_w_k, "wk"), (wv, ml_w_v, "wv")):
                for h in range(H):
                    dma(wt_[:, h, :, :], src[j, h].rearrange("(c p) e -> p c e", p=128), (), [nm], "ld_" + nm,
                        eng="pool")
            dma(wg, wgate_d[j].rearrange("p (a b c) -> p a b c", a=3, b=16), (), ["wg"], "ld_wg", eng="pool")
            for c in range(16):
                for k in range(MK):
                    ts(diag[:, c, k, :], ident_f, sm("ml_w_conv", (j * 16 + c) * MK + k), None, ALU.mult, None,
                       CONST, ["diag"], eng="dve")
            CT32 = A.alloc([H, 4, DH], F32)
            CTb = A.alloc([H, 4, DH], BF16)
            n32 = A.alloc([H, 4], F32)
            nb_ = A.alloc([H, 4], BF16)
            nin = A.alloc([NS * H * 4], F32)
            nout = A.alloc([NSEQ * H * 4], F32)
            min_ = A.alloc([NS], F32, parts=4)
            mout = A.alloc([NSEQ], F32, parts=4)
            mcur = [A.alloc([1], F32, parts=4) for _ in range(2)]
            zeros4 = A.alloc([2 * LS], F32, parts=4)
            bgt = A.alloc([1], F32, parts=8)
            HM = MK - 1
            HP = 32
            xmb = [A.alloc([16, 2 * HP + NB], BF16) for _ in range(2)]
            xmh = A.alloc([16, 2, HM], F32)
            zt = [A.alloc([16, NB], BF16) for _ in range(1)]
            ot = [A.alloc([H * DH], BF16) for _ in range(1)]
            xc = A.alloc([16, NB], BF16)
            sxc = A.alloc([16, NB], BF16)
            qT = A.alloc([16, NB], BF16)
            kT = A.alloc([16, NB], BF16)
            vT = A.alloc([16, NB], BF16)
            yv = [A.alloc([16, NB], BF16) for _ in range(1)]
            g8 = A.alloc([NB], F32, parts=8)
            fgb = A.alloc([NB], F32, parts=4)
            a1 = A.alloc([NB], F32, parts=4)
            lf = A.alloc([NB], F32, parts=4)
            brow = A.alloc([2 * LS], F32, parts=4)
            grow = A.alloc([2 * LS], F32, parts=4)
            Mrow = A.alloc([2 * LS], F32, parts=4)
            negM = A.alloc([2 * LS], F32, parts=4)
            rowA = A.alloc([2 * LS], F32, parts=4)
            rowB = A.alloc([2 * LS], F32, parts=4)
            rowC = A.alloc([2 * LS], F32, parts=4)
            dgA = A.alloc([2, 4], F32, parts=4)
            cols = A.alloc([2, 12], F32)
            wks = A.alloc([2, 4], F32)
            decb = A.alloc([2, 4], F32)
            WT = [A.alloc([2 * LS], F32) for _ in range(H)]
            SpT = [A.alloc([2 * LS], BF16) for _ in range(H)]
            kp = [A.alloc([DH], BF16) for _ in range(H)]
            vtok = [A.alloc([DH], BF16) for _ in range(H)]
            dsb = A.alloc([2 * H], F32)
            dd = A.alloc([4, H], F32)
            tmpA = [A.alloc([DH], F32) for _ in range(H)]
            bst = A.alloc([H, 8], F32)
            hn = [A.alloc([DH], BF16) for _ in range(H)]
            t1 = A.alloc([H, 4 * 2 * LS], F32)
            selrow = constp[0:4, C_SELROW:C_SELROW + 4 * 128].rearrange("p (h t) -> p h t", h=4)
            selF = constp[0:8, C_SELF:C_SELF + 4]
            id4 = constp[0:4, C_ID:C_ID + 4]
            memset(zeros4, 0.0, ["zeros4"])
            cp(bgt, smallp[0:8, _SM["b_gate"] + j:_SM["b_gate"] + j + 1], ["smallp"], ["bgt"])
            dma(nin, snT[j], (), ["nin"], "ld_nin")
            dma(min_, smT[j], (), ["min"], "ld_min")
            hc = [0]
            mi = [0]
            CK32 = [("C32", h) for h in range(H)]
            CKB = [("Cb", h) for h in range(H)]

            def load_xm(bj):
                g0_, segs_ = blocks[bj]
                xb, xbk = xmb[bj % 2], ("xmb", bj % 2)
                for si, sg in enumerate(segs_):
                    c0, L, s = sg["c0"], sg["L"], sg["seq"]
                    gc = g0_ + c0
                    xo = si * (HP + L) + HP - HM
                    if sg["first"]:
                        if s == 0:
                            memset(xb[:, :, xo:xo + HM], 0.0, [(xbk, "h")], eng="pool")
                        else:
                            dma(xmh[:, :, si, :], smconvT[j, s - 1].rearrange("(c p) k -> p c k", p=128), (),
                                [("xmh", si)], "ld_xmh%d" % si)
                            cp(xb[:, :, xo:xo + HM], xmh[:, :, si, :], [("xmh", si)], [(xbk, "h")], eng="pool")
                        dma(xb[:, :, xo + HM:xo + HM + L], XM_v[:, :, gc:gc + L], (), [xbk], "ld_xmb%d" % (bj % 2))
                    else:
                        dma(xb[:, :, xo:xo + HM + L], XM_v[:, :, gc - HM:gc + L], (), [xbk, (xbk, "h")],
                            "ld_xmb%d" % (bj % 2))

            BP = {}

            def block_part(bi):
                g0, segs = blocks[bi]
                b2 = bi % 2
                xb, xbk = xmb[b2], ("xmb", b2)
                if bi == 0:
                    load_xm(0)
                    dma(zt[0], ZZ_v[:, :, g0:g0 + NB], (), [("zt", 0)], "ld_zt0")
                if bi + 1 < len(blocks):
                    load_xm(bi + 1)
                XBK = [xbk, (xbk, "h")]
                for cg in range(4):
                    ps, pk = PS()
                    for ci in range(4):
                        cc = cg * 4 + ci
                        for si, sg in enumerate(segs):
                            c0, L = sg["c0"], sg["L"]
                            for k in range(MK):
                                mm(ps[:, ci * NB + c0:ci * NB + c0 + L], diag[:, cc, k, :],
                                   xb[:, cc, si * (HP + L) + HP - HM + k:si * (HP + L) + HP - HM + k + L],
                                   k == 0, k == MK - 1, XBK + ["diag"], pk)
                    for ci in range(4):
                        cc = cg * 4 + ci
                        act(xc[:, cc, :], ps[:, ci * NB:(ci + 1) * NB], AF.Silu, [pk, "smallp"], [("xc", cc)],
                            bias=sm("ml_b_conv", j * 16 + cc))
                for (dst, dk, wt_, wn, srcsel) in ((qT, "qT", wq, "wq", 0), (kT, "kT", wk_, "wk", 0), (vT, "vT", wv, "wv", 1)):
                    for h in range(H):
                        ps, pk = PS()
                        for ec in range(4):
                            if srcsel == 0:
                                for kc in range(4):
                                    cc = h * 4 + kc
                                    mm(ps[:, ec * NB:(ec + 1) * NB], wt_[:, h, kc, ec * 128:(ec + 1) * 128],
                                       xc[:, cc, :], kc == 0, kc == 3, [("xc", cc), wn], pk)
                            else:
                                for si, sg in enumerate(segs):
                                    c0, L = sg["c0"], sg["L"]
                                    for kc in range(4):
                                        cc = h * 4 + kc
                                        mm(ps[:, ec * NB + c0:ec * NB + c0 + L],
                                           wt_[:, h, kc, ec * 128:(ec + 1) * 128],
                                           xb[:, cc, si * (HP + L) + HP:si * (HP + L) + HP + L],
                                           kc == 0, kc == 3, XBK + [wn], pk)
                        dv = dst[:, h * 4:(h + 1) * 4, :]
                        pv = ps[:, :].rearrange("p (a t) -> p a t", a=4)
                        if h % 2 == 0:
                            act(dv, pv, AF.Copy, [pk], [(dk, h)])
                        else:
                            cp(dv, pv, [pk], [(dk, h)])
                ps, pk = PS()
                n = 0
                for si_, (srcT, dk) in enumerate(((qT, "qT"), (kT, "kT"), (vT, "vT"))):
                    for cc in range(16):
                        mm(ps[0:8, 0:NB], wg[:, si_, cc, :], srcT[:, cc, :], n == 0, n == 47, [(dk, cc // 4), "wg"], pk)
                        n += 1
                act(g8, ps[0:8, 0:NB], AF.Identity, [pk, "bgt"], ["g8"], bias=bgt[:, 0:1])
                ps2, pk2 = PS()
                mm(ps2[0:4, 0:NB], selF, g8, True, True, ["g8", "constp"], pk2)
                cp(fgb, ps2[0:4, 0:NB], [pk2], ["fgb"])
                act(a1, fgb, AF.Abs, ["fgb"], ["a1"])
                act(a1, a1, AF.Exp, ["a1"], ["a1"], scale=-1.0)
                act(a1, a1, AF.Ln, ["a1"], ["a1"], bias=1.0)
                ts(lf, fgb, 0.0, None, ALU.min, None, ["fgb"], ["lf"])
                tt(lf, lf, a1, ALU.subtract, ["lf", "a1"], ["lf"])
                chunks = []
                for si, sg in enumerate(segs):
                    Lc = 2 * LS if sg["L"] % (2 * LS) == 0 else LS
                    for ch in range(sg["L"] // Lc):
                        chunks.append((sg, ch, Lc))
                for ci, (sg, ch, Lc) in enumerate(chunks):
                    s = sg["seq"]
                    cs = sg["c0"] + ch * Lc
                    csl = slice(cs, cs + Lc)
                    rs = slice(ci * Lc, (ci + 1) * Lc)
                    lastc = slice((ci + 1) * Lc - 1, (ci + 1) * Lc)
                    first = sg["first"] and ch == 0
                    last = sg["last"] and ch == sg["L"] // Lc - 1
                    if first:
                        if s == 0:
                            memset(mcur[mi[0] % 2], 0.0, [("m", mi[0] % 2)])
                        else:
                            cp(mcur[mi[0] % 2], min_[:, s - 1:s], ["min"], [("m", mi[0] % 2)])
                    m_t, mk_ = mcur[mi[0] % 2], ("m", mi[0] % 2)
                    m_n, mkn = mcur[(mi[0] + 1) % 2], ("m", (mi[0] + 1) % 2)
                    mi[0] += 1
                    kb, kg, kM, kn, kA, kB, kC = [(nm, ci) for nm in ("brow", "grow", "Mrow", "negM", "rowA", "rowB", "rowC")]
                    S.op("dve", lambda e, o=brow[:, rs], d0=lf[:, csl], z0=zeros4[:, 0:Lc]: e.tensor_tensor_scan(
                        out=o, data0=d0, data1=z0, initial=0.0, op0=ALU.add, op1=ALU.add),
                        ["lf", "zeros4"], [kb])
                    tt(grow[:, rs], g8[0:4, csl], brow[:, rs], ALU.subtract, ["g8", kb], [kg])
                    S.op("dve", lambda e, o=Mrow[:, rs], d0=grow[:, rs], ini=m_t[:, 0:1]: e.tensor_tensor_scan(
                        out=o, data0=d0, data1=d0, initial=ini, op0=ALU.max, op1=ALU.max),
                        [kg, mk_], [kM])
                    ts(negM[:, rs], Mrow[:, rs], -1.0, None, ALU.mult, None, [kM], [kn])
                    ts(rowA[:, rs], Mrow[:, rs], -1.0, m_t[:, 0:1], ALU.mult, ALU.add, [kM, mk_], [kA])
                    stt(rowB[:, rs], brow[:, rs], -1.0, Mrow[:, rs], ALU.mult, ALU.subtract, [kb, kM], [kB])
                    ts(rowC[:, rs], grow[:, rs], Mrow[:, lastc], None, ALU.subtract, None, [kg, kM], [kC])
                    tt(m_n, brow[:, lastc], Mrow[:, lastc], ALU.add, [kb, kM], [mkn])
                    ts(dgA[:, ci, :], id4, rowA[:, lastc], None, ALU.mult, None, [kA, "constp"], [("dgA", ci)])
                    psc, pkc = PS()
                    for qi, (rw, rk) in enumerate(((rowA, kA), (rowB, kB), (rowC, kC))):
                        mm(psc[0:Lc, qi * 4:(qi + 1) * 4], rw[:, rs], id4, True, True, [rk, "constp"], pkc)
                    act(cols[0:Lc, ci, :], psc[0:Lc, 0:12], AF.Exp, [pkc], [("cols", ci)])
                    ts(wks[0:Lc, ci, :], cols[0:Lc, ci, 8:12], KSCALE, None, ALU.mult, None, [("cols", ci)], [("wks", ci)])
                    psd, pkd = PS()
                    mm(psd[:, 0:4], ones_f[0:4, :], dgA[:, ci, :], True, True, [("dgA", ci), "constp"], pkd)
                    act(decb[:, ci, :], psd[:, 0:4], AF.Exp, [pkd], [("decb", ci)])
                    if last:
                        cp(mout[:, s:s + 1], m_n, [mkn], ["mout"])
                BP[bi] = chunks

            def sxc_part(bi):
                for cc in range(16):
                    ts(sxc[:, cc, :], xc[:, cc, :], sm("ml_skip", j * 16 + cc), None, ALU.mult, None,
                       [("xc", cc), "smallp"], [("sxc", cc)], eng="dve")

            def rec_a(bi, cix):
                g0, segs = blocks[bi]
                b2 = bi % 2
                for ci, (sg, ch, Lc) in [(cix, BP[bi][cix])]:
                    s = sg["seq"]
                    cs = sg["c0"] + ch * Lc
                    csl = slice(cs, cs + Lc)
                    rs = slice(ci * Lc, (ci + 1) * Lc)
                    gcol = g0 + cs
                    first = sg["first"] and ch == 0
                    last = sg["last"] and ch == sg["L"] // Lc - 1
                    o_t, ok = ot[0], ("ot", 0)
                    dma(o_t[0:Lc], OO[gcol:gcol + Lc, :], (), [ok], "ld_ot0")
                    if first:
                        if s == 0:
                            for h in range(H):
                                memset(CT32[:, h], 0.0, [("C32", h)], eng="pool")
                                memset(CTb[:, h], 0.0, [("Cb", h)], eng="pool")
                            memset(n32, 0.0, ["n32"])
                            memset(nb_, 0.0, ["nb"])
                        else:
                            for h in range(H):
                                dma(CT32[:, h], sCT[j, s - 1, h].rearrange("(c p) v -> p c v", p=128), (),
                                    [("C32", h)], "ld_C%d" % h)
                                cp(CTb[:, h], CT32[:, h], [("C32", h)], [("Cb", h)], eng="pool")
                            nv = nin[:, (s - 1) * 16:s * 16].rearrange("p (h c) -> p h c", h=H)
                            cp(n32, nv, ["nin"], ["n32"])
                            cp(nb_, nv, ["nin"], ["nb"])
                    HQ = [[h * 4 + kc for kc in range(4)] for h in range(H)]
                    kg, kn = ("grow", ci), ("negM", ci)
                    for h in range(H):
                        psw, pkw = PS()
                        mm(psw[0:Lc, 0:Lc], grow[:, rs], selrow[:, h, 0:Lc], True, False, [kg, "constp"], pkw)
                        mm(psw[0:Lc, 0:Lc], selrow[:, h, 0:Lc], negM[:, rs], False, False, [kn, "constp"], pkw)
                        mm(psw[0:Lc, 0:Lc], ident_f[0:Lc, 0:Lc], maskT[0:Lc, 0:Lc], False, True, ["constp"], pkw)
                        act(WT[h][0:Lc, 0:Lc], psw[0:Lc, 0:Lc], AF.Exp, [pkw], [("WT", h)])
                    for h in range(H):
                        pss, pks = PS()
                        for kc in range(4):
                            mm(pss[0:Lc, 0:Lc], kT[:, HQ[h][kc], csl], qT[:, HQ[h][kc], csl], kc == 0, kc == 3,
                               [("kT", h), ("qT", h)], pks)
                        stt(SpT[h][0:Lc, 0:Lc], pss[0:Lc, 0:Lc], KSCALE, WT[h][0:Lc, 0:Lc], ALU.mult, ALU.mult,
                            [pks, ("WT", h)], [("SpT", h)])
                    for h in range(H):
                        pst_k, pkk = PS()
                        pkb = pst_k.bitcast(BF16)
                        for kc in range(4):
                            tr(pkb[0:Lc, kc * 128:(kc + 1) * 128], kT[:, HQ[h][kc], csl], ident_b, [("kT", h), "constb"], pkk)
                        act(kp[h][0:Lc], pkb[0:Lc, 0:DH], AF.Identity, [pkk, ("wks", ci)], [("kp", h)],
                            scale=wks[0:Lc, ci, h:h + 1])
                    for h in range(H):
                        pst_v, pkv = PS()
                        pvb = pst_v.bitcast(BF16)
                        for kc in range(4):
                            tr(pvb[0:Lc, kc * 128:(kc + 1) * 128], vT[:, HQ[h][kc], csl], ident_b, [("vT", h), "constb"], pkv)
                        cp(vtok[h][0:Lc], pvb[0:Lc, 0:DH], [pkv], [("vtok", h)])
                    psn, pkn = PS()
                    for h in range(H):
                        for kc in range(4):
                            mm(psn[0:Lc, 2 * h:2 * h + 1], qT[:, HQ[h][kc], csl], nb_[:, h, kc:kc + 1], kc == 0, kc == 3,
                               [("qT", h), "nb"], pkn)
                        mm(psn[0:Lc, 2 * h + 1:2 * h + 2], SpT[h][0:Lc, 0:Lc], ones_b[0:Lc, 0:1], True, True,
                           [("SpT", h), "constb"], pkn)
                    cp(dsb[0:Lc], psn[0:Lc, 0:2 * H], [pkn], ["dsb"])
                    dsv = dsb[0:Lc].rearrange("p (h t) -> p h t", t=2)
                    CK_ = ("cols", ci)
                    tt(dd[0:Lc, 0, :], dsv[:, :, 0], cols[0:Lc, ci, 0:4], ALU.mult, ["dsb", CK_], ["den"])
                    tt(dd[0:Lc, 0, :], dd[0:Lc, 0, :], dsv[:, :, 1], ALU.add, ["dsb", "den"], ["den"])
                    act(dd[0:Lc, 1, :], dd[0:Lc, 0, :], AF.Abs, ["den"], ["den"])
                    tt(dd[0:Lc, 1, :], dd[0:Lc, 1, :], cols[0:Lc, ci, 4:8], ALU.max, ["den", CK_], ["den"])
                    recip(dd[0:Lc, 2, :], dd[0:Lc, 1, :], ["den"], ["den"])
                    tt(dd[0:Lc, 3, :], dd[0:Lc, 2, :], cols[0:Lc, ci, 0:4], ALU.mult, ["den", CK_], ["den"])
                    for h in range(H):
                        psa, pka = PS()
                        for kc in range(4):
                            mm(psa[0:Lc, :], qT[:, HQ[h][kc], csl], CTb[:, h, kc, :], kc == 0, kc == 3,
                               [("qT", h), ("Cb", h)], pka)
                        act(tmpA[h][0:Lc], psa[0:Lc, :], AF.Identity, [pka, "den"], [("hh", h)], scale=dd[0:Lc, 3, h:h + 1])
                    for h in range(H):
                        psb, pkb_ = PS()
                        mm(psb[0:Lc, :], SpT[h][0:Lc, 0:Lc], vtok[h][0:Lc], True, True, [("SpT", h), ("vtok", h)], pkb_)
                        stt(tmpA[h][0:Lc], psb[0:Lc, :], dd[0:Lc, 2, h:h + 1], tmpA[h][0:Lc], ALU.mult, ALU.add,
                            [pkb_, "den", ("hh", h)], [("hh", h)])
                    for h in range(H):
                        tt(tmpA[h][0:Lc], tmpA[h][0:Lc], o_t[0:Lc, h * DH:(h + 1) * DH], ALU.mult, [("hh", h), ok], [("hh", h)])
                    for kc in range(4):
                        for h in range(H):
                            psc2, pkc2 = PS()
                            mm(psc2[:, :], kp[h][0:Lc, kc * 128:(kc + 1) * 128], vtok[h][0:Lc], True, True,
                               [("kp", h), ("vtok", h)], pkc2)
                            stt(CT32[:, h, kc, :], CT32[:, h, kc, :], decb[:, ci, h:h + 1], psc2[:, :], ALU.mult, ALU.add,
                                [("C32", h), ("decb", ci), pkc2], [("C32", h)])
                            act(CTb[:, h, kc, :], CT32[:, h, kc, :], AF.Copy, [("C32", h)], [("Cb", h)])
                    for h in range(H):
                        psn2, pkn2 = PS()
                        for kc in range(4):
                            mm(psn2[:, kc:kc + 1], kp[h][0:Lc, kc * 128:(kc + 1) * 128], ones_b[0:Lc, 0:1], True, True,
                               [("kp", h), "constb"], pkn2)
                        stt(n32[:, h, :], n32[:, h, :], decb[:, ci, h:h + 1], psn2[:, 0:4], ALU.mult, ALU.add,
                            ["n32", ("decb", ci), pkn2], ["n32"])
                        cp(nb_[:, h, :], n32[:, h, :], ["n32"], ["nb"])
                    if last:
                        for h in range(H):
                            dma(oCT[j, s, h].rearrange("(c p) v -> p c v", p=128), CT32[:, h], [("C32", h)], (),
                                "st_C%d" % h)
                        cp(nout[:, s * 16:(s + 1) * 16].rearrange("p (h c) -> p h c", h=H), n32, ["n32"], ["nout"])

            def rec_b(bi, cix):
                g0, segs = blocks[bi]
                b2 = bi % 2
                for ci, (sg, ch, Lc) in [(cix, BP[bi][cix])]:
                    s = sg["seq"]
                    cs = sg["c0"] + ch * Lc
                    csl = slice(cs, cs + Lc)
                    rs = slice(ci * Lc, (ci + 1) * Lc)
                    gcol = g0 + cs
                    first = sg["first"] and ch == 0
                    last = sg["last"] and ch == sg["L"] // Lc - 1
                    for h in range(H):
                        S.op("dve", lambda e, o=bst[0:Lc, h, 0:6], i=tmpA[h][0:Lc]: e.bn_stats(out=o, in_=i), [("hh", h)],
                             [("bst", h)])
                    for h in range(H):
                        S.op("dve", lambda e, o=bst[0:Lc, h, 6:8], i=bst[0:Lc, h, 0:6]: e.bn_aggr(out=o, in_=i),
                             [("bst", h)], [("bst", h)])
                    BK = [("bst", h) for h in range(H)]
                    act(bst[0:Lc, :, 7], bst[0:Lc, :, 7], AF.Sqrt, BK, BK, bias=EPS, scale=1.0)
                    recip(bst[0:Lc, :, 7], bst[0:Lc, :, 7], BK, BK)
                    for h in range(H):
                        ts(hn[h][0:Lc], tmpA[h][0:Lc], bst[0:Lc, h, 6:7], bst[0:Lc, h, 7:8], ALU.subtract, ALU.mult,
                           [("hh", h), ("bst", h)], [("hn", h)])
                    for h in range(H):
                        psh, pkh = PS()
                        phb = psh.bitcast(BF16)
                        for vc in range(4):
                            tr(phb[:, vc * Lc:(vc + 1) * Lc], hn[h][0:Lc, vc * 128:(vc + 1) * 128], ident_b[0:Lc, 0:Lc],
                               [("hn", h), "constb"], pkh)
                        for vc in range(4):
                            cc = h * 4 + vc
                            stt(t1[:, h, vc * Lc:(vc + 1) * Lc], phb[:, vc * Lc:(vc + 1) * Lc], sm("ml_gn_g", j * 16 + cc),
                                sxc[:, cc, csl], ALU.mult, ALU.add, [pkh, "smallp", ("sxc", cc)], [("t1", h)])
                    for h in range(H):
                        tt(yv[0][:, h * 4:(h + 1) * 4, csl], t1[:, h, 0:4 * Lc].rearrange("p (a t) -> p a t", a=4),
                           zt[0][:, h * 4:(h + 1) * 4, csl], ALU.mult, [("t1", h), ("zt", 0)], [("yv", 0)])

            def block_end(bi):
                g0, segs = blocks[bi]
                b2 = bi % 2
                if bi + 1 < len(blocks):
                    gn_ = blocks[bi + 1][0]
                    dma(zt[0], ZZ_v[:, :, gn_:gn_ + NB], (), [("zt", 0)], "ld_zt0")
                dma(YY_v[:, :, g0:g0 + NB], yv[0], [("yv", 0)], (), "st_yv0")

            block_part(0)
            sxc_part(0)
            for bi in range(len(blocks)):
                nch = len(BP[bi])
                for cix in range(nch):
                    rec_a(bi, cix)
                    if cix == nch - 1 and bi + 1 < len(blocks):
                        block_part(bi + 1)
                    rec_b(bi, cix)
                block_end(bi)
                if bi + 1 < len(blocks):
                    sxc_part(bi + 1)
            dma(onT[j], nout, ["nout"], (), "st_nout")
            dma(omT[j], mout, ["mout"], (), "st_mout")
            S.barrier()
            psw_[0] = 6
            A.release()

        def mlstm_C(l, final):
            j = l // 2
            NB = 256
            blocks = make_blocks(NB)
            A.mark()
            w_out = A.alloc([16, D], BF16)
            dma(w_out, ml_w_out[j].rearrange("(c p) e -> p c e", p=128), (), ["w_out"], "ld_w_out", eng="pool")
            xts = [A.alloc([8, NB], F32) for _ in range(2)]
            yts = [A.alloc([16, NB], BF16) for _ in range(2)]
            xsq = A.alloc([8, NB], BF16)
            rstd = A.alloc([NB], F32)
            yfin = [A.alloc([8, NB], F32) for _ in range(2)]

            def loads(bi):
                g0 = blocks[bi][0]
                load_x(xts[bi % 2], ("x", bi % 2), xres_v, g0, NB)
                dma(yts[bi % 2], YY_v[:, :, g0:g0 + NB], (), [("y", bi % 2)], "ld_y%d" % (bi % 2))

            loads(0)
            for bi, (g0, segs) in enumerate(blocks):
                b2 = bi % 2
                xt, xk = xts[b2], ("x", b2)
                if bi + 1 < len(blocks):
                    loads(bi + 1)
                for dc in range(8):
                    ps, pk = PS()
                    for c in range(16):
                        mm(ps[:, 0:NB], w_out[:, c, dc * 128:(dc + 1) * 128], yts[b2][:, c, :], c == 0, c == 15,
                           [("y", b2), "w_out"], pk)
                    for sg in segs:
                        c0, L, s = sg["c0"], sg["L"], sg["seq"]
                        stt(xt[:, dc, c0:c0 + L], ps[:, c0:c0 + L], modt[:, l, 16 + dc, s:s + 1], xt[:, dc, c0:c0 + L],
                            ALU.mult, ALU.add, [pk, "modt", xk], [xk])
                if not final:
                    dma(xres_v[:, :, g0:g0 + NB], xt[:, :, 0:NB], [xk], (), "st_x%d" % b2)
                else:
                    act(xsq, xt, AF.Square, [xk], ["xsq"])
                    ps, pk = PS()
                    for c in range(8):
                        mm(ps[:, 0:NB], ones_b, xsq[:, c, :], c == 0, c == 7, ["xsq", "constb"], pk)
                    act(rstd, ps[:, 0:NB], AF.Sqrt, [pk], ["rstd"], bias=EPS, scale=1.0 / D)
                    recip(rstd, rstd, ["rstd"], ["rstd"])
                    for c in range(8):
                        stt(yfin[b2][:, c, :], xt[:, c, :], sm("final_g", c), rstd, ALU.mult, ALU.mult,
                            [xk, "smallp", "rstd"], [("yfin", b2)])
                    dma(yT_v[:, :, g0:g0 + NB], yfin[b2], [("yfin", b2)], (), "st_yfin%d" % b2)
            S.barrier()
            A.release()

        src = xT_v
        for l in range(nlayers):
            if l % 2 == 0:
                conv_layer(l, src)
            else:
                mlstm_A(l, src)
                mlstm_B(l)
                mlstm_C(l, final=(l == DEPTH - 1))
            src = xres_v
        if nlayers < DEPTH:
            A.mark()
            xt = A.alloc([8, 256], F32)
            for b in range(NT // 256):
                dma(xt, xres_v[:, :, b * 256:(b + 1) * 256], (), ["xdbg"], "ld_dbg")
                dma(yT_v[:, :, b * 256:(b + 1) * 256], xt, ["xdbg"], (), "st_dbg")
            A.release()
        S.emit()
    return nc


def _vec(v):
    v = np.asarray(v, np.float32)
    lead = v.shape[:-1]
    c = v.shape[-1] // 128
    return np.moveaxis(v.reshape(lead + (c, 128)), -1, 0)


def kernel(x_prompt, x_sample, c_prompt, c_sample, state_conv, state_mconv, state_C, state_n, state_m,
           norm_g, ada_w, ada_b, cv_w_in, cv_w_dw, cv_b_dw, cv_ln_g, cv_ln_b, cv_w_out,
           ml_w_in, ml_w_conv, ml_b_conv, ml_w_q, ml_w_k, ml_w_v, ml_w_gate, ml_b_gate,
           ml_gn_g, ml_skip, ml_w_out, final_g, _nlayers=DEPTH):
    f = lambda a: np.ascontiguousarray(np.asarray(a, np.float32))
    smallp = np.zeros((128, NSMALL), np.float32)

    def put(name, arr):
        arr = np.asarray(arr, np.float32).reshape(128, -1)
        smallp[:, _SM[name]:_SM[name] + arr.shape[1]] = arr

    put("norm_g", _vec(norm_g))
    put("ada_b", _vec(ada_b))
    put("cv_w_dw", np.transpose(_vec(cv_w_dw), (0, 1, 3, 2)))
    put("cv_b_dw", _vec(cv_b_dw))
    put("cv_ln_g", _vec(cv_ln_g))
    put("cv_ln_b", _vec(cv_ln_b))
    put("ml_w_conv", np.transpose(_vec(ml_w_conv), (0, 1, 3, 2)))
    put("ml_b_conv", _vec(ml_b_conv))
    put("ml_gn_g", _vec(ml_gn_g))
    put("ml_skip", _vec(ml_skip))
    put("final_g", _vec(final_g))
    bg = np.zeros((128, 2), np.float32)
    bg[0:8, :] = np.asarray(ml_b_gate, np.float32).T
    put("b_gate", bg)
    constp = np.zeros((128, NCONST), np.float32)
    constp[:, C_ID:C_ID + 128] = np.eye(128, dtype=np.float32)
    s_idx = np.arange(128)[:, None]
    t_idx = np.arange(128)[None, :]
    constp[:, C_MASK:C_MASK + 128] = np.where(s_idx <= t_idx, 0.0, NEG)
    for h in range(4):
        constp[h, C_SELROW + h * 128:C_SELROW + (h + 1) * 128] = 1.0
        constp[4 + h, C_SELF + h] = 1.0
    constp[:, C_ONES:C_ONES + 128] = 1.0
    wg = np.asarray(ml_w_gate, np.float32).reshape(2, 3, 16, 128, 8)
    wgate = np.ascontiguousarray(np.transpose(wg, (0, 3, 1, 2, 4)).reshape(2, 128, 384))

    x_prompt = np.asarray(x_prompt, np.float32)
    x_sample = np.asarray(x_sample, np.float32)
    shared = dict(smallp=smallp, constp=constp, ada_w=f(ada_w), cv_w_in=f(cv_w_in), cv_w_out=f(cv_w_out),
                  ml_w_in=f(ml_w_in), ml_w_q=f(ml_w_q), ml_w_k=f(ml_w_k), ml_w_v=f(ml_w_v), wgate=wgate,
                  ml_w_out=f(ml_w_out))
    in_maps = []
    for c in range(8):
        sl = slice(4 * c, 4 * c + 4)
        xs = np.concatenate([x_prompt[c], x_sample[sl].reshape(NS * LS, D)], axis=0)
        cc = np.concatenate([np.asarray(c_prompt, np.float32)[c:c + 1], np.asarray(c_sample, np.float32)[sl]], axis=0)
        m = dict(shared)
        m["xT"] = np.ascontiguousarray(xs.T)
        m["cT"] = np.ascontiguousarray(cc.T)
        m["sconvT"] = np.ascontiguousarray(np.transpose(np.asarray(state_conv, np.float32)[:, sl], (0, 1, 3, 2)))
        m["smconvT"] = np.ascontiguousarray(np.transpose(np.asarray(state_mconv, np.float32)[:, sl], (0, 1, 3, 2)))
        m["sCT"] = np.ascontiguousarray(np.transpose(np.asarray(state_C, np.float32)[:, sl], (0, 1, 2, 4, 3)))
        sn = np.asarray(state_n, np.float32)[:, sl].reshape(2, NS, H, 4, 128)
        m["snT"] = np.ascontiguousarray(np.transpose(sn, (0, 4, 1, 2, 3)).reshape(2, 128, NS * H * 4))
        m["smT"] = np.ascontiguousarray(np.transpose(np.asarray(state_m, np.float32)[:, sl], (0, 2, 1)))
        in_maps.append(m)
    if _nlayers == "prep":
        return in_maps
    nc = build(_nlayers)
    res = run_bass_kernel_spmd(nc, in_maps, core_ids=list(range(8)))
    return assemble(res.results)


def assemble(R, cores=range(8)):
    y_prompt = np.empty((8, TP, D), np.float32)
    y_sample = np.empty((32, LS, D), np.float32)
    p_conv = np.empty((2, 8, CK - 1, D), np.float32)
    s_conv = np.empty((2, 32, CK - 1, D), np.float32)
    p_mconv = np.empty((2, 8, MK - 1, MW), np.float32)
    s_mconv = np.empty((2, 32, MK - 1, MW), np.float32)
    p_C = np.empty((2, 8, H, DH, DH), np.float32)
    s_C = np.empty((2, 32, H, DH, DH), np.float32)
    p_n = np.empty((2, 8, H, DH), np.float32)
    s_n = np.empty((2, 32, H, DH), np.float32)
    p_m = np.empty((2, 8, H), np.float32)
    s_m = np.empty((2, 32, H), np.float32)
    for c in cores:
        r = R[c]
        sl = slice(4 * c, 4 * c + 4)
        yt = np.asarray(r["yT"]).T
        y_prompt[c] = yt[:TP]
        y_sample[sl] = yt[TP:].reshape(NS, LS, D)
        oc = np.transpose(np.asarray(r["oconvT"]), (0, 1, 3, 2))
        p_conv[:, c] = oc[:, 0]
        s_conv[:, sl] = oc[:, 1:]
        om = np.transpose(np.asarray(r["omconvT"]), (0, 1, 3, 2))
        p_mconv[:, c] = om[:, 0]
        s_mconv[:, sl] = om[:, 1:]
        oC = np.transpose(np.asarray(r["oCT"]), (0, 1, 2, 4, 3))
        p_C[:, c] = oC[:, 0]
        s_C[:, sl] = oC[:, 1:]
        on = np.transpose(np.asarray(r["onT"]).reshape(2, 128, NSEQ, H, 4), (0, 2, 3, 4, 1)).reshape(2, NSEQ, H, DH)
        p_n[:, c] = on[:, 0]
        s_n[:, sl] = on[:, 1:]
        omm = np.transpose(np.asarray(r["omT"]), (0, 2, 1))
        p_m[:, c] = omm[:, 0]
        s_m[:, sl] = omm[:, 1:]
    return (y_prompt, y_sample, p_conv, p_mconv, p_C, p_n, p_m, s_conv, s_mconv, s_C, s_n, s_m)
```

```python
import contextlib
import os
import numpy as np
import concourse.bass as bass
import concourse.mybir as mybir
from concourse.bass_utils import run_bass_kernel_spmd

F32 = mybir.dt.float32
BF16 = mybir.dt.bfloat16
AF = mybir.ActivationFunctionType
ALU = mybir.AluOpType

D = 1024
TP = 4096
NS = 4
LS = 64
NT = TP + NS * LS
NSEQ = 5
DEPTH = 4
EPS = 1e-6
CK = 31
MW = 2048
H = 4
DH = 512
MK = 4
KSCALE = DH ** -0.5
NEG = -30000.0

_SM = {}
_o = 0
for _n, _w in [("norm_g", 32), ("ada_b", 96), ("cv_w_dw", 2 * 8 * 31), ("cv_b_dw", 16), ("cv_ln_g", 16),
               ("cv_ln_b", 16), ("ml_w_conv", 2 * 16 * 4), ("ml_b_conv", 32), ("ml_gn_g", 32), ("ml_skip", 32),
               ("final_g", 8), ("b_gate", 2)]:
    _SM[_n] = _o
    _o += _w
NSMALL = _o
C_ID = 0
C_ONES = 128
C_MASK = 256
C_SELROW = 384
C_SELF = 896
NCONST = 900
NCONSTB = 256


class Sched:
    ENG = ["pe", "act", "dve", "pool", "sp"]

    def __init__(self, nc):
        self.nc = nc
        self.prog = {e: [] for e in self.ENG}
        self.cnt = {e: 0 for e in self.ENG}
        self.seen = {e: {} for e in self.ENG}
        self.lastw = {}
        self.rd = {}
        self.dcnt = {}

    def op(self, eng, fn, reads=(), writes=(), dma=None):
        psr = [r for r in reads if isinstance(r, tuple) and len(r) == 2 and r[0] == "ps"]
        if psr:
            writes = list(writes) + [r for r in psr if r not in writes]
        deps = {}

        def add(ev):
            if ev is None:
                return
            k, v = ev
            if k == "pe" and eng == "pe":
                return
            if deps.get(k, 0) < v:
                deps[k] = v

        for r in reads:
            add(self.lastw.get(r))
        for w in writes:
            add(self.lastw.get(w))
            for ev in self.rd.get(w, {}).items():
                add(ev)
        waits = []
        for k, v in deps.items():
            if self.seen[eng].get(k, 0) >= v:
                continue
            self.seen[eng][k] = v
            waits.append((k, v))
        if dma is None:
            self.cnt[eng] += 1
            ev = (eng, self.cnt[eng])
        else:
            self.dcnt[dma] = self.dcnt.get(dma, 0) + 16
            ev = (dma, self.dcnt[dma])
        self.prog[eng].append((fn, waits, ev))
        for w in writes:
            self.lastw[w] = ev
            self.rd[w] = {}
        for r in reads:
            d = self.rd.setdefault(r, {})
            if d.get(ev[0], 0) < ev[1]:
                d[ev[0]] = ev[1]
        return ev

    def barrier(self):
        tot = dict(self.cnt)
        tot.update(self.dcnt)
        for e in self.ENG:
            waits = []
            for k, v in tot.items():
                if v <= 0 or k == e:
                    continue
                if self.seen[e].get(k, 0) >= v:
                    continue
                self.seen[e][k] = v
                waits.append((k, v))
            if waits:
                self.prog[e].append((None, waits, None))
        self.lastw = {}
        self.rd = {}

    def emit(self):
        nc = self.nc
        semnames = set(self.ENG) | set(self.dcnt.keys())
        with contextlib.ExitStack() as st:
            sems = {k: st.enter_context(nc.semaphore("s_" + k)) for k in sorted(semnames)}
            block = st.enter_context(nc.Block())
            fin = []
            for k in sorted(semnames):
                v = self.cnt[k] if k in self.cnt else self.dcnt[k]
                if v > 0 and k != "sp":
                    fin.append((k, v))

            def run(engname, eng, extra=None):
                for fn, waits, ev in self.prog[engname]:
                    for k, v in waits:
                        eng.wait_ge(sems[k], v)
                    if fn is None:
                        continue
                    ins = fn(eng)
                    ins.then_inc(sems[ev[0]], 16 if ev[0] not in self.cnt else 1)
                if extra:
                    for k, v in extra:
                        eng.wait_ge(sems[k], v)

            @block.tensor
            def _(e):
                run("pe", e)

            @block.scalar
            def _(e):
                run("act", e)

            @block.vector
            def _(e):
                run("dve", e)

            @block.gpsimd
            def _(e):
                run("pool", e)

            @block.sync
            def _(e):
                run("sp", e, fin)


class Arena:
    def __init__(self, tens, cap_bytes):
        self.t = tens
        self.cap = cap_bytes
        self.off = 0
        self.marks = []

    def alloc(self, free_shape, dt, parts=128):
        n = int(np.prod(free_shape))
        nb = n * (4 if dt == F32 else 2)
        nb = (nb + 63) // 64 * 64
        assert self.off + nb <= self.cap, ("SBUF arena overflow", self.off, nb, self.cap)
        o32 = self.off // 4
        ap = self.t[:, o32:o32 + nb // 4]
        if dt != F32:
            ap = ap.bitcast(dt)
        ap = ap[:, 0:n]
        if len(free_shape) == 2:
            ap = ap.rearrange("p (a b) -> p a b", a=free_shape[0])
        elif len(free_shape) == 3:
            ap = ap.rearrange("p (a b c) -> p a b c", a=free_shape[0], b=free_shape[1])
        elif len(free_shape) == 4:
            ap = ap.rearrange("p (a b c d) -> p a b c d", a=free_shape[0], b=free_shape[1], c=free_shape[2])
        if parts != 128:
            ap = ap[0:parts]
        self.off += nb
        return ap

    def mark(self):
        self.marks.append(self.off)

    def release(self):
        if os.environ.get("KVERB"):
            print("arena high-water at release:", self.off, "of", self.cap)
        self.off = self.marks.pop()


def seq_gcol(s):
    return 0 if s == 0 else TP + (s - 1) * LS


def make_blocks(nb):
    blocks = []
    for b in range(TP // nb):
        blocks.append((b * nb, [dict(seq=0, c0=0, L=nb, t0=b * nb, first=(b == 0), last=(b == TP // nb - 1))]))
    per = nb // LS
    for b in range(NS * LS // nb):
        segs = []
        for i in range(per):
            segs.append(dict(seq=1 + b * per + i, c0=i * LS, L=LS, t0=0, first=True, last=True))
        blocks.append((TP + b * nb, segs))
    return blocks


def build(nlayers=DEPTH):
    nc = bass.Bass("TRN2", target_bir_lowering=False)

    def din(name, shape, dt=F32):
        return nc.dram_tensor(name, list(shape), dt, kind="ExternalInput").ap()

    def dout(name, shape, dt=F32):
        return nc.dram_tensor(name, list(shape), dt, kind="ExternalOutput").ap()

    def dint(name, shape, dt=F32):
        return nc.dram_tensor(name, list(shape), dt, kind="Internal").ap()

    xT = din("xT", [D, NT])
    cT = din("cT", [D, NSEQ])
    sconvT = din("sconvT", [2, NS, D, CK - 1])
    smconvT = din("smconvT", [2, NS, MW, MK - 1])
    sCT = din("sCT", [2, NS, H, DH, DH])
    snT = din("snT", [2, 128, NS * H * 4])
    smT = din("smT", [2, H, NS])
    smallp_d = din("smallp", [128, NSMALL])
    constp_d = din("constp", [128, NCONST])
    ada_w = din("ada_w", [DEPTH, D, 3 * D])
    cv_w_in = din("cv_w_in", [2, D, 3 * D])
    cv_w_out = din("cv_w_out", [2, D, D])
    ml_w_in = din("ml_w_in", [2, D, 3 * MW])
    ml_w_q = din("ml_w_q", [2, H, DH, DH])
    ml_w_k = din("ml_w_k", [2, H, DH, DH])
    ml_w_v = din("ml_w_v", [2, H, DH, DH])
    wgate_d = din("wgate", [2, 128, 3 * 16 * 8])
    ml_w_out = din("ml_w_out", [2, MW, D])

    yT = dout("yT", [D, NT])
    oconvT = dout("oconvT", [2, NSEQ, D, CK - 1])
    omconvT = dout("omconvT", [2, NSEQ, MW, MK - 1])
    oCT = dout("oCT", [2, NSEQ, H, DH, DH])
    onT = dout("onT", [2, 128, NSEQ * H * 4])
    omT = dout("omT", [2, H, NSEQ])

    xres = dint("xres", [D, NT])
    XM = dint("XMs", [MW, NT], BF16)
    ZZ = dint("ZZs", [MW, NT], BF16)
    OO = dint("OOs", [NT, MW], BF16)
    YY = dint("YYs", [MW, NT], BF16)

    xT_v = xT.rearrange("(c p) t -> p c t", p=128)
    xres_v = xres.rearrange("(c p) t -> p c t", p=128)
    yT_v = yT.rearrange("(c p) t -> p c t", p=128)
    XM_v = XM.rearrange("(c p) t -> p c t", p=128)
    ZZ_v = ZZ.rearrange("(c p) t -> p c t", p=128)
    YY_v = YY.rearrange("(c p) t -> p c t", p=128)

    CAP = int(os.environ.get('KCAP', '207')) * 1024
    with contextlib.ExitStack() as st:
        arena_t = st.enter_context(nc.sbuf_tensor("arena", [128, CAP // 4], F32))
        pst = [st.enter_context(nc.psum_tensor("ps%d" % i, [128, 512], F32)) for i in range(8)]
        S = Sched(nc)
        A = Arena(arena_t, CAP)
        psi = [0]

        psw_ = [6]

        def PS():
            i = psi[0] % psw_[0]
            psi[0] += 1
            return pst[i], ("ps", i)

        def PSX(i):
            return pst[6 + i], ("ps", 6 + i)

        def act(out, in_, func, reads, writes, bias=None, scale=None, eng="act"):
            kw = {}
            if bias is not None:
                kw["bias"] = bias
            if scale is not None:
                kw["scale"] = scale
            S.op(eng, lambda e: e.activation(out=out, in_=in_, func=func, **kw), reads, writes)

        def tt(out, in0, in1, op, reads, writes, eng="dve"):
            S.op(eng, lambda e: e.tensor_tensor(out=out, in0=in0, in1=in1, op=op), reads, writes)

        def ts(out, in0, s1, s2, op0, op1, reads, writes, eng="dve"):
            if op1 is None:
                S.op(eng, lambda e: e.tensor_scalar(out=out, in0=in0, scalar1=s1, scalar2=None, op0=op0), reads, writes)
            else:
                S.op(eng, lambda e: e.tensor_scalar(out=out, in0=in0, scalar1=s1, scalar2=s2, op0=op0, op1=op1), reads, writes)

        def stt(out, in0, scalar, in1, op0, op1, reads, writes):
            S.op("dve", lambda e: e.scalar_tensor_tensor(out=out, in0=in0, scalar=scalar, in1=in1, op0=op0, op1=op1), reads, writes)

        def cp(out, in_, reads, writes, eng="dve"):
            S.op(eng, lambda e: e.tensor_copy(out=out, in_=in_), reads, writes)

        def memset(ap, val, writes, eng="dve"):
            S.op(eng, lambda e: e.memset(ap, val), (), writes)

        def recip(out, in_, reads, writes):
            S.op("dve", lambda e: e.reciprocal(out=out, in_=in_), reads, writes)

        def mm(ps_ap, lhsT, rhs, start, stop, reads, pskey):
            S.op("pe", lambda e: e.matmul(ps_ap, lhsT=lhsT, rhs=rhs, start=start, stop=stop), reads, [pskey])

        def tr(ps_ap, in_, ident, reads, pskey):
            S.op("pe", lambda e: e.transpose(out=ps_ap, in_=in_, identity=ident), reads, [pskey])

        def dma(out, in_, reads, writes, sem, eng="sp"):
            S.op(eng, lambda e: e.dma_start(out=out, in_=in_), reads, writes, dma=sem)

        smallp = A.alloc([NSMALL], F32)
        constp = A.alloc([NCONST], F32)
        constb = A.alloc([NCONSTB], BF16)
        modt = A.alloc([DEPTH, 24, NSEQ], F32)
        Amod = A.alloc([DEPTH, 8, NSEQ], F32)
        dma(smallp, smallp_d, (), ["smallp"], "ld_smallp")
        dma(constp, constp_d, (), ["constp"], "ld_constp")
        cp(constb, constp[:, 0:NCONSTB], ["constp"], ["constb"])
        ident_f = constp[:, C_ID:C_ID + 128]
        ident_b = constb[:, C_ID:C_ID + 128]
        ones_b = constb[:, C_ONES:C_ONES + 128]
        ones_f = constp[:, C_ONES:C_ONES + 128]
        maskT = constp[:, C_MASK:C_MASK + 128]
        CONST = ["constp", "constb", "smallp"]

        def sm(name, off, n=1):
            o = _SM[name] + off
            return smallp[:, o:o + n]

        A.mark()
        cTf = A.alloc([8, NSEQ], BF16)
        dma(cTf, cT.rearrange("(c p) s -> p c s", p=128), (), ["cTf"], "ld_cT", eng="pool")
        adw = [A.alloc([8, 3 * D], BF16) for _ in range(2)]
        for l in range(nlayers):
            w = adw[l % 2]
            wk = ("adw", l % 2)
            dma(w, ada_w[l].rearrange("(c p) e -> p c e", p=128), (), [wk], "ld_adw%d" % (l % 2), eng="pool")
            ps, pk = PS()
            for e in range(24):
                for c in range(8):
                    mm(ps[:, e * NSEQ:(e + 1) * NSEQ], w[:, c, e * 128:(e + 1) * 128], cTf[:, c, :], c == 0, c == 7,
                       [wk, "cTf"], pk)
            psv = ps[:, 0:24 * NSEQ].rearrange("p (e s) -> p e s", s=NSEQ)
            for s in range(NSEQ):
                tt(modt[:, l, :, s], psv[:, :, s], sm("ada_b", l * 24, 24), ALU.add, [pk, "smallp"], ["modt"])
            for s in range(NSEQ):
                stt(Amod[:, l, :, s], modt[:, l, 8:16, s], 1.0, sm("norm_g", l * 8, 8), ALU.add, ALU.mult,
                    ["modt", "smallp"], ["Amod"])
        S.barrier()
        A.release()

        def load_x(xt, xkey, src_v, g0, nb):
            dma(xt[:, :, 0:nb], src_v[:, :, g0:g0 + nb], (), [xkey], "ld_" + xkey[0] + str(xkey[1]))

        def norm_h(l, xt, xkey, ht, xsq, rstd, tmp, segs, nb, tmpkey="tmp", xsq_w=(), h_w=lambda c: []):
            act(xsq[:, :, 0:nb], xt[:, :, 0:nb], AF.Square, [xkey], ["xsq"] + list(xsq_w))
            ps, pk = PS()
            for c in range(8):
                mm(ps[:, 0:nb], ones_b, xsq[:, c, 0:nb], c == 0, c == 7, ["xsq", "constb"], pk)
            act(rstd[:, 0:nb], ps[:, 0:nb], AF.Sqrt, [pk], ["rstd"], bias=EPS, scale=1.0 / D)
            recip(rstd[:, 0:nb], rstd[:, 0:nb], ["rstd"], ["rstd"])
            for sg in segs:
                c0, L, s = sg["c0"], sg["L"], sg["seq"]
                for c in range(8):
                    tk = (tmpkey, c % 2)
                    tt(tmp[:, c % 2, 0:L], xt[:, c, c0:c0 + L], rstd[:, c0:c0 + L], ALU.mult, [xkey, "rstd"], [tk])
                    act(ht[:, c, c0:c0 + L], tmp[:, c % 2, 0:L], AF.Identity, [tk, "Amod", "modt"], [("h", c)] + h_w(c),
                        bias=modt[:, l, c, s:s + 1], scale=Amod[:, l, c, s:s + 1])

        HK = [("h", c) for c in range(8)]

        def conv_layer(l, src_v):
            j = l // 2
            NB = 256
            blocks = make_blocks(NB)
            A.mark()
            w_in = A.alloc([8, 3 * D], BF16)
            w_out = A.alloc([8, D], BF16)
            diag = A.alloc([8, CK, 128], BF16)
            dma(w_in, cv_w_in[j].rearrange("(c p) e -> p c e", p=128), (), ["w_in"], "ld_w_in", eng="pool")
            dma(w_out, cv_w_out[j].rearrange("(c p) e -> p c e", p=128), (), ["w_out"], "ld_w_out", eng="pool")
            for c in range(8):
                for k in range(CK):
                    ts(diag[:, c, k, :], ident_f, sm("cv_w_dw", (j * 8 + c) * CK + k), None, ALU.mult, None,
                       CONST, ["diag"], eng="dve")
            xts = [A.alloc([8, NB], F32) for _ in range(2)]
            ht = A.alloc([8, NB], BF16)
            xsq = A.alloc([8, NB], BF16)
            rstd = A.alloc([NB], F32)
            HL = CK - 1
            gb = [A.alloc([8, 4 * (HL + LS)], BF16) for _ in range(2)]
            gtail = A.alloc([8, 4, HL], F32)
            hst = A.alloc([8, 4, HL], F32)
            yf = A.alloc([8, NB], F32)
            ybf = A.alloc([2, NB], BF16)
            ysq = A.alloc([2, NB], BF16)
            zz = xsq
            yo = ht
            mean = A.alloc([NB], F32)
            rs2 = A.alloc([NB], F32)
            m2 = A.alloc([NB], F32)
            t1 = A.alloc([2, NB], F32)
            t2 = A.alloc([2, NB], F32)
            s3 = A.alloc([2, NB], F32)
            tmp = t1
            sig = t2
            load_x(xts[0], ("x", 0), src_v, blocks[0][0], NB)
            for bi, (g0, segs) in enumerate(blocks):
                xt, xk = xts[bi % 2], ("x", bi % 2)
                g, gk = gb[bi % 2], ("gb", bi % 2)
                gp, gpk = gb[(bi + 1) % 2], ("gb", (bi + 1) % 2)
                if bi + 1 < len(blocks):
                    load_x(xts[(bi + 1) % 2], ("x", (bi + 1) % 2), src_v if True else None, blocks[bi + 1][0], NB)
                norm_h(l, xt, xk, ht, xsq, rstd, tmp, segs, NB, tmpkey="t1", xsq_w=[("zz", c) for c in range(8)],
                       h_w=lambda c: [("yo", c)])
                for si, sg in enumerate(segs):
                    if sg["first"]:
                        if sg["seq"] == 0:
                            memset(g[:, :, 0:HL], 0.0, [(gk, "h")], eng="pool")
                        else:
                            dma(hst[:, :, si, :], sconvT[j, sg["seq"] - 1].rearrange("(c p) k -> p c k", p=128), (),
                                [("hst", si)], "ld_hst%d" % si)
                            cp(g[:, :, si * (HL + LS):si * (HL + LS) + HL], hst[:, :, si, :], [("hst", si)], [(gk, "h")], eng="pool")
                    else:
                        cp(g[:, :, 0:HL], gp[:, :, NB:NB + HL], [gpk, (gpk, "h")], [(gk, "h")], eng="pool")
                for cc in range(8):
                    ps_g, pkg = PS()
                    for c in range(8):
                        mm(ps_g[:, 0:NB], w_in[:, c, (8 + cc) * 128:(9 + cc) * 128], ht[:, c, :], c == 0, c == 7,
                           HK + ["w_in"], pkg)
                    sk = ("t2", cc % 2)
                    act(sig[:, cc % 2, :], ps_g[:, 0:NB], AF.Sigmoid, [pkg], [sk])
                    ps_a, pka = PS()
                    for c in range(8):
                        mm(ps_a[:, 0:NB], w_in[:, c, cc * 128:(cc + 1) * 128], ht[:, c, :], c == 0, c == 7,
                           HK + ["w_in"], pka)
                    for si, sg in enumerate(segs):
                        c0, L = sg["c0"], sg["L"]
                        go = si * (HL + L)
                        tt(g[:, cc, go + HL:go + HL + L], ps_a[:, c0:c0 + L], sig[:, cc % 2, c0:c0 + L], ALU.mult,
                           [pka, sk], [gk])
                        if sg["last"]:
                            tt(gtail[:, cc, si, :], ps_a[:, c0 + L - HL:c0 + L], sig[:, cc % 2, c0 + L - HL:c0 + L],
                               ALU.mult, [pka, sk], [("gtail", si)])
                for si, sg in enumerate(segs):
                    if sg["last"]:
                        dma(oconvT[j, sg["seq"]].rearrange("(c p) k -> p c k", p=128), gtail[:, :, si, :],
                            [("gtail", si)], (), "st_gtail%d" % si)
                ps_m, pkm = PSX(0)
                ps_q, pkq = PSX(1)
                for cc in range(8):
                    ps_c, pkc = PS()
                    for si, sg in enumerate(segs):
                        c0, L = sg["c0"], sg["L"]
                        for k in range(CK):
                            mm(ps_c[:, c0:c0 + L], diag[:, cc, k, :],
                               g[:, cc, si * (HL + L) + k:si * (HL + L) + k + L], k == 0, k == CK - 1,
                               [gk, (gk, "h"), "diag"], pkc)
                    k2 = cc % 2
                    act(yf[:, cc, :], ps_c[:, 0:NB], AF.Identity, [pkc, "smallp"], [("yf", cc)],
                        bias=sm("cv_b_dw", j * 8 + cc))
                    cp(ybf[:, k2, :], yf[:, cc, :], [("yf", cc)], [("ybf", k2)], eng="pool")
                    act(ysq[:, k2, :], yf[:, cc, :], AF.Square, [("yf", cc)], [("ysq", k2)])
                    mm(ps_m[:, 0:NB], ones_b, ybf[:, k2, :], cc == 0, cc == 7, [("ybf", k2), "constb"], pkm)
                    mm(ps_q[:, 0:NB], ones_b, ysq[:, k2, :], cc == 0, cc == 7, [("ysq", k2), "constb"], pkq)
                for cc in range(8):
                    ps_z, pkz = PS()
                    for c in range(8):
                        mm(ps_z[:, 0:NB], w_in[:, c, (16 + cc) * 128:(17 + cc) * 128], ht[:, c, :], c == 0, c == 7,
                           HK + ["w_in"], pkz)
                    act(zz[:, cc, :], ps_z[:, 0:NB], AF.Silu, [pkz], [("zz", cc), "xsq"])
                ts(mean, ps_m[:, 0:NB], 1.0 / D, None, ALU.mult, None, [pkm], ["mean"])
                tt(m2, mean, mean, ALU.mult, ["mean"], ["m2"])
                stt(rs2, ps_q[:, 0:NB], 1.0 / D, m2, ALU.mult, ALU.subtract, [pkq, "m2"], ["rs2"])
                act(rs2, rs2, AF.Sqrt, ["rs2"], ["rs2"], bias=EPS, scale=1.0)
                recip(rs2, rs2, ["rs2"], ["rs2"])
                for cc in range(8):
                    k2 = cc % 2
                    tt(t1[:, k2, :], yf[:, cc, :], mean, ALU.subtract, [("yf", cc), "mean"], [("t1", k2)], eng="pool")
                    tt(t2[:, k2, :], t1[:, k2, :], rs2, ALU.mult, [("t1", k2), "rs2"], [("t2", k2)])
                    act(s3[:, k2, :], t2[:, k2, :], AF.Silu, [("t2", k2), "smallp"], [("s3", k2)],
                        bias=sm("cv_ln_b", j * 8 + cc), scale=sm("cv_ln_g", j * 8 + cc))
                    tt(yo[:, cc, :], s3[:, k2, :], zz[:, cc, :], ALU.mult, [("s3", k2), ("zz", cc)], [("yo", cc), ("h", cc)])
                YOK = [("yo", c) for c in range(8)]
                for dc in range(8):
                    ps_o, pko = PS()
                    for c in range(8):
                        mm(ps_o[:, 0:NB], w_out[:, c, dc * 128:(dc + 1) * 128], yo[:, c, :], c == 0, c == 7,
                           YOK + ["w_out"], pko)
                    for sg in segs:
                        c0, L, s = sg["c0"], sg["L"], sg["seq"]
                        stt(xt[:, dc, c0:c0 + L], ps_o[:, c0:c0 + L], modt[:, l, 16 + dc, s:s + 1],
                            xt[:, dc, c0:c0 + L], ALU.mult, ALU.add, [pko, "modt", xk], [xk])
                dma(xres_v[:, :, g0:g0 + NB], xt[:, :, 0:NB], [xk], (), "st_x%d" % (bi % 2))
            S.barrier()
            A.release()

        def mlstm_A(l, src_v):
            j = l // 2
            NB = 256
            blocks = make_blocks(NB)
            A.mark()
            w_in = A.alloc([8, 3 * MW], BF16)
            dma(w_in, ml_w_in[j].rearrange("(c p) e -> p c e", p=128), (), ["w_in"], "ld_w_in", eng="pool")
            xts = [A.alloc([8, NB], F32) for _ in range(2)]
            ht = A.alloc([8, NB], BF16)
            xsq = A.alloc([8, NB], BF16)
            rstd = A.alloc([NB], F32)
            tmp = A.alloc([2, NB], F32)
            xm = [A.alloc([16, NB], BF16) for _ in range(2)]
            zt = [A.alloc([16, NB], BF16) for _ in range(2)]
            ot = [A.alloc([2, MW], BF16) for _ in range(2)]
            xtail = A.alloc([16, 4, MK - 1], F32)
            load_x(xts[0], ("x", 0), src_v, blocks[0][0], NB)
            for bi, (g0, segs) in enumerate(blocks):
                b2 = bi % 2
                xt, xk = xts[b2], ("x", b2)
                if bi + 1 < len(blocks):
                    load_x(xts[(bi + 1) % 2], ("x", (bi + 1) % 2), src_v, blocks[bi + 1][0], NB)
                norm_h(l, xt, xk, ht, xsq, rstd, tmp, segs, NB)
                for cc in range(16):
                    ps, pk = PS()
                    for c in range(8):
                        mm(ps[:, 0:NB], w_in[:, c, cc * 128:(cc + 1) * 128], ht[:, c, :], c == 0, c == 7,
                           HK + ["w_in"], pk)
                    if cc % 2 == 0:
                        act(xm[b2][:, cc, :], ps[:, 0:NB], AF.Copy, [pk], [("xm", b2, cc)])
                    else:
                        cp(xm[b2][:, cc, :], ps[:, 0:NB], [pk], [("xm", b2, cc)])
                    for si, sg in enumerate(segs):
                        if sg["last"]:
                            c0, L = sg["c0"], sg["L"]
                            cp(xtail[:, cc, si, :], ps[:, c0 + L - (MK - 1):c0 + L], [pk], [("xtail", si)])
                for si, sg in enumerate(segs):
                    if sg["last"]:
                        dma(omconvT[j, sg["seq"]].rearrange("(c p) k -> p c k", p=128), xtail[:, :, si, :],
                            [("xtail", si)], (), "st_xtail%d" % si)
                dma(XM_v[:, :, g0:g0 + NB], xm[b2], [("xm", b2, cc) for cc in range(16)], (), "st_xm%d" % b2)
                for cc in range(16):
                    ps, pk = PS()
                    for c in range(8):
                        mm(ps[:, 0:NB], w_in[:, c, MW + cc * 128:MW + (cc + 1) * 128], ht[:, c, :], c == 0, c == 7,
                           HK + ["w_in"], pk)
                    act(zt[b2][:, cc, :], ps[:, 0:NB], AF.Silu, [pk], [("zt", b2, cc)])
                dma(ZZ_v[:, :, g0:g0 + NB], zt[b2], [("zt", b2, cc) for cc in range(16)], (), "st_zt%d" % b2)
                for tti in range(NB // 128):
                    for oc in range(4):
                        ps, pk = PS()
                        for c in range(8):
                            mm(ps[:, :], ht[:, c, tti * 128:(tti + 1) * 128],
                               w_in[:, c, 2 * MW + oc * 512:2 * MW + (oc + 1) * 512], c == 0, c == 7,
                               HK + ["w_in"], pk)
                        act(ot[b2][:, tti, oc * 512:(oc + 1) * 512], ps[:, :], AF.Sigmoid, [pk], [("ot", b2)])
                dma(OO[g0:g0 + NB, :].rearrange("(a p) e -> p a e", p=128), ot[b2], [("ot", b2)], (), "st_ot%d" % b2)
            S.barrier()
            A.release()

        def mlstm_B(l):
            j = l // 2
            NB = 128
            blocks = make_blocks(NB)
            A.mark()
            psw_[0] = 8
            wq = A.alloc([H, 4, DH], BF16)
            wk_ = A.alloc([H, 4, DH], BF16)
            wv = A.alloc([H, 4, DH], BF16)
            wg = A.alloc([3, 16, 8], BF16)
            diag = A.alloc([16, MK, 128], BF16)
            for wt_, src, nm in ((wq, ml_w_q, "wq"), (wk_, ml_w_k, "wk"), (wv, ml_w_v, "wv")):
                for h in range(H):
                    dma(wt_[:, h, :, :], src[j, h].rearrange("(c p) e -> p c e", p=128), (), [nm], "ld_" + nm,
                        eng="pool")
            dma(wg, wgate_d[j].rearrange("p (a b c) -> p a b c", a=3, b=16), (), ["wg"], "ld_wg", eng="pool")
            for c in range(16):
                for k in range(MK):
                    ts(diag[:, c, k, :], ident_f, sm("ml_w_conv", (j * 16 + c) * MK + k), None, ALU.mult, None,
                       CONST, ["diag"], eng="dve")
            CT32 = A.alloc([H, 4, DH], F32)
            CTb = A.alloc([H, 4, DH], BF16)
            n32 = A.alloc([H, 4], F32)
            nb_ = A.alloc([H, 4], BF16)
            nin = A.alloc([NS * H * 4], F32)
            nout = A.alloc([NSEQ * H * 4], F32)
            min_ = A.alloc([NS], F32, parts=4)
            mout = A.alloc([NSEQ], F32, parts=4)
            mcur = [A.alloc([1], F32, parts=4) for _ in range(2)]
            zeros4 = A.alloc([2 * LS], F32, parts=4)
            bgt = A.alloc([1], F32, parts=8)
            HM = MK - 1
            HP = 32
            xmb = [A.alloc([16, 2 * HP + NB], BF16) for _ in range(2)]
            xmh = A.alloc([16, 2, HM], F32)
            zt = [A.alloc([16, NB], BF16) for _ in range(1)]
            ot = [A.alloc([H * DH], BF16) for _ in range(1)]
            xc = A.alloc([16, NB], BF16)
            sxc = A.alloc([16, NB], BF16)
            qT = A.alloc([16, NB], BF16)
            kT = A.alloc([16, NB], BF16)
            vT = A.alloc([16, NB], BF16)
            yv = [A.alloc([16, NB], BF16) for _ in range(1)]
            g8 = A.alloc([NB], F32, parts=8)
            fgb = A.alloc([NB], F32, parts=4)
            a1 = A.alloc([NB], F32, parts=4)
            lf = A.alloc([NB], F32, parts=4)
            brow = A.alloc([2 * LS], F32, parts=4)
            grow = A.alloc([2 * LS], F32, parts=4)
            Mrow = A.alloc([2 * LS], F32, parts=4)
            negM = A.alloc([2 * LS], F32, parts=4)
            rowA = A.alloc([2 * LS], F32, parts=4)
            rowB = A.alloc([2 * LS], F32, parts=4)
            rowC = A.alloc([2 * LS], F32, parts=4)
            dgA = A.alloc([2, 4], F32, parts=4)
            cols = A.alloc([2, 12], F32)
            wks = A.alloc([2, 4], F32)
            decb = A.alloc([2, 4], F32)
            WT = [A.alloc([2 * LS], F32) for _ in range(H)]
            SpT = [A.alloc([2 * LS], BF16) for _ in range(H)]
            kp = [A.alloc([DH], BF16) for _ in range(H)]
            vtok = [A.alloc([DH], BF16) for _ in range(H)]
            dsb = A.alloc([2 * H], F32)
            dd = A.alloc([4, H], F32)
            tmpA = [A.alloc([DH], F32) for _ in range(H)]
            bst = A.alloc([H, 8], F32)
            hn = [A.alloc([DH], BF16) for _ in range(H)]
            t1 = A.alloc([H, 4 * 2 * LS], F32)
            selrow = constp[0:4, C_SELROW:C_SELROW + 4 * 128].rearrange("p (h t) -> p h t", h=4)
            selF = constp[0:8, C_SELF:C_SELF + 4]
            id4 = constp[0:4, C_ID:C_ID + 4]
            memset(zeros4, 0.0, ["zeros4"])
            cp(bgt, smallp[0:8, _SM["b_gate"] + j:_SM["b_gate"] + j + 1], ["smallp"], ["bgt"])
            dma(nin, snT[j], (), ["nin"], "ld_nin")
            dma(min_, smT[j], (), ["min"], "ld_min")
            hc = [0]
            mi = [0]
            CK32 = [("C32", h) for h in range(H)]
            CKB = [("Cb", h) for h in range(H)]

            def load_xm(bj):
                g0_, segs_ = blocks[bj]
                xb, xbk = xmb[bj % 2], ("xmb", bj % 2)
                for si, sg in enumerate(segs_):
                    c0, L, s = sg["c0"], sg["L"], sg["seq"]
                    gc = g0_ + c0
                    xo = si * (HP + L) + HP - HM
                    if sg["first"]:
                        if s == 0:
                            memset(xb[:, :, xo:xo + HM], 0.0, [(xbk, "h")], eng="pool")
                        else:
                            dma(xmh[:, :, si, :], smconvT[j, s - 1].rearrange("(c p) k -> p c k", p=128), (),
                                [("xmh", si)], "ld_xmh%d" % si)
                            cp(xb[:, :, xo:xo + HM], xmh[:, :, si, :], [("xmh", si)], [(xbk, "h")], eng="pool")
                        dma(xb[:, :, xo + HM:xo + HM + L], XM_v[:, :, gc:gc + L], (), [xbk], "ld_xmb%d" % (bj % 2))
                    else:
                        dma(xb[:, :, xo:xo + HM + L], XM_v[:, :, gc - HM:gc + L], (), [xbk, (xbk, "h")],
                            "ld_xmb%d" % (bj % 2))

            BP = {}

            def block_part(bi):
                g0, segs = blocks[bi]
                b2 = bi % 2
                xb, xbk = xmb[b2], ("xmb", b2)
                if bi == 0:
                    load_xm(0)
                    dma(zt[0], ZZ_v[:, :, g0:g0 + NB], (), [("zt", 0)], "ld_zt0")
                if bi + 1 < len(blocks):
                    load_xm(bi + 1)
                XBK = [xbk, (xbk, "h")]
                for cg in range(4):
                    ps, pk = PS()
                    for ci in range(4):
                        cc = cg * 4 + ci
                        for si, sg in enumerate(segs):
                            c0, L = sg["c0"], sg["L"]
                            for k in range(MK):
                                mm(ps[:, ci * NB + c0:ci * NB + c0 + L], diag[:, cc, k, :],
                                   xb[:, cc, si * (HP + L) + HP - HM + k:si * (HP + L) + HP - HM + k + L],
                                   k == 0, k == MK - 1, XBK + ["diag"], pk)
                    for ci in range(4):
                        cc = cg * 4 + ci
                        act(xc[:, cc, :], ps[:, ci * NB:(ci + 1) * NB], AF.Silu, [pk, "smallp"], [("xc", cc)],
                            bias=sm("ml_b_conv", j * 16 + cc))
                for (dst, dk, wt_, wn, srcsel) in ((qT, "qT", wq, "wq", 0), (kT, "kT", wk_, "wk", 0), (vT, "vT", wv, "wv", 1)):
                    for h in range(H):
                        ps, pk = PS()
                        for ec in range(4):
                            if srcsel == 0:
                                for kc in range(4):
                                    cc = h * 4 + kc
                                    mm(ps[:, ec * NB:(ec + 1) * NB], wt_[:, h, kc, ec * 128:(ec + 1) * 128],
                                       xc[:, cc, :], kc == 0, kc == 3, [("xc", cc), wn], pk)
                            else:
                                for si, sg in enumerate(segs):
                                    c0, L = sg["c0"], sg["L"]
                                    for kc in range(4):
                                        cc = h * 4 + kc
                                        mm(ps[:, ec * NB + c0:ec * NB + c0 + L],
                                           wt_[:, h, kc, ec * 128:(ec + 1) * 128],
                                           xb[:, cc, si * (HP + L) + HP:si * (HP + L) + HP + L],
                                           kc == 0, kc == 3, XBK + [wn], pk)
                        dv = dst[:, h * 4:(h + 1) * 4, :]
                        pv = ps[:, :].rearrange("p (a t) -> p a t", a=4)
                        if h % 2 == 0:
                            act(dv, pv, AF.Copy, [pk], [(dk, h)])
                        else:
                            cp(dv, pv, [pk], [(dk, h)])
                ps, pk = PS()
                n = 0
                for si_, (srcT, dk) in enumerate(((qT, "qT"), (kT, "kT"), (vT, "vT"))):
                    for cc in range(16):
                        mm(ps[0:8, 0:NB], wg[:, si_, cc, :], srcT[:, cc, :], n == 0, n == 47, [(dk, cc // 4), "wg"], pk)
                        n += 1
                act(g8, ps[0:8, 0:NB], AF.Identity, [pk, "bgt"], ["g8"], bias=bgt[:, 0:1])
                ps2, pk2 = PS()
                mm(ps2[0:4, 0:NB], selF, g8, True, True, ["g8", "constp"], pk2)
                cp(fgb, ps2[0:4, 0:NB], [pk2], ["fgb"])
                act(a1, fgb, AF.Abs, ["fgb"], ["a1"])
                act(a1, a1, AF.Exp, ["a1"], ["a1"], scale=-1.0)
                act(a1, a1, AF.Ln, ["a1"], ["a1"], bias=1.0)
                ts(lf, fgb, 0.0, None, ALU.min, None, ["fgb"], ["lf"])
                tt(lf, lf, a1, ALU.subtract, ["lf", "a1"], ["lf"])
                chunks = []
                for si, sg in enumerate(segs):
                    Lc = 2 * LS if sg["L"] % (2 * LS) == 0 else LS
                    for ch in range(sg["L"] // Lc):
                        chunks.append((sg, ch, Lc))
                for ci, (sg, ch, Lc) in enumerate(chunks):
                    s = sg["seq"]
                    cs = sg["c0"] + ch * Lc
                    csl = slice(cs, cs + Lc)
                    rs = slice(ci * Lc, (ci + 1) * Lc)
                    lastc = slice((ci + 1) * Lc - 1, (ci + 1) * Lc)
                    first = sg["first"] and ch == 0
                    last = sg["last"] and ch == sg["L"] // Lc - 1
                    if first:
                        if s == 0:
                            memset(mcur[mi[0] % 2], 0.0, [("m", mi[0] % 2)])
                        else:
                            cp(mcur[mi[0] % 2], min_[:, s - 1:s], ["min"], [("m", mi[0] % 2)])
                    m_t, mk_ = mcur[mi[0] % 2], ("m", mi[0] % 2)
                    m_n, mkn = mcur[(mi[0] + 1) % 2], ("m", (mi[0] + 1) % 2)
                    mi[0] += 1
                    kb, kg, kM, kn, kA, kB, kC = [(nm, ci) for nm in ("brow", "grow", "Mrow", "negM", "rowA", "rowB", "rowC")]
                    S.op("dve", lambda e, o=brow[:, rs], d0=lf[:, csl], z0=zeros4[:, 0:Lc]: e.tensor_tensor_scan(
                        out=o, data0=d0, data1=z0, initial=0.0, op0=ALU.add, op1=ALU.add),
                        ["lf", "zeros4"], [kb])
                    tt(grow[:, rs], g8[0:4, csl], brow[:, rs], ALU.subtract, ["g8", kb], [kg])
                    S.op("dve", lambda e, o=Mrow[:, rs], d0=grow[:, rs], ini=m_t[:, 0:1]: e.tensor_tensor_scan(
                        out=o, data0=d0, data1=d0, initial=ini, op0=ALU.max, op1=ALU.max),
                        [kg, mk_], [kM])
                    ts(negM[:, rs], Mrow[:, rs], -1.0, None, ALU.mult, None, [kM], [kn])
                    ts(rowA[:, rs], Mrow[:, rs], -1.0, m_t[:, 0:1], ALU.mult, ALU.add, [kM, mk_], [kA])
                    stt(rowB[:, rs], brow[:, rs], -1.0, Mrow[:, rs], ALU.mult, ALU.subtract, [kb, kM], [kB])
                    ts(rowC[:, rs], grow[:, rs], Mrow[:, lastc], None, ALU.subtract, None, [kg, kM], [kC])
                    tt(m_n, brow[:, lastc], Mrow[:, lastc], ALU.add, [kb, kM], [mkn])
                    ts(dgA[:, ci, :], id4, rowA[:, lastc], None, ALU.mult, None, [kA, "constp"], [("dgA", ci)])
                    if last:
                        cp(mout[:, s:s + 1], m_n, [mkn], ["mout"])
                BP[bi] = chunks

            def sxc_part(bi):
                for cc in range(16):
                    ts(sxc[:, cc, :], xc[:, cc, :], sm("ml_skip", j * 16 + cc), None, ALU.mult, None,
                       [("xc", cc), "smallp"], [("sxc", cc)], eng="dve")

            def rec_a(bi, cix):
                g0, segs = blocks[bi]
                b2 = bi % 2
                for ci, (sg, ch, Lc) in [(cix, BP[bi][cix])]:
                    s = sg["seq"]
                    cs = sg["c0"] + ch * Lc
                    csl = slice(cs, cs + Lc)
                    rs = slice(ci * Lc, (ci + 1) * Lc)
                    gcol = g0 + cs
                    first = sg["first"] and ch == 0
                    last = sg["last"] and ch == sg["L"] // Lc - 1
                    o_t, ok = ot[0], ("ot", 0)
                    dma(o_t[0:Lc], OO[gcol:gcol + Lc, :], (), [ok], "ld_ot0")
                    if first:
                        if s == 0:
                            for h in range(H):
                                memset(CT32[:, h], 0.0, [("C32", h)], eng="pool")
                                memset(CTb[:, h], 0.0, [("Cb", h)], eng="pool")
                            memset(n32, 0.0, ["n32"])
                            memset(nb_, 0.0, ["nb"])
                        else:
                            for h in range(H):
                                dma(CT32[:, h], sCT[j, s - 1, h].rearrange("(c p) v -> p c v", p=128), (),
                                    [("C32", h)], "ld_C%d" % h)
                                cp(CTb[:, h], CT32[:, h], [("C32", h)], [("Cb", h)], eng="pool")
                            nv = nin[:, (s - 1) * 16:s * 16].rearrange("p (h c) -> p h c", h=H)
                            cp(n32, nv, ["nin"], ["n32"])
                            cp(nb_, nv, ["nin"], ["nb"])
                    HQ = [[h * 4 + kc for kc in range(4)] for h in range(H)]
                    kg, kn = ("grow", ci), ("negM", ci)
                    for h in range(H):
                        psw, pkw = PS()
                        mm(psw[0:Lc, 0:Lc], grow[:, rs], selrow[:, h, 0:Lc], True, False, [kg, "constp"], pkw)
                        mm(psw[0:Lc, 0:Lc], selrow[:, h, 0:Lc], negM[:, rs], False, False, [kn, "constp"], pkw)
                        mm(psw[0:Lc, 0:Lc], ident_f[0:Lc, 0:Lc], maskT[0:Lc, 0:Lc], False, True, ["constp"], pkw)
                        act(WT[h][0:Lc, 0:Lc], psw[0:Lc, 0:Lc], AF.Exp, [pkw], [("WT", h)])
                    for h in range(H):
                        pss, pks = PS()
                        for kc in range(4):
                            mm(pss[0:Lc, 0:Lc], kT[:, HQ[h][kc], csl], qT[:, HQ[h][kc], csl], kc == 0, kc == 3,
                               [("kT", h), ("qT", h)], pks)
                        stt(SpT[h][0:Lc, 0:Lc], pss[0:Lc, 0:Lc], KSCALE, WT[h][0:Lc, 0:Lc], ALU.mult, ALU.mult,
                            [pks, ("WT", h)], [("SpT", h)])
                    for h in range(H):
                        pst_v, pkv = PS()
                        pvb = pst_v.bitcast(BF16)
                        for kc in range(4):
                            tr(pvb[0:Lc, kc * 128:(kc + 1) * 128], vT[:, HQ[h][kc], csl], ident_b, [("vT", h), "constb"], pkv)
                        cp(vtok[h][0:Lc], pvb[0:Lc, 0:DH], [pkv], [("vtok", h)])
                    kA, kB, kC = ("rowA", ci), ("rowB", ci), ("rowC", ci)
                    psc, pkc = PS()
                    for qi, (rw, rk) in enumerate(((rowA, kA), (rowB, kB), (rowC, kC))):
                        mm(psc[0:Lc, qi * 4:(qi + 1) * 4], rw[:, rs], id4, True, True, [rk, "constp"], pkc)
                    act(cols[0:Lc, ci, :], psc[0:Lc, 0:12], AF.Exp, [pkc], [("cols", ci)])
                    ts(wks[0:Lc, ci, :], cols[0:Lc, ci, 8:12], KSCALE, None, ALU.mult, None, [("cols", ci)], [("wks", ci)])
                    psd, pkd = PS()
                    mm(psd[:, 0:4], ones_f[0:4, :], dgA[:, ci, :], True, True, [("dgA", ci), "constp"], pkd)
                    act(decb[:, ci, :], psd[:, 0:4], AF.Exp, [pkd], [("decb", ci)])
                    for h in range(H):
                        pst_k, pkk = PS()
                        pkb = pst_k.bitcast(BF16)
                        for kc in range(4):
                            tr(pkb[0:Lc, kc * 128:(kc + 1) * 128], kT[:, HQ[h][kc], csl], ident_b, [("kT", h), "constb"], pkk)
                        act(kp[h][0:Lc], pkb[0:Lc, 0:DH], AF.Identity, [pkk, ("wks", ci)], [("kp", h)],
                            scale=wks[0:Lc, ci, h:h + 1])
                    psn, pkn = PS()
                    for h in range(H):
                        for kc in range(4):
                            mm(psn[0:Lc, 2 * h:2 * h + 1], qT[:, HQ[h][kc], csl], nb_[:, h, kc:kc + 1], kc == 0, kc == 3,
                               [("qT", h), "nb"], pkn)
                        mm(psn[0:Lc, 2 * h + 1:2 * h + 2], SpT[h][0:Lc, 0:Lc], ones_b[0:Lc, 0:1], True, True,
                           [("SpT", h), "constb"], pkn)
                    cp(dsb[0:Lc], psn[0:Lc, 0:2 * H], [pkn], ["dsb"])
                    dsv = dsb[0:Lc].rearrange("p (h t) -> p h t", t=2)
                    CK_ = ("cols", ci)
                    tt(dd[0:Lc, 0, :], dsv[:, :, 0], cols[0:Lc, ci, 0:4], ALU.mult, ["dsb", CK_], ["den"])
                    tt(dd[0:Lc, 0, :], dd[0:Lc, 0, :], dsv[:, :, 1], ALU.add, ["dsb", "den"], ["den"])
                    act(dd[0:Lc, 1, :], dd[0:Lc, 0, :], AF.Abs, ["den"], ["den"])
                    tt(dd[0:Lc, 1, :], dd[0:Lc, 1, :], cols[0:Lc, ci, 4:8], ALU.max, ["den", CK_], ["den"])
                    recip(dd[0:Lc, 2, :], dd[0:Lc, 1, :], ["den"], ["den"])
                    tt(dd[0:Lc, 3, :], dd[0:Lc, 2, :], cols[0:Lc, ci, 0:4], ALU.mult, ["den", CK_], ["den"])
                    for h in range(H):
                        psa, pka = PS()
                        for kc in range(4):
                            mm(psa[0:Lc, :], qT[:, HQ[h][kc], csl], CTb[:, h, kc, :], kc == 0, kc == 3,
                               [("qT", h), ("Cb", h)], pka)
                        act(tmpA[h][0:Lc], psa[0:Lc, :], AF.Identity, [pka, "den"], [("hh", h)], scale=dd[0:Lc, 3, h:h + 1])
                    for h in range(H):
                        psb, pkb_ = PS()
                        mm(psb[0:Lc, :], SpT[h][0:Lc, 0:Lc], vtok[h][0:Lc], True, True, [("SpT", h), ("vtok", h)], pkb_)
                        stt(tmpA[h][0:Lc], psb[0:Lc, :], dd[0:Lc, 2, h:h + 1], tmpA[h][0:Lc], ALU.mult, ALU.add,
                            [pkb_, "den", ("hh", h)], [("hh", h)])
                    for h in range(H):
                        tt(tmpA[h][0:Lc], tmpA[h][0:Lc], o_t[0:Lc, h * DH:(h + 1) * DH], ALU.mult, [("hh", h), ok], [("hh", h)])
                    for kc in range(4):
                        for h in range(H):
                            psc2, pkc2 = PS()
                            mm(psc2[:, :], kp[h][0:Lc, kc * 128:(kc + 1) * 128], vtok[h][0:Lc], True, True,
                               [("kp", h), ("vtok", h)], pkc2)
                            stt(CT32[:, h, kc, :], CT32[:, h, kc, :], decb[:, ci, h:h + 1], psc2[:, :], ALU.mult, ALU.add,
                                [("C32", h), ("decb", ci), pkc2], [("C32", h)])
                            act(CTb[:, h, kc, :], CT32[:, h, kc, :], AF.Copy, [("C32", h)], [("Cb", h)])
                    for h in range(H):
                        psn2, pkn2 = PS()
                        for kc in range(4):
                            mm(psn2[:, kc:kc + 1], kp[h][0:Lc, kc * 128:(kc + 1) * 128], ones_b[0:Lc, 0:1], True, True,
                               [("kp", h), "constb"], pkn2)
                        stt(n32[:, h, :], n32[:, h, :], decb[:, ci, h:h + 1], psn2[:, 0:4], ALU.mult, ALU.add,
                            ["n32", ("decb", ci), pkn2], ["n32"])
                        cp(nb_[:, h, :], n32[:, h, :], ["n32"], ["nb"])
                    if last:
                        for h in range(H):
                            dma(oCT[j, s, h].rearrange("(c p) v -> p c v", p=128), CT32[:, h], [("C32", h)], (),
                                "st_C%d" % h)
                        cp(nout[:, s * 16:(s + 1) * 16].rearrange("p (h c) -> p h c", h=H), n32, ["n32"], ["nout"])

            def rec_b(bi, cix):
                g0, segs = blocks[bi]
                b2 = bi % 2
                for ci, (sg, ch, Lc) in [(cix, BP[bi][cix])]:
                    s = sg["seq"]
                    cs = sg["c0"] + ch * Lc
                    csl = slice(cs, cs + Lc)
                    rs = slice(ci * Lc, (ci + 1) * Lc)
                    gcol = g0 + cs
                    first = sg["first"] and ch == 0
                    last = sg["last"] and ch == sg["L"] // Lc - 1
                    for h in range(H):
                        S.op("dve", lambda e, o=bst[0:Lc, h, 0:6], i=tmpA[h][0:Lc]: e.bn_stats(out=o, in_=i), [("hh", h)],
                             [("bst", h)])
                    for h in range(H):
                        S.op("dve", lambda e, o=bst[0:Lc, h, 6:8], i=bst[0:Lc, h, 0:6]: e.bn_aggr(out=o, in_=i),
                             [("bst", h)], [("bst", h)])
                    BK = [("bst", h) for h in range(H)]
                    act(bst[0:Lc, :, 7], bst[0:Lc, :, 7], AF.Sqrt, BK, BK, bias=EPS, scale=1.0)
                    recip(bst[0:Lc, :, 7], bst[0:Lc, :, 7], BK, BK)
                    for h in range(H):
                        ts(hn[h][0:Lc], tmpA[h][0:Lc], bst[0:Lc, h, 6:7], bst[0:Lc, h, 7:8], ALU.subtract, ALU.mult,
                           [("hh", h), ("bst", h)], [("hn", h)])
                    for h in range(H):
                        psh, pkh = PS()
                        phb = psh.bitcast(BF16)
                        for vc in range(4):
                            tr(phb[:, vc * Lc:(vc + 1) * Lc], hn[h][0:Lc, vc * 128:(vc + 1) * 128], ident_b[0:Lc, 0:Lc],
                               [("hn", h), "constb"], pkh)
                        for vc in range(4):
                            cc = h * 4 + vc
                            stt(t1[:, h, vc * Lc:(vc + 1) * Lc], phb[:, vc * Lc:(vc + 1) * Lc], sm("ml_gn_g", j * 16 + cc),
                                sxc[:, cc, csl], ALU.mult, ALU.add, [pkh, "smallp", ("sxc", cc)], [("t1", h)])
                    for h in range(H):
                        tt(yv[0][:, h * 4:(h + 1) * 4, csl], t1[:, h, 0:4 * Lc].rearrange("p (a t) -> p a t", a=4),
                           zt[0][:, h * 4:(h + 1) * 4, csl], ALU.mult, [("t1", h), ("zt", 0)], [("yv", 0)])

            def block_end(bi):
                g0, segs = blocks[bi]
                b2 = bi % 2
                if bi + 1 < len(blocks):
                    gn_ = blocks[bi + 1][0]
                    dma(zt[0], ZZ_v[:, :, gn_:gn_ + NB], (), [("zt", 0)], "ld_zt0")
                dma(YY_v[:, :, g0:g0 + NB], yv[0], [("yv", 0)], (), "st_yv0")

            block_part(0)
            sxc_part(0)
            for bi in range(len(blocks)):
                nch = len(BP[bi])
                for cix in range(nch):
                    rec_a(bi, cix)
                    if cix == nch - 1 and bi + 1 < len(blocks):
                        block_part(bi + 1)
                    rec_b(bi, cix)
                block_end(bi)
                if bi + 1 < len(blocks):
                    sxc_part(bi + 1)
            dma(onT[j], nout, ["nout"], (), "st_nout")
            dma(omT[j], mout, ["mout"], (), "st_mout")
            S.barrier()
            psw_[0] = 6
            A.release()

        def mlstm_C(l, final):
            j = l // 2
            NB = 256
            blocks = make_blocks(NB)
            A.mark()
            w_out = A.alloc([16, D], BF16)
            dma(w_out, ml_w_out[j].rearrange("(c p) e -> p c e", p=128), (), ["w_out"], "ld_w_out", eng="pool")
            xts = [A.alloc([8, NB], F32) for _ in range(2)]
            yts = [A.alloc([16, NB], BF16) for _ in range(2)]
            xsq = A.alloc([8, NB], BF16)
            rstd = A.alloc([NB], F32)
            yfin = [A.alloc([8, NB], F32) for _ in range(2)]

            def loads(bi):
                g0 = blocks[bi][0]
                load_x(xts[bi % 2], ("x", bi % 2), xres_v, g0, NB)
                dma(yts[bi % 2], YY_v[:, :, g0:g0 + NB], (), [("y", bi % 2)], "ld_y%d" % (bi % 2))

            loads(0)
            for bi, (g0, segs) in enumerate(blocks):
                b2 = bi % 2
                xt, xk = xts[b2], ("x", b2)
                if bi + 1 < len(blocks):
                    loads(bi + 1)
                for dc in range(8):
                    ps, pk = PS()
                    for c in range(16):
                        mm(ps[:, 0:NB], w_out[:, c, dc * 128:(dc + 1) * 128], yts[b2][:, c, :], c == 0, c == 15,
                           [("y", b2), "w_out"], pk)
                    for sg in segs:
                        c0, L, s = sg["c0"], sg["L"], sg["seq"]
                        stt(xt[:, dc, c0:c0 + L], ps[:, c0:c0 + L], modt[:, l, 16 + dc, s:s + 1], xt[:, dc, c0:c0 + L],
                            ALU.mult, ALU.add, [pk, "modt", xk], [xk])
                if not final:
                    dma(xres_v[:, :, g0:g0 + NB], xt[:, :, 0:NB], [xk], (), "st_x%d" % b2)
                else:
                    act(xsq, xt, AF.Square, [xk], ["xsq"])
                    ps, pk = PS()
                    for c in range(8):
                        mm(ps[:, 0:NB], ones_b, xsq[:, c, :], c == 0, c == 7, ["xsq", "constb"], pk)
                    act(rstd, ps[:, 0:NB], AF.Sqrt, [pk], ["rstd"], bias=EPS, scale=1.0 / D)
                    recip(rstd, rstd, ["rstd"], ["rstd"])
                    for c in range(8):
                        stt(yfin[b2][:, c, :], xt[:, c, :], sm("final_g", c), rstd, ALU.mult, ALU.mult,
                            [xk, "smallp", "rstd"], [("yfin", b2)])
                    dma(yT_v[:, :, g0:g0 + NB], yfin[b2], [("yfin", b2)], (), "st_yfin%d" % b2)
            S.barrier()
            A.release()

        src = xT_v
        for l in range(nlayers):
            if l % 2 == 0:
                conv_layer(l, src)
            else:
                mlstm_A(l, src)
                mlstm_B(l)
                mlstm_C(l, final=(l == DEPTH - 1))
            src = xres_v
        if nlayers < DEPTH:
            A.mark()
            xt = A.alloc([8, 256], F32)
            for b in range(NT // 256):
                dma(xt, xres_v[:, :, b * 256:(b + 1) * 256], (), ["xdbg"], "ld_dbg")
                dma(yT_v[:, :, b * 256:(b + 1) * 256], xt, ["xdbg"], (), "st_dbg")
            A.release()
        S.emit()
    return nc


def _vec(v):
    v = np.asarray(v, np.float32)
    lead = v.shape[:-1]
    c = v.shape[-1] // 128
    return np.moveaxis(v.reshape(lead + (c, 128)), -1, 0)


def kernel(x_prompt, x_sample, c_prompt, c_sample, state_conv, state_mconv, state_C, state_n, state_m,
           norm_g, ada_w, ada_b, cv_w_in, cv_w_dw, cv_b_dw, cv_ln_g, cv_ln_b, cv_w_out,
           ml_w_in, ml_w_conv, ml_b_conv, ml_w_q, ml_w_k, ml_w_v, ml_w_gate, ml_b_gate,
           ml_gn_g, ml_skip, ml_w_out, final_g, _nlayers=DEPTH):
    f = lambda a: np.ascontiguousarray(np.asarray(a, np.float32))
    smallp = np.zeros((128, NSMALL), np.float32)

    def put(name, arr):
        arr = np.asarray(arr, np.float32).reshape(128, -1)
        smallp[:, _SM[name]:_SM[name] + arr.shape[1]] = arr

    put("norm_g", _vec(norm_g))
    put("ada_b", _vec(ada_b))
    put("cv_w_dw", np.transpose(_vec(cv_w_dw), (0, 1, 3, 2)))
    put("cv_b_dw", _vec(cv_b_dw))
    put("cv_ln_g", _vec(cv_ln_g))
    put("cv_ln_b", _vec(cv_ln_b))
    put("ml_w_conv", np.transpose(_vec(ml_w_conv), (0, 1, 3, 2)))
    put("ml_b_conv", _vec(ml_b_conv))
    put("ml_gn_g", _vec(ml_gn_g))
    put("ml_skip", _vec(ml_skip))
    put("final_g", _vec(final_g))
    bg = np.zeros((128, 2), np.float32)
    bg[0:8, :] = np.asarray(ml_b_gate, np.float32).T
    put("b_gate", bg)
    constp = np.zeros((128, NCONST), np.float32)
    constp[:, C_ID:C_ID + 128] = np.eye(128, dtype=np.float32)
    s_idx = np.arange(128)[:, None]
    t_idx = np.arange(128)[None, :]
    constp[:, C_MASK:C_MASK + 128] = np.where(s_idx <= t_idx, 0.0, NEG)
    for h in range(4):
        constp[h, C_SELROW + h * 128:C_SELROW + (h + 1) * 128] = 1.0
        constp[4 + h, C_SELF + h] = 1.0
    constp[:, C_ONES:C_ONES + 128] = 1.0
    wg = np.asarray(ml_w_gate, np.float32).reshape(2, 3, 16, 128, 8)
    wgate = np.ascontiguousarray(np.transpose(wg, (0, 3, 1, 2, 4)).reshape(2, 128, 384))

    x_prompt = np.asarray(x_prompt, np.float32)
    x_sample = np.asarray(x_sample, np.float32)
    shared = dict(smallp=smallp, constp=constp, ada_w=f(ada_w), cv_w_in=f(cv_w_in), cv_w_out=f(cv_w_out),
                  ml_w_in=f(ml_w_in), ml_w_q=f(ml_w_q), ml_w_k=f(ml_w_k), ml_w_v=f(ml_w_v), wgate=wgate,
                  ml_w_out=f(ml_w_out))
    in_maps = []
    for c in range(8):
        sl = slice(4 * c, 4 * c + 4)
        xs = np.concatenate([x_prompt[c], x_sample[sl].reshape(NS * LS, D)], axis=0)
        cc = np.concatenate([np.asarray(c_prompt, np.float32)[c:c + 1], np.asarray(c_sample, np.float32)[sl]], axis=0)
        m = dict(shared)
        m["xT"] = np.ascontiguousarray(xs.T)
        m["cT"] = np.ascontiguousarray(cc.T)
        m["sconvT"] = np.ascontiguousarray(np.transpose(np.asarray(state_conv, np.float32)[:, sl], (0, 1, 3, 2)))
        m["smconvT"] = np.ascontiguousarray(np.transpose(np.asarray(state_mconv, np.float32)[:, sl], (0, 1, 3, 2)))
        m["sCT"] = np.ascontiguousarray(np.transpose(np.asarray(state_C, np.float32)[:, sl], (0, 1, 2, 4, 3)))
        sn = np.asarray(state_n, np.float32)[:, sl].reshape(2, NS, H, 4, 128)
        m["snT"] = np.ascontiguousarray(np.transpose(sn, (0, 4, 1, 2, 3)).reshape(2, 128, NS * H * 4))
        m["smT"] = np.ascontiguousarray(np.transpose(np.asarray(state_m, np.float32)[:, sl], (0, 2, 1)))
        in_maps.append(m)
    if _nlayers == "prep":
        return in_maps
    nc = build(_nlayers)
    res = run_bass_kernel_spmd(nc, in_maps, core_ids=list(range(8)))
    return assemble(res.results)


def assemble(R, cores=range(8)):
    y_prompt = np.empty((8, TP, D), np.float32)
    y_sample = np.empty((32, LS, D), np.float32)
    p_conv = np.empty((2, 8, CK - 1, D), np.float32)
    s_conv = np.empty((2, 32, CK - 1, D), np.float32)
    p_mconv = np.empty((2, 8, MK - 1, MW), np.float32)
    s_mconv = np.empty((2, 32, MK - 1, MW), np.float32)
    p_C = np.empty((2, 8, H, DH, DH), np.float32)
    s_C = np.empty((2, 32, H, DH, DH), np.float32)
    p_n = np.empty((2, 8, H, DH), np.float32)
    s_n = np.empty((2, 32, H, DH), np.float32)
    p_m = np.empty((2, 8, H), np.float32)
    s_m = np.empty((2, 32, H), np.float32)
    for c in cores:
        r = R[c]
        sl = slice(4 * c, 4 * c + 4)
        yt = np.asarray(r["yT"]).T
        y_prompt[c] = yt[:TP]
        y_sample[sl] = yt[TP:].reshape(NS, LS, D)
        oc = np.transpose(np.asarray(r["oconvT"]), (0, 1, 3, 2))
        p_conv[:, c] = oc[:, 0]
        s_conv[:, sl] = oc[:, 1:]
        om = np.transpose(np.asarray(r["omconvT"]), (0, 1, 3, 2))
        p_mconv[:, c] = om[:, 0]
        s_mconv[:, sl] = om[:, 1:]
        oC = np.transpose(np.asarray(r["oCT"]), (0, 1, 2, 4, 3))
        p_C[:, c] = oC[:, 0]
        s_C[:, sl] = oC[:, 1:]
        on = np.transpose(np.asarray(r["onT"]).reshape(2, 128, NSEQ, H, 4), (0, 2, 3, 4, 1)).reshape(2, NSEQ, H, DH)
        p_n[:, c] = on[:, 0]
        s_n[:, sl] = on[:, 1:]
        omm = np.transpose(np.asarray(r["omT"]), (0, 2, 1))
        p_m[:, c] = omm[:, 0]
        s_m[:, sl] = omm[:, 1:]
    return (y_prompt, y_sample, p_conv, p_mconv, p_C, p_n, p_m, s_conv, s_mconv, s_C, s_n, s_m)
```

```python
import contextlib
import os
import numpy as np
import concourse.bass as bass
import concourse.mybir as mybir
from concourse.bass_utils import run_bass_kernel_spmd

F32 = mybir.dt.float32
BF16 = mybir.dt.bfloat16
AF = mybir.ActivationFunctionType
ALU = mybir.AluOpType

D = 1024
TP = 4096
NS = 4
LS = 64
NT = TP + NS * LS
NSEQ = 5
DEPTH = 4
EPS = 1e-6
CK = 31
MW = 2048
H = 4
DH = 512
MK = 4
KSCALE = DH ** -0.5
NEG = -30000.0

_SM = {}
_o = 0
for _n, _w in [("norm_g", 32), ("ada_b", 96), ("cv_w_dw", 2 * 8 * 31), ("cv_b_dw", 16), ("cv_ln_g", 16),
               ("cv_ln_b", 16), ("ml_w_conv", 2 * 16 * 4), ("ml_b_conv", 32), ("ml_gn_g", 32), ("ml_skip", 32),
               ("final_g", 8), ("b_gate", 2)]:
    _SM[_n] = _o
    _o += _w
NSMALL = _o
C_ID = 0
C_ONES = 128
C_MASK = 256
C_SELROW = 384
C_SELF = 896
NCONST = 900
NCONSTB = 256


class Sched:
    ENG = ["pe", "act", "dve", "pool", "sp"]

    def __init__(self, nc):
        self.nc = nc
        self.prog = {e: [] for e in self.ENG}
        self.cnt = {e: 0 for e in self.ENG}
        self.seen = {e: {} for e in self.ENG}
        self.lastw = {}
        self.rd = {}
        self.dcnt = {}

    def op(self, eng, fn, reads=(), writes=(), dma=None):
        psr = [r for r in reads if isinstance(r, tuple) and len(r) == 2 and r[0] == "ps"]
        if psr:
            writes = list(writes) + [r for r in psr if r not in writes]
        deps = {}

        def add(ev):
            if ev is None:
                return
            k, v = ev
            if k == "pe" and eng == "pe":
                return
            if deps.get(k, 0) < v:
                deps[k] = v

        for r in reads:
            add(self.lastw.get(r))
        for w in writes:
            add(self.lastw.get(w))
            for ev in self.rd.get(w, {}).items():
                add(ev)
        waits = []
        for k, v in deps.items():
            if self.seen[eng].get(k, 0) >= v:
                continue
            self.seen[eng][k] = v
            waits.append((k, v))
        if dma is None:
            self.cnt[eng] += 1
            ev = (eng, self.cnt[eng])
        else:
            self.dcnt[dma] = self.dcnt.get(dma, 0) + 16
            ev = (dma, self.dcnt[dma])
        self.prog[eng].append((fn, waits, ev))
        for w in writes:
            self.lastw[w] = ev
            self.rd[w] = {}
        for r in reads:
            d = self.rd.setdefault(r, {})
            if d.get(ev[0], 0) < ev[1]:
                d[ev[0]] = ev[1]
        return ev

    def barrier(self):
        tot = dict(self.cnt)
        tot.update(self.dcnt)
        for e in self.ENG:
            waits = []
            for k, v in tot.items():
                if v <= 0 or k == e:
                    continue
                if self.seen[e].get(k, 0) >= v:
                    continue
                self.seen[e][k] = v
                waits.append((k, v))
            if waits:
                self.prog[e].append((None, waits, None))
        self.lastw = {}
        self.rd = {}

    def emit(self):
        nc = self.nc
        semnames = set(self.ENG) | set(self.dcnt.keys())
        with contextlib.ExitStack() as st:
            sems = {k: st.enter_context(nc.semaphore("s_" + k)) for k in sorted(semnames)}
            block = st.enter_context(nc.Block())
            fin = []
            for k in sorted(semnames):
                v = self.cnt[k] if k in self.cnt else self.dcnt[k]
                if v > 0 and k != "sp":
                    fin.append((k, v))

            def run(engname, eng, extra=None):
                for fn, waits, ev in self.prog[engname]:
                    for k, v in waits:
                        eng.wait_ge(sems[k], v)
                    if fn is None:
                        continue
                    ins = fn(eng)
                    ins.then_inc(sems[ev[0]], 16 if ev[0] not in self.cnt else 1)
                if extra:
                    for k, v in extra:
                        eng.wait_ge(sems[k], v)

            @block.tensor
            def _(e):
                run("pe", e)

            @block.scalar
            def _(e):
                run("act", e)

            @block.vector
            def _(e):
                run("dve", e)

            @block.gpsimd
            def _(e):
                run("pool", e)

            @block.sync
            def _(e):
                run("sp", e, fin)


class Arena:
    def __init__(self, tens, cap_bytes):
        self.t = tens
        self.cap = cap_bytes
        self.off = 0
        self.marks = []

    def alloc(self, free_shape, dt, parts=128):
        n = int(np.prod(free_shape))
        nb = n * (4 if dt == F32 else 2)
        nb = (nb + 63) // 64 * 64
        assert self.off + nb <= self.cap, ("SBUF arena overflow", self.off, nb, self.cap)
        o32 = self.off // 4
        ap = self.t[:, o32:o32 + nb // 4]
        if dt != F32:
            ap = ap.bitcast(dt)
        ap = ap[:, 0:n]
        if len(free_shape) == 2:
            ap = ap.rearrange("p (a b) -> p a b", a=free_shape[0])
        elif len(free_shape) == 3:
            ap = ap.rearrange("p (a b c) -> p a b c", a=free_shape[0], b=free_shape[1])
        elif len(free_shape) == 4:
            ap = ap.rearrange("p (a b c d) -> p a b c d", a=free_shape[0], b=free_shape[1], c=free_shape[2])
        if parts != 128:
            ap = ap[0:parts]
        self.off += nb
        return ap

    def mark(self):
        self.marks.append(self.off)

    def release(self):
        if os.environ.get("KVERB"):
            print("arena high-water at release:", self.off, "of", self.cap)
        self.off = self.marks.pop()


def seq_gcol(s):
    return 0 if s == 0 else TP + (s - 1) * LS


def make_blocks(nb):
    blocks = []
    for b in range(TP // nb):
        blocks.append((b * nb, [dict(seq=0, c0=0, L=nb, t0=b * nb, first=(b == 0), last=(b == TP // nb - 1))]))
    per = nb // LS
    for b in range(NS * LS // nb):
        segs = []
        for i in range(per):
            segs.append(dict(seq=1 + b * per + i, c0=i * LS, L=LS, t0=0, first=True, last=True))
        blocks.append((TP + b * nb, segs))
    return blocks


def build(nlayers=DEPTH):
    nc = bass.Bass("TRN2", target_bir_lowering=False)

    def din(name, shape, dt=F32):
        return nc.dram_tensor(name, list(shape), dt, kind="ExternalInput").ap()

    def dout(name, shape, dt=F32):
        return nc.dram_tensor(name, list(shape), dt, kind="ExternalOutput").ap()

    def dint(name, shape, dt=F32):
        return nc.dram_tensor(name, list(shape), dt, kind="Internal").ap()

    xT = din("xT", [D, NT])
    cT = din("cT", [D, NSEQ])
    sconvT = din("sconvT", [2, NS, D, CK - 1])
    smconvT = din("smconvT", [2, NS, MW, MK - 1])
    sCT = din("sCT", [2, NS, H, DH, DH])
    snT = din("snT", [2, 128, NS * H * 4])
    smT = din("smT", [2, H, NS])
    smallp_d = din("smallp", [128, NSMALL])
    constp_d = din("constp", [128, NCONST])
    ada_w = din("ada_w", [DEPTH, D, 3 * D])
    cv_w_in = din("cv_w_in", [2, D, 3 * D])
    cv_w_out = din("cv_w_out", [2, D, D])
    ml_w_in = din("ml_w_in", [2, D, 3 * MW])
    ml_w_q = din("ml_w_q", [2, H, DH, DH])
    ml_w_k = din("ml_w_k", [2, H, DH, DH])
    ml_w_v = din("ml_w_v", [2, H, DH, DH])
    wgate_d = din("wgate", [2, 128, 3 * 16 * 8])
    ml_w_out = din("ml_w_out", [2, MW, D])

    yT = dout("yT", [D, NT])
    oconvT = dout("oconvT", [2, NSEQ, D, CK - 1])
    omconvT = dout("omconvT", [2, NSEQ, MW, MK - 1])
    oCT = dout("oCT", [2, NSEQ, H, DH, DH])
    onT = dout("onT", [2, 128, NSEQ * H * 4])
    omT = dout("omT", [2, H, NSEQ])

    xres = dint("xres", [D, NT])
    XM = dint("XMs", [MW, NT], BF16)
    ZZ = dint("ZZs", [MW, NT], BF16)
    OO = dint("OOs", [NT, MW], BF16)
    YY = dint("YYs", [MW, NT], BF16)

    xT_v = xT.rearrange("(c p) t -> p c t", p=128)
    xres_v = xres.rearrange("(c p) t -> p c t", p=128)
    yT_v = yT.rearrange("(c p) t -> p c t", p=128)
    XM_v = XM.rearrange("(c p) t -> p c t", p=128)
    ZZ_v = ZZ.rearrange("(c p) t -> p c t", p=128)
    YY_v = YY.rearrange("(c p) t -> p c t", p=128)

    CAP = int(os.environ.get('KCAP', '207')) * 1024
    with contextlib.ExitStack() as st:
        arena_t = st.enter_context(nc.sbuf_tensor("arena", [128, CAP // 4], F32))
        pst = [st.enter_context(nc.psum_tensor("ps%d" % i, [128, 512], F32)) for i in range(8)]
        S = Sched(nc)
        A = Arena(arena_t, CAP)
        psi = [0]

        psw_ = [6]

        def PS():
            i = psi[0] % psw_[0]
            psi[0] += 1
            return pst[i], ("ps", i)

        def PSX(i):
            return pst[6 + i], ("ps", 6 + i)

        def act(out, in_, func, reads, writes, bias=None, scale=None, eng="act"):
            kw = {}
            if bias is not None:
                kw["bias"] = bias
            if scale is not None:
                kw["scale"] = scale
            S.op(eng, lambda e: e.activation(out=out, in_=in_, func=func, **kw), reads, writes)

        def tt(out, in0, in1, op, reads, writes, eng="dve"):
            S.op(eng, lambda e: e.tensor_tensor(out=out, in0=in0, in1=in1, op=op), reads, writes)

        def ts(out, in0, s1, s2, op0, op1, reads, writes, eng="dve"):
            if op1 is None:
                S.op(eng, lambda e: e.tensor_scalar(out=out, in0=in0, scalar1=s1, scalar2=None, op0=op0), reads, writes)
            else:
                S.op(eng, lambda e: e.tensor_scalar(out=out, in0=in0, scalar1=s1, scalar2=s2, op0=op0, op1=op1), reads, writes)

        def stt(out, in0, scalar, in1, op0, op1, reads, writes):
            S.op("dve", lambda e: e.scalar_tensor_tensor(out=out, in0=in0, scalar=scalar, in1=in1, op0=op0, op1=op1), reads, writes)

        def cp(out, in_, reads, writes, eng="dve"):
            S.op(eng, lambda e: e.tensor_copy(out=out, in_=in_), reads, writes)

        def memset(ap, val, writes, eng="dve"):
            S.op(eng, lambda e: e.memset(ap, val), (), writes)

        def recip(out, in_, reads, writes):
            S.op("dve", lambda e: e.reciprocal(out=out, in_=in_), reads, writes)

        def mm(ps_ap, lhsT, rhs, start, stop, reads, pskey):
            S.op("pe", lambda e: e.matmul(ps_ap, lhsT=lhsT, rhs=rhs, start=start, stop=stop), reads, [pskey])

        def tr(ps_ap, in_, ident, reads, pskey):
            S.op("pe", lambda e: e.transpose(out=ps_ap, in_=in_, identity=ident), reads, [pskey])

        def dma(out, in_, reads, writes, sem, eng="sp"):
            S.op(eng, lambda e: e.dma_start(out=out, in_=in_), reads, writes, dma=sem)

        smallp = A.alloc([NSMALL], F32)
        constp = A.alloc([NCONST], F32)
        constb = A.alloc([NCONSTB], BF16)
        modt = A.alloc([DEPTH, 24, NSEQ], F32)
        Amod = A.alloc([DEPTH, 8, NSEQ], F32)
        dma(smallp, smallp_d, (), ["smallp"], "ld_smallp")
        dma(constp, constp_d, (), ["constp"], "ld_constp")
        cp(constb, constp[:, 0:NCONSTB], ["constp"], ["constb"])
        ident_f = constp[:, C_ID:C_ID + 128]
        ident_b = constb[:, C_ID:C_ID + 128]
        ones_b = constb[:, C_ONES:C_ONES + 128]
        ones_f = constp[:, C_ONES:C_ONES + 128]
        maskT = constp[:, C_MASK:C_MASK + 128]
        CONST = ["constp", "constb", "smallp"]

        def sm(name, off, n=1):
            o = _SM[name] + off
            return smallp[:, o:o + n]

        A.mark()
        cTf = A.alloc([8, NSEQ], BF16)
        dma(cTf, cT.rearrange("(c p) s -> p c s", p=128), (), ["cTf"], "ld_cT", eng="pool")
        adw = [A.alloc([8, 3 * D], BF16) for _ in range(2)]
        for l in range(nlayers):
            w = adw[l % 2]
            wk = ("adw", l % 2)
            dma(w, ada_w[l].rearrange("(c p) e -> p c e", p=128), (), [wk], "ld_adw%d" % (l % 2), eng="pool")
            ps, pk = PS()
            for e in range(24):
                for c in range(8):
                    mm(ps[:, e * NSEQ:(e + 1) * NSEQ], w[:, c, e * 128:(e + 1) * 128], cTf[:, c, :], c == 0, c == 7,
                       [wk, "cTf"], pk)
            psv = ps[:, 0:24 * NSEQ].rearrange("p (e s) -> p e s", s=NSEQ)
            for s in range(NSEQ):
                tt(modt[:, l, :, s], psv[:, :, s], sm("ada_b", l * 24, 24), ALU.add, [pk, "smallp"], ["modt"])
            for s in range(NSEQ):
                stt(Amod[:, l, :, s], modt[:, l, 8:16, s], 1.0, sm("norm_g", l * 8, 8), ALU.add, ALU.mult,
                    ["modt", "smallp"], ["Amod"])
        S.barrier()
        A.release()

        def load_x(xt, xkey, src_v, g0, nb):
            dma(xt[:, :, 0:nb], src_v[:, :, g0:g0 + nb], (), [xkey], "ld_" + xkey[0] + str(xkey[1]))

        def norm_h(l, xt, xkey, ht, xsq, rstd, tmp, segs, nb, tmpkey="tmp", xsq_w=(), h_w=lambda c: []):
            act(xsq[:, :, 0:nb], xt[:, :, 0:nb], AF.Square, [xkey], ["xsq"] + list(xsq_w))
            ps, pk = PS()
            for c in range(8):
                mm(ps[:, 0:nb], ones_b, xsq[:, c, 0:nb], c == 0, c == 7, ["xsq", "constb"], pk)
            act(rstd[:, 0:nb], ps[:, 0:nb], AF.Sqrt, [pk], ["rstd"], bias=EPS, scale=1.0 / D)
            recip(rstd[:, 0:nb], rstd[:, 0:nb], ["rstd"], ["rstd"])
            for sg in segs:
                c0, L, s = sg["c0"], sg["L"], sg["seq"]
                for c in range(8):
                    tk = (tmpkey, c % 2)
                    tt(tmp[:, c % 2, 0:L], xt[:, c, c0:c0 + L], rstd[:, c0:c0 + L], ALU.mult, [xkey, "rstd"], [tk])
                    act(ht[:, c, c0:c0 + L], tmp[:, c % 2, 0:L], AF.Identity, [tk, "Amod", "modt"], [("h", c)] + h_w(c),
                        bias=modt[:, l, c, s:s + 1], scale=Amod[:, l, c, s:s + 1])

        HK = [("h", c) for c in range(8)]

        def conv_layer(l, src_v):
            j = l // 2
            NB = 256
            blocks = make_blocks(NB)
            A.mark()
            w_in = A.alloc([8, 3 * D], BF16)
            w_out = A.alloc([8, D], BF16)
            diag = A.alloc([8, CK, 128], BF16)
            dma(w_in, cv_w_in[j].rearrange("(c p) e -> p c e", p=128), (), ["w_in"], "ld_w_in", eng="pool")
            dma(w_out, cv_w_out[j].rearrange("(c p) e -> p c e", p=128), (), ["w_out"], "ld_w_out", eng="pool")
            for c in range(8):
                for k in range(CK):
                    ts(diag[:, c, k, :], ident_f, sm("cv_w_dw", (j * 8 + c) * CK + k), None, ALU.mult, None,
                       CONST, ["diag"], eng="dve")
            xts = [A.alloc([8, NB], F32) for _ in range(2)]
            ht = A.alloc([8, NB], BF16)
            xsq = A.alloc([8, NB], BF16)
            rstd = A.alloc([NB], F32)
            HL = CK - 1
            gb = [A.alloc([8, 4 * (HL + LS)], BF16) for _ in range(2)]
            gtail = A.alloc([8, 4, HL], F32)
            hst = A.alloc([8, 4, HL], F32)
            yf = A.alloc([8, NB], F32)
            ybf = A.alloc([2, NB], BF16)
            ysq = A.alloc([2, NB], BF16)
            zz = xsq
            yo = ht
            mean = A.alloc([NB], F32)
            rs2 = A.alloc([NB], F32)
            m2 = A.alloc([NB], F32)
            t1 = A.alloc([2, NB], F32)
            t2 = A.alloc([2, NB], F32)
            s3 = A.alloc([2, NB], F32)
            tmp = t1
            sig = t2
            load_x(xts[0], ("x", 0), src_v, blocks[0][0], NB)
            for bi, (g0, segs) in enumerate(blocks):
                xt, xk = xts[bi % 2], ("x", bi % 2)
                g, gk = gb[bi % 2], ("gb", bi % 2)
                gp, gpk = gb[(bi + 1) % 2], ("gb", (bi + 1) % 2)
                if bi + 1 < len(blocks):
                    load_x(xts[(bi + 1) % 2], ("x", (bi + 1) % 2), src_v if True else None, blocks[bi + 1][0], NB)
                norm_h(l, xt, xk, ht, xsq, rstd, tmp, segs, NB, tmpkey="t1", xsq_w=[("zz", c) for c in range(8)],
                       h_w=lambda c: [("yo", c)])
                for si, sg in enumerate(segs):
                    if sg["first"]:
                        if sg["seq"] == 0:
                            memset(g[:, :, 0:HL], 0.0, [(gk, "h")], eng="pool")
                        else:
                            dma(hst[:, :, si, :], sconvT[j, sg["seq"] - 1].rearrange("(c p) k -> p c k", p=128), (),
                                [("hst", si)], "ld_hst%d" % si)
                            cp(g[:, :, si * (HL + LS):si * (HL + LS) + HL], hst[:, :, si, :], [("hst", si)], [(gk, "h")], eng="pool")
                    else:
                        cp(g[:, :, 0:HL], gp[:, :, NB:NB + HL], [gpk, (gpk, "h")], [(gk, "h")], eng="pool")
                for cc in range(8):
                    ps_g, pkg = PS()
                    for c in range(8):
                        mm(ps_g[:, 0:NB], w_in[:, c, (8 + cc) * 128:(9 + cc) * 128], ht[:, c, :], c == 0, c == 7,
                           HK + ["w_in"], pkg)
                    sk = ("t2", cc % 2)
                    act(sig[:, cc % 2, :], ps_g[:, 0:NB], AF.Sigmoid, [pkg], [sk])
                    ps_a, pka = PS()
                    for c in range(8):
                        mm(ps_a[:, 0:NB], w_in[:, c, cc * 128:(cc + 1) * 128], ht[:, c, :], c == 0, c == 7,
                           HK + ["w_in"], pka)
                    for si, sg in enumerate(segs):
                        c0, L = sg["c0"], sg["L"]
                        go = si * (HL + L)
                        tt(g[:, cc, go + HL:go + HL + L], ps_a[:, c0:c0 + L], sig[:, cc % 2, c0:c0 + L], ALU.mult,
                           [pka, sk], [gk])
                        if sg["last"]:
                            tt(gtail[:, cc, si, :], ps_a[:, c0 + L - HL:c0 + L], sig[:, cc % 2, c0 + L - HL:c0 + L],
                               ALU.mult, [pka, sk], [("gtail", si)])
                for si, sg in enumerate(segs):
                    if sg["last"]:
                        dma(oconvT[j, sg["seq"]].rearrange("(c p) k -> p c k", p=128), gtail[:, :, si, :],
                            [("gtail", si)], (), "st_gtail%d" % si)
                ps_m, pkm = PSX(0)
                ps_q, pkq = PSX(1)
                for cc in range(8):
                    ps_c, pkc = PS()
                    for si, sg in enumerate(segs):
                        c0, L = sg["c0"], sg["L"]
                        for k in range(CK):
                            mm(ps_c[:, c0:c0 + L], diag[:, cc, k, :],
                               g[:, cc, si * (HL + L) + k:si * (HL + L) + k + L], k == 0, k == CK - 1,
                               [gk, (gk, "h"), "diag"], pkc)
                    k2 = cc % 2
                    act(yf[:, cc, :], ps_c[:, 0:NB], AF.Identity, [pkc, "smallp"], [("yf", cc)],
                        bias=sm("cv_b_dw", j * 8 + cc))
                    cp(ybf[:, k2, :], yf[:, cc, :], [("yf", cc)], [("ybf", k2)], eng="pool")
                    act(ysq[:, k2, :], yf[:, cc, :], AF.Square, [("yf", cc)], [("ysq", k2)])
                    mm(ps_m[:, 0:NB], ones_b, ybf[:, k2, :], cc == 0, cc == 7, [("ybf", k2), "constb"], pkm)
                    mm(ps_q[:, 0:NB], ones_b, ysq[:, k2, :], cc == 0, cc == 7, [("ysq", k2), "constb"], pkq)
                for cc in range(8):
                    ps_z, pkz = PS()
                    for c in range(8):
                        mm(ps_z[:, 0:NB], w_in[:, c, (16 + cc) * 128:(17 + cc) * 128], ht[:, c, :], c == 0, c == 7,
                           HK + ["w_in"], pkz)
                    act(zz[:, cc, :], ps_z[:, 0:NB], AF.Silu, [pkz], [("zz", cc), "xsq"])
                ts(mean, ps_m[:, 0:NB], 1.0 / D, None, ALU.mult, None, [pkm], ["mean"])
                tt(m2, mean, mean, ALU.mult, ["mean"], ["m2"])
                stt(rs2, ps_q[:, 0:NB], 1.0 / D, m2, ALU.mult, ALU.subtract, [pkq, "m2"], ["rs2"])
                ts(rs2, rs2, 0.0, None, ALU.max, None, ["rs2"], ["rs2"])
                act(rs2, rs2, AF.Sqrt, ["rs2"], ["rs2"], bias=EPS, scale=1.0)
                recip(rs2, rs2, ["rs2"], ["rs2"])
                for cc in range(8):
                    k2 = cc % 2
                    tt(t1[:, k2, :], yf[:, cc, :], mean, ALU.subtract, [("yf", cc), "mean"], [("t1", k2)], eng="pool")
                    tt(t2[:, k2, :], t1[:, k2, :], rs2, ALU.mult, [("t1", k2), "rs2"], [("t2", k2)])
                    act(s3[:, k2, :], t2[:, k2, :], AF.Silu, [("t2", k2), "smallp"], [("s3", k2)],
                        bias=sm("cv_ln_b", j * 8 + cc), scale=sm("cv_ln_g", j * 8 + cc))
                    tt(yo[:, cc, :], s3[:, k2, :], zz[:, cc, :], ALU.mult, [("s3", k2), ("zz", cc)], [("yo", cc), ("h", cc)])
                YOK = [("yo", c) for c in range(8)]
                for dc in range(8):
                    ps_o, pko = PS()
                    for c in range(8):
                        mm(ps_o[:, 0:NB], w_out[:, c, dc * 128:(dc + 1) * 128], yo[:, c, :], c == 0, c == 7,
                           YOK + ["w_out"], pko)
                    for sg in segs:
                        c0, L, s = sg["c0"], sg["L"], sg["seq"]
                        stt(xt[:, dc, c0:c0 + L], ps_o[:, c0:c0 + L], modt[:, l, 16 + dc, s:s + 1],
                            xt[:, dc, c0:c0 + L], ALU.mult, ALU.add, [pko, "modt", xk], [xk])
                dma(xres_v[:, :, g0:g0 + NB], xt[:, :, 0:NB], [xk], (), "st_x%d" % (bi % 2))
            S.barrier()
            A.release()

        def mlstm_A(l, src_v):
            j = l // 2
            NB = 256
            blocks = make_blocks(NB)
            A.mark()
            w_in = A.alloc([8, 3 * MW], BF16)
            dma(w_in, ml_w_in[j].rearrange("(c p) e -> p c e", p=128), (), ["w_in"], "ld_w_in", eng="pool")
            xts = [A.alloc([8, NB], F32) for _ in range(2)]
            ht = A.alloc([8, NB], BF16)
            xsq = A.alloc([8, NB], BF16)
            rstd = A.alloc([NB], F32)
            tmp = A.alloc([2, NB], F32)
            xm = [A.alloc([16, NB], BF16) for _ in range(2)]
            zt = [A.alloc([16, NB], BF16) for _ in range(2)]
            ot = [A.alloc([2, MW], BF16) for _ in range(2)]
            xtail = A.alloc([16, 4, MK - 1], F32)
            load_x(xts[0], ("x", 0), src_v, blocks[0][0], NB)
            for bi, (g0, segs) in enumerate(blocks):
                b2 = bi % 2
                xt, xk = xts[b2], ("x", b2)
                if bi + 1 < len(blocks):
                    load_x(xts[(bi + 1) % 2], ("x", (bi + 1) % 2), src_v, blocks[bi + 1][0], NB)
                norm_h(l, xt, xk, ht, xsq, rstd, tmp, segs, NB)
                for cc in range(16):
                    ps, pk = PS()
                    for c in range(8):
                        mm(ps[:, 0:NB], w_in[:, c, cc * 128:(cc + 1) * 128], ht[:, c, :], c == 0, c == 7,
                           HK + ["w_in"], pk)
                    if cc % 2 == 0:
                        act(xm[b2][:, cc, :], ps[:, 0:NB], AF.Copy, [pk], [("xm", b2, cc)])
                    else:
                        cp(xm[b2][:, cc, :], ps[:, 0:NB], [pk], [("xm", b2, cc)])
                    for si, sg in enumerate(segs):
                        if sg["last"]:
                            c0, L = sg["c0"], sg["L"]
                            cp(xtail[:, cc, si, :], ps[:, c0 + L - (MK - 1):c0 + L], [pk], [("xtail", si)])
                for si, sg in enumerate(segs):
                    if sg["last"]:
                        dma(omconvT[j, sg["seq"]].rearrange("(c p) k -> p c k", p=128), xtail[:, :, si, :],
                            [("xtail", si)], (), "st_xtail%d" % si)
                dma(XM_v[:, :, g0:g0 + NB], xm[b2], [("xm", b2, cc) for cc in range(16)], (), "st_xm%d" % b2)
                for cc in range(16):
                    ps, pk = PS()
                    for c in range(8):
                        mm(ps[:, 0:NB], w_in[:, c, MW + cc * 128:MW + (cc + 1) * 128], ht[:, c, :], c == 0, c == 7,
                           HK + ["w_in"], pk)
                    act(zt[b2][:, cc, :], ps[:, 0:NB], AF.Silu, [pk], [("zt", b2, cc)])
                dma(ZZ_v[:, :, g0:g0 + NB], zt[b2], [("zt", b2, cc) for cc in range(16)], (), "st_zt%d" % b2)
                for tti in range(NB // 128):
                    for oc in range(4):
                        ps, pk = PS()
                        for c in range(8):
                            mm(ps[:, :], ht[:, c, tti * 128:(tti + 1) * 128],
                               w_in[:, c, 2 * MW + oc * 512:2 * MW + (oc + 1) * 512], c == 0, c == 7,
                               HK + ["w_in"], pk)
                        act(ot[b2][:, tti, oc * 512:(oc + 1) * 512], ps[:, :], AF.Sigmoid, [pk], [("ot", b2)])
                dma(OO[g0:g0 + NB, :].rearrange("(a p) e -> p a e", p=128), ot[b2], [("ot", b2)], (), "st_ot%d" % b2)
            S.barrier()
            A.release()

        def mlstm_B(l):
            j = l // 2
            NB = 128
            blocks = make_blocks(NB)
            A.mark()
            psw_[0] = 8
            wq = A.alloc([H, 4, DH], BF16)
            wk_ = A.alloc([H, 4, DH], BF16)
            wv = A.alloc([H, 4, DH], BF16)
            wg = A.alloc([3, 16, 8], BF16)
            diag = A.alloc([16, MK, 128], BF16)
            for wt_, src, nm in ((wq, ml_w_q, "wq"), (wk_, ml_w_k, "wk"), (wv, ml_w_v, "wv")):
                for h in range(H):
                    dma(wt_[:, h, :, :], src[j, h].rearrange("(c p) e -> p c e", p=128), (), [nm], "ld_" + nm,
                        eng="pool")
            dma(wg, wgate_d[j].rearrange("p (a b c) -> p a b c", a=3, b=16), (), ["wg"], "ld_wg", eng="pool")
            for c in range(16):
                for k in range(MK):
                    ts(diag[:, c, k, :], ident_f, sm("ml_w_conv", (j * 16 + c) * MK + k), None, ALU.mult, None,
                       CONST, ["diag"], eng="dve")
            CT32 = A.alloc([H, 4, DH], F32)
            CTb = A.alloc([H, 4, DH], BF16)
            n32 = A.alloc([H, 4], F32)
            nb_ = A.alloc([H, 4], BF16)
            nin = A.alloc([NS * H * 4], F32)
            nout = A.alloc([NSEQ * H * 4], F32)
            min_ = A.alloc([NS], F32, parts=4)
            mout = A.alloc([NSEQ], F32, parts=4)
            mcur = [A.alloc([1], F32, parts=4) for _ in range(2)]
            zeros4 = A.alloc([2 * LS], F32, parts=4)
            bgt = A.alloc([1], F32, parts=8)
            HM = MK - 1
            HP = 32
            xmb = [A.alloc([16, 2 * HP + NB], BF16) for _ in range(2)]
            xmh = A.alloc([16, 2, HM], F32)
            zt = [A.alloc([16, NB], BF16) for _ in range(1)]
            ot = [A.alloc([H * DH], BF16) for _ in range(1)]
            xc = A.alloc([16, NB], BF16)
            sxc = A.alloc([16, NB], BF16)
            qT = A.alloc([16, NB], BF16)
            kT = A.alloc([16, NB], BF16)
            vT = A.alloc([16, NB], BF16)
            yv = [A.alloc([16, NB], BF16) for _ in range(1)]
            g8 = A.alloc([NB], F32, parts=8)
            fgb = A.alloc([NB], F32, parts=4)
            a1 = A.alloc([NB], F32, parts=4)
            lf = A.alloc([NB], F32, parts=4)
            brow = A.alloc([2 * LS], F32, parts=4)
            grow = A.alloc([2 * LS], F32, parts=4)
            Mrow = A.alloc([2 * LS], F32, parts=4)
            negM = A.alloc([2 * LS], F32, parts=4)
            rowA = A.alloc([2 * LS], F32, parts=4)
            rowB = A.alloc([2 * LS], F32, parts=4)
            rowC = A.alloc([2 * LS], F32, parts=4)
            dgA = A.alloc([2, 4], F32, parts=4)
            cols = A.alloc([2, 12], F32)
            wks = A.alloc([2, 4], F32)
            decb = A.alloc([2, 4], F32)
            WT = [A.alloc([2 * LS], F32) for _ in range(H)]
            SpT = [A.alloc([2 * LS], BF16) for _ in range(H)]
            kp = [A.alloc([DH], BF16) for _ in range(H)]
            vtok = [A.alloc([DH], BF16) for _ in range(H)]
            dsb = A.alloc([2 * H], F32)
            dd = A.alloc([4, H], F32)
            tmpA = [A.alloc([DH], F32) for _ in range(H)]
            bst = A.alloc([H, 8], F32)
            hn = [A.alloc([DH], BF16) for _ in range(H)]
            t1 = A.alloc([H, 4 * 2 * LS], F32)
            selrow = constp[0:4, C_SELROW:C_SELROW + 4 * 128].rearrange("p (h t) -> p h t", h=4)
            selF = constp[0:8, C_SELF:C_SELF + 4]
            id4 = constp[0:4, C_ID:C_ID + 4]
            memset(zeros4, 0.0, ["zeros4"])
            cp(bgt, smallp[0:8, _SM["b_gate"] + j:_SM["b_gate"] + j + 1], ["smallp"], ["bgt"])
            dma(nin, snT[j], (), ["nin"], "ld_nin")
            dma(min_, smT[j], (), ["min"], "ld_min")
            hc = [0]
            mi = [0]
            CK32 = [("C32", h) for h in range(H)]
            CKB = [("Cb", h) for h in range(H)]

            def load_xm(bj):
                g0_, segs_ = blocks[bj]
                xb, xbk = xmb[bj % 2], ("xmb", bj % 2)
                for si, sg in enumerate(segs_):
                    c0, L, s = sg["c0"], sg["L"], sg["seq"]
                    gc = g0_ + c0
                    xo = si * (HP + L) + HP - HM
                    if sg["first"]:
                        if s == 0:
                            memset(xb[:, :, xo:xo + HM], 0.0, [(xbk, "h")], eng="pool")
                        else:
                            dma(xmh[:, :, si, :], smconvT[j, s - 1].rearrange("(c p) k -> p c k", p=128), (),
                                [("xmh", si)], "ld_xmh%d" % si)
                            cp(xb[:, :, xo:xo + HM], xmh[:, :, si, :], [("xmh", si)], [(xbk, "h")], eng="pool")
                        dma(xb[:, :, xo + HM:xo + HM + L], XM_v[:, :, gc:gc + L], (), [xbk], "ld_xmb%d" % (bj % 2))
                    else:
                        dma(xb[:, :, xo:xo + HM + L], XM_v[:, :, gc - HM:gc + L], (), [xbk, (xbk, "h")],
                            "ld_xmb%d" % (bj % 2))

            BP = {}

            def block_part(bi):
                g0, segs = blocks[bi]
                b2 = bi % 2
                xb, xbk = xmb[b2], ("xmb", b2)
                if bi == 0:
                    load_xm(0)
                    dma(zt[0], ZZ_v[:, :, g0:g0 + NB], (), [("zt", 0)], "ld_zt0")
                if bi + 1 < len(blocks):
                    load_xm(bi + 1)
                XBK = [xbk, (xbk, "h")]
                for cg in range(4):
                    ps, pk = PS()
                    for ci in range(4):
                        cc = cg * 4 + ci
                        for si, sg in enumerate(segs):
                            c0, L = sg["c0"], sg["L"]
                            for k in range(MK):
                                mm(ps[:, ci * NB + c0:ci * NB + c0 + L], diag[:, cc, k, :],
                                   xb[:, cc, si * (HP + L) + HP - HM + k:si * (HP + L) + HP - HM + k + L],
                                   k == 0, k == MK - 1, XBK + ["diag"], pk)
                    for ci in range(4):
                        cc = cg * 4 + ci
                        act(xc[:, cc, :], ps[:, ci * NB:(ci + 1) * NB], AF.Silu, [pk, "smallp"], [("xc", cc)],
                            bias=sm("ml_b_conv", j * 16 + cc))
                for (dst, dk, wt_, wn, srcsel) in ((qT, "qT", wq, "wq", 0), (kT, "kT", wk_, "wk", 0), (vT, "vT", wv, "wv", 1)):
                    for h in range(H):
                        ps, pk = PS()
                        for ec in range(4):
                            if srcsel == 0:
                                for kc in range(4):
                                    cc = h * 4 + kc
                                    mm(ps[:, ec * NB:(ec + 1) * NB], wt_[:, h, kc, ec * 128:(ec + 1) * 128],
                                       xc[:, cc, :], kc == 0, kc == 3, [("xc", cc), wn], pk)
                            else:
                                for si, sg in enumerate(segs):
                                    c0, L = sg["c0"], sg["L"]
                                    for kc in range(4):
                                        cc = h * 4 + kc
                                        mm(ps[:, ec * NB + c0:ec * NB + c0 + L],
                                           wt_[:, h, kc, ec * 128:(ec + 1) * 128],
                                           xb[:, cc, si * (HP + L) + HP:si * (HP + L) + HP + L],
                                           kc == 0, kc == 3, XBK + [wn], pk)
                        dv = dst[:, h * 4:(h + 1) * 4, :]
                        pv = ps[:, :].rearrange("p (a t) -> p a t", a=4)
                        if h % 2 == 0:
                            act(dv, pv, AF.Copy, [pk], [(dk, h)])
                        else:
                            cp(dv, pv, [pk], [(dk, h)])
                ps, pk = PS()
                n = 0
                for si_, (srcT, dk) in enumerate(((qT, "qT"), (kT, "kT"), (vT, "vT"))):
                    for cc in range(16):
                        mm(ps[0:8, 0:NB], wg[:, si_, cc, :], srcT[:, cc, :], n == 0, n == 47, [(dk, cc // 4), "wg"], pk)
                        n += 1
                act(g8, ps[0:8, 0:NB], AF.Identity, [pk, "bgt"], ["g8"], bias=bgt[:, 0:1])
                ps2, pk2 = PS()
                mm(ps2[0:4, 0:NB], selF, g8, True, True, ["g8", "constp"], pk2)
                cp(fgb, ps2[0:4, 0:NB], [pk2], ["fgb"])
                act(a1, fgb, AF.Abs, ["fgb"], ["a1"])
                act(a1, a1, AF.Exp, ["a1"], ["a1"], scale=-1.0)
                act(a1, a1, AF.Ln, ["a1"], ["a1"], bias=1.0)
                ts(lf, fgb, 0.0, None, ALU.min, None, ["fgb"], ["lf"])
                tt(lf, lf, a1, ALU.subtract, ["lf", "a1"], ["lf"])
                chunks = []
                for si, sg in enumerate(segs):
                    Lc = 2 * LS if sg["L"] % (2 * LS) == 0 else LS
                    for ch in range(sg["L"] // Lc):
                        chunks.append((sg, ch, Lc))
                for ci, (sg, ch, Lc) in enumerate(chunks):
                    s = sg["seq"]
                    cs = sg["c0"] + ch * Lc
                    csl = slice(cs, cs + Lc)
                    rs = slice(ci * Lc, (ci + 1) * Lc)
                    lastc = slice((ci + 1) * Lc - 1, (ci + 1) * Lc)
                    first = sg["first"] and ch == 0
                    last = sg["last"] and ch == sg["L"] // Lc - 1
                    if first:
                        if s == 0:
                            memset(mcur[mi[0] % 2], 0.0, [("m", mi[0] % 2)])
                        else:
                            cp(mcur[mi[0] % 2], min_[:, s - 1:s], ["min"], [("m", mi[0] % 2)])
                    m_t, mk_ = mcur[mi[0] % 2], ("m", mi[0] % 2)
                    m_n, mkn = mcur[(mi[0] + 1) % 2], ("m", (mi[0] + 1) % 2)
                    mi[0] += 1
                    kb, kg, kM, kn, kA, kB, kC = [(nm, ci) for nm in ("brow", "grow", "Mrow", "negM", "rowA", "rowB", "rowC")]
                    S.op("dve", lambda e, o=brow[:, rs], d0=lf[:, csl], z0=zeros4[:, 0:Lc]: e.tensor_tensor_scan(
                        out=o, data0=d0, data1=z0, initial=0.0, op0=ALU.add, op1=ALU.add),
                        ["lf", "zeros4"], [kb])
                    tt(grow[:, rs], g8[0:4, csl], brow[:, rs], ALU.subtract, ["g8", kb], [kg])
                    S.op("dve", lambda e, o=Mrow[:, rs], d0=grow[:, rs], ini=m_t[:, 0:1]: e.tensor_tensor_scan(
                        out=o, data0=d0, data1=d0, initial=ini, op0=ALU.max, op1=ALU.max),
                        [kg, mk_], [kM])
                    ts(negM[:, rs], Mrow[:, rs], -1.0, None, ALU.mult, None, [kM], [kn])
                    ts(rowA[:, rs], Mrow[:, rs], -1.0, m_t[:, 0:1], ALU.mult, ALU.add, [kM, mk_], [kA])
                    stt(rowB[:, rs], brow[:, rs], -1.0, Mrow[:, rs], ALU.mult, ALU.subtract, [kb, kM], [kB])
                    ts(rowC[:, rs], grow[:, rs], Mrow[:, lastc], None, ALU.subtract, None, [kg, kM], [kC])
                    tt(m_n, brow[:, lastc], Mrow[:, lastc], ALU.add, [kb, kM], [mkn])
                    ts(dgA[:, ci, :], id4, rowA[:, lastc], None, ALU.mult, None, [kA, "constp"], [("dgA", ci)])
                    psc, pkc = PS()
                    for qi, (rw, rk) in enumerate(((rowA, kA), (rowB, kB), (rowC, kC))):
                        mm(psc[0:Lc, qi * 4:(qi + 1) * 4], rw[:, rs], id4, True, True, [rk, "constp"], pkc)
                    act(cols[0:Lc, ci, :], psc[0:Lc, 0:12], AF.Exp, [pkc], [("cols", ci)])
                    ts(wks[0:Lc, ci, :], cols[0:Lc, ci, 8:12], KSCALE, None, ALU.mult, None, [("cols", ci)], [("wks", ci)])
                    psd, pkd = PS()
                    mm(psd[:, 0:4], ones_f[0:4, :], dgA[:, ci, :], True, True, [("dgA", ci), "constp"], pkd)
                    act(decb[:, ci, :], psd[:, 0:4], AF.Exp, [pkd], [("decb", ci)])
                    if last:
                        cp(mout[:, s:s + 1], m_n, [mkn], ["mout"])
                BP[bi] = chunks

            def sxc_part(bi):
                for cc in range(16):
                    ts(sxc[:, cc, :], xc[:, cc, :], sm("ml_skip", j * 16 + cc), None, ALU.mult, None,
                       [("xc", cc), "smallp"], [("sxc", cc)], eng="dve")

            def rec_a(bi, cix):
                g0, segs = blocks[bi]
                b2 = bi % 2
                for ci, (sg, ch, Lc) in [(cix, BP[bi][cix])]:
                    s = sg["seq"]
                    cs = sg["c0"] + ch * Lc
                    csl = slice(cs, cs + Lc)
                    rs = slice(ci * Lc, (ci + 1) * Lc)
                    gcol = g0 + cs
                    first = sg["first"] and ch == 0
                    last = sg["last"] and ch == sg["L"] // Lc - 1
                    o_t, ok = ot[0], ("ot", 0)
                    dma(o_t[0:Lc], OO[gcol:gcol + Lc, :], (), [ok], "ld_ot0")
                    if first:
                        if s == 0:
                            for h in range(H):
                                memset(CT32[:, h], 0.0, [("C32", h)], eng="pool")
                                memset(CTb[:, h], 0.0, [("Cb", h)], eng="pool")
                            memset(n32, 0.0, ["n32"])
                            memset(nb_, 0.0, ["nb"])
                        else:
                            for h in range(H):
                                dma(CT32[:, h], sCT[j, s - 1, h].rearrange("(c p) v -> p c v", p=128), (),
                                    [("C32", h)], "ld_C%d" % h)
                                cp(CTb[:, h], CT32[:, h], [("C32", h)], [("Cb", h)], eng="pool")
                            nv = nin[:, (s - 1) * 16:s * 16].rearrange("p (h c) -> p h c", h=H)
                            cp(n32, nv, ["nin"], ["n32"])
                            cp(nb_, nv, ["nin"], ["nb"])
                    HQ = [[h * 4 + kc for kc in range(4)] for h in range(H)]
                    kg, kn = ("grow", ci), ("negM", ci)
                    for h in range(H):
                        psw, pkw = PS()
                        mm(psw[0:Lc, 0:Lc], grow[:, rs], selrow[:, h, 0:Lc], True, False, [kg, "constp"], pkw)
                        mm(psw[0:Lc, 0:Lc], selrow[:, h, 0:Lc], negM[:, rs], False, False, [kn, "constp"], pkw)
                        mm(psw[0:Lc, 0:Lc], ident_f[0:Lc, 0:Lc], maskT[0:Lc, 0:Lc], False, True, ["constp"], pkw)
                        act(WT[h][0:Lc, 0:Lc], psw[0:Lc, 0:Lc], AF.Exp, [pkw], [("WT", h)])
                    for h in range(H):
                        pss, pks = PS()
                        for kc in range(4):
                            mm(pss[0:Lc, 0:Lc], kT[:, HQ[h][kc], csl], qT[:, HQ[h][kc], csl], kc == 0, kc == 3,
                               [("kT", h), ("qT", h)], pks)
                        stt(SpT[h][0:Lc, 0:Lc], pss[0:Lc, 0:Lc], KSCALE, WT[h][0:Lc, 0:Lc], ALU.mult, ALU.mult,
                            [pks, ("WT", h)], [("SpT", h)])
                    for h in range(H):
                        pst_k, pkk = PS()
                        pkb = pst_k.bitcast(BF16)
                        for kc in range(4):
                            tr(pkb[0:Lc, kc * 128:(kc + 1) * 128], kT[:, HQ[h][kc], csl], ident_b, [("kT", h), "constb"], pkk)
                        act(kp[h][0:Lc], pkb[0:Lc, 0:DH], AF.Identity, [pkk, ("wks", ci)], [("kp", h)],
                            scale=wks[0:Lc, ci, h:h + 1])
                    for h in range(H):
                        pst_v, pkv = PS()
                        pvb = pst_v.bitcast(BF16)
                        for kc in range(4):
                            tr(pvb[0:Lc, kc * 128:(kc + 1) * 128], vT[:, HQ[h][kc], csl], ident_b, [("vT", h), "constb"], pkv)
                        cp(vtok[h][0:Lc], pvb[0:Lc, 0:DH], [pkv], [("vtok", h)])
                    psn, pkn = PS()
                    for h in range(H):
                        for kc in range(4):
                            mm(psn[0:Lc, 2 * h:2 * h + 1], qT[:, HQ[h][kc], csl], nb_[:, h, kc:kc + 1], kc == 0, kc == 3,
                               [("qT", h), "nb"], pkn)
                        mm(psn[0:Lc, 2 * h + 1:2 * h + 2], SpT[h][0:Lc, 0:Lc], ones_b[0:Lc, 0:1], True, True,
                           [("SpT", h), "constb"], pkn)
                    cp(dsb[0:Lc], psn[0:Lc, 0:2 * H], [pkn], ["dsb"])
                    dsv = dsb[0:Lc].rearrange("p (h t) -> p h t", t=2)
                    CK_ = ("cols", ci)
                    tt(dd[0:Lc, 0, :], dsv[:, :, 0], cols[0:Lc, ci, 0:4], ALU.mult, ["dsb", CK_], ["den"])
                    tt(dd[0:Lc, 0, :], dd[0:Lc, 0, :], dsv[:, :, 1], ALU.add, ["dsb", "den"], ["den"])
                    act(dd[0:Lc, 1, :], dd[0:Lc, 0, :], AF.Abs, ["den"], ["den"])
                    tt(dd[0:Lc, 1, :], dd[0:Lc, 1, :], cols[0:Lc, ci, 4:8], ALU.max, ["den", CK_], ["den"])
                    recip(dd[0:Lc, 2, :], dd[0:Lc, 1, :], ["den"], ["den"])
                    tt(dd[0:Lc, 3, :], dd[0:Lc, 2, :], cols[0:Lc, ci, 0:4], ALU.mult, ["den", CK_], ["den"])
                    for h in range(H):
                        psa, pka = PS()
                        for kc in range(4):
                            mm(psa[0:Lc, :], qT[:, HQ[h][kc], csl], CTb[:, h, kc, :], kc == 0, kc == 3,
                               [("qT", h), ("Cb", h)], pka)
                        act(tmpA[h][0:Lc], psa[0:Lc, :], AF.Identity, [pka, "den"], [("hh", h)], scale=dd[0:Lc, 3, h:h + 1])
                    for h in range(H):
                        psb, pkb_ = PS()
                        mm(psb[0:Lc, :], SpT[h][0:Lc, 0:Lc], vtok[h][0:Lc], True, True, [("SpT", h), ("vtok", h)], pkb_)
                        stt(tmpA[h][0:Lc], psb[0:Lc, :], dd[0:Lc, 2, h:h + 1], tmpA[h][0:Lc], ALU.mult, ALU.add,
                            [pkb_, "den", ("hh", h)], [("hh", h)])
                    for h in range(H):
                        tt(tmpA[h][0:Lc], tmpA[h][0:Lc], o_t[0:Lc, h * DH:(h + 1) * DH], ALU.mult, [("hh", h), ok], [("hh", h)])
                    for kc in range(4):
                        for h in range(H):
                            psc2, pkc2 = PS()
                            mm(psc2[:, :], kp[h][0:Lc, kc * 128:(kc + 1) * 128], vtok[h][0:Lc], True, True,
                               [("kp", h), ("vtok", h)], pkc2)
                            stt(CT32[:, h, kc, :], CT32[:, h, kc, :], decb[:, ci, h:h + 1], psc2[:, :], ALU.mult, ALU.add,
                                [("C32", h), ("decb", ci), pkc2], [("C32", h)])
                            act(CTb[:, h, kc, :], CT32[:, h, kc, :], AF.Copy, [("C32", h)], [("Cb", h)])
                    for h in range(H):
                        psn2, pkn2 = PS()
                        for kc in range(4):
                            mm(psn2[:, kc:kc + 1], kp[h][0:Lc, kc * 128:(kc + 1) * 128], ones_b[0:Lc, 0:1], True, True,
                               [("kp", h), "constb"], pkn2)
                        stt(n32[:, h, :], n32[:, h, :], decb[:, ci, h:h + 1], psn2[:, 0:4], ALU.mult, ALU.add,
                            ["n32", ("decb", ci), pkn2], ["n32"])
                        cp(nb_[:, h, :], n32[:, h, :], ["n32"], ["nb"])
                    if last:
                        for h in range(H):
                            dma(oCT[j, s, h].rearrange("(c p) v -> p c v", p=128), CT32[:, h], [("C32", h)], (),
                                "st_C%d" % h)
                        cp(nout[:, s * 16:(s + 1) * 16].rearrange("p (h c) -> p h c", h=H), n32, ["n32"], ["nout"])

            def rec_b(bi, cix):
                g0, segs = blocks[bi]
                b2 = bi % 2
                for ci, (sg, ch, Lc) in [(cix, BP[bi][cix])]:
                    s = sg["seq"]
                    cs = sg["c0"] + ch * Lc
                    csl = slice(cs, cs + Lc)
                    rs = slice(ci * Lc, (ci + 1) * Lc)
                    gcol = g0 + cs
                    first = sg["first"] and ch == 0
                    last = sg["last"] and ch == sg["L"] // Lc - 1
                    for h in range(H):
                        S.op("dve", lambda e, o=bst[0:Lc, h, 0:6], i=tmpA[h][0:Lc]: e.bn_stats(out=o, in_=i), [("hh", h)],
                             [("bst", h)])
                    for h in range(H):
                        S.op("dve", lambda e, o=bst[0:Lc, h, 6:8], i=bst[0:Lc, h, 0:6]: e.bn_aggr(out=o, in_=i),
                             [("bst", h)], [("bst", h)])
                    BK = [("bst", h) for h in range(H)]
                    act(bst[0:Lc, :, 7], bst[0:Lc, :, 7], AF.Sqrt, BK, BK, bias=EPS, scale=1.0)
                    recip(bst[0:Lc, :, 7], bst[0:Lc, :, 7], BK, BK)
                    for h in range(H):
                        ts(hn[h][0:Lc], tmpA[h][0:Lc], bst[0:Lc, h, 6:7], bst[0:Lc, h, 7:8], ALU.subtract, ALU.mult,
                           [("hh", h), ("bst", h)], [("hn", h)])
                    for h in range(H):
                        psh, pkh = PS()
                        phb = psh.bitcast(BF16)
                        for vc in range(4):
                            tr(phb[:, vc * Lc:(vc + 1) * Lc], hn[h][0:Lc, vc * 128:(vc + 1) * 128], ident_b[0:Lc, 0:Lc],
                               [("hn", h), "constb"], pkh)
                        for vc in range(4):
                            cc = h * 4 + vc
                            stt(t1[:, h, vc * Lc:(vc + 1) * Lc], phb[:, vc * Lc:(vc + 1) * Lc], sm("ml_gn_g", j * 16 + cc),
                                sxc[:, cc, csl], ALU.mult, ALU.add, [pkh, "smallp", ("sxc", cc)], [("t1", h)])
                    for h in range(H):
                        tt(yv[0][:, h * 4:(h + 1) * 4, csl], t1[:, h, 0:4 * Lc].rearrange("p (a t) -> p a t", a=4),
                           zt[0][:, h * 4:(h + 1) * 4, csl], ALU.mult, [("t1", h), ("zt", 0)], [("yv", 0)])

            def block_end(bi):
                g0, segs = blocks[bi]
                b2 = bi % 2
                if bi + 1 < len(blocks):
                    gn_ = blocks[bi + 1][0]
                    dma(zt[0], ZZ_v[:, :, gn_:gn_ + NB], (), [("zt", 0)], "ld_zt0")
                dma(YY_v[:, :, g0:g0 + NB], yv[0], [("yv", 0)], (), "st_yv0")

            block_part(0)
            sxc_part(0)
            for bi in range(len(blocks)):
                nch = len(BP[bi])
                for cix in range(nch):
                    rec_a(bi, cix)
                    if cix == nch - 1 and bi + 1 < len(blocks):
                        block_part(bi + 1)
                    rec_b(bi, cix)
                block_end(bi)
                if bi + 1 < len(blocks):
                    sxc_part(bi + 1)
            dma(onT[j], nout, ["nout"], (), "st_nout")
            dma(omT[j], mout, ["mout"], (), "st_mout")
            S.barrier()
            psw_[0] = 6
            A.release()

        def mlstm_C(l, final):
            j = l // 2
            NB = 256
            blocks = make_blocks(NB)
            A.mark()
            w_out = A.alloc([16, D], BF16)
            dma(w_out, ml_w_out[j].rearrange("(c p) e -> p c e", p=128), (), ["w_out"], "ld_w_out", eng="pool")
            xts = [A.alloc([8, NB], F32) for _ in range(2)]
            yts = [A.alloc([16, NB], BF16) for _ in range(2)]
            xsq = A.alloc([8, NB], BF16)
            rstd = A.alloc([NB], F32)
            yfin = [A.alloc([8, NB], F32) for _ in range(2)]

            def loads(bi):
                g0 = blocks[bi][0]
                load_x(xts[bi % 2], ("x", bi % 2), xres_v, g0, NB)
                dma(yts[bi % 2], YY_v[:, :, g0:g0 + NB], (), [("y", bi % 2)], "ld_y%d" % (bi % 2))

            loads(0)
            for bi, (g0, segs) in enumerate(blocks):
                b2 = bi % 2
                xt, xk = xts[b2], ("x", b2)
                if bi + 1 < len(blocks):
                    loads(bi + 1)
                for dc in range(8):
                    ps, pk = PS()
                    for c in range(16):
                        mm(ps[:, 0:NB], w_out[:, c, dc * 128:(dc + 1) * 128], yts[b2][:, c, :], c == 0, c == 15,
                           [("y", b2), "w_out"], pk)
                    for sg in segs:
                        c0, L, s = sg["c0"], sg["L"], sg["seq"]
                        stt(xt[:, dc, c0:c0 + L], ps[:, c0:c0 + L], modt[:, l, 16 + dc, s:s + 1], xt[:, dc, c0:c0 + L],
                            ALU.mult, ALU.add, [pk, "modt", xk], [xk])
                if not final:
                    dma(xres_v[:, :, g0:g0 + NB], xt[:, :, 0:NB], [xk], (), "st_x%d" % b2)
                else:
                    act(xsq, xt, AF.Square, [xk], ["xsq"])
                    ps, pk = PS()
                    for c in range(8):
                        mm(ps[:, 0:NB], ones_b, xsq[:, c, :], c == 0, c == 7, ["xsq", "constb"], pk)
                    act(rstd, ps[:, 0:NB], AF.Sqrt, [pk], ["rstd"], bias=EPS, scale=1.0 / D)
                    recip(rstd, rstd, ["rstd"], ["rstd"])
                    for c in range(8):
                        stt(yfin[b2][:, c, :], xt[:, c, :], sm("final_g", c), rstd, ALU.mult, ALU.mult,
                            [xk, "smallp", "rstd"], [("yfin", b2)])
                    dma(yT_v[:, :, g0:g0 + NB], yfin[b2], [("yfin", b2)], (), "st_yfin%d" % b2)
            S.barrier()
            A.release()

        src = xT_v
        for l in range(nlayers):
            if l % 2 == 0:
                conv_layer(l, src)
            else:
                mlstm_A(l, src)
                mlstm_B(l)
                mlstm_C(l, final=(l == DEPTH - 1))
            src = xres_v
        if nlayers < DEPTH:
            A.mark()
            xt = A.alloc([8, 256], F32)
            for b in range(NT // 256):
                dma(xt, xres_v[:, :, b * 256:(b + 1) * 256], (), ["xdbg"], "ld_dbg")
                dma(yT_v[:, :, b * 256:(b + 1) * 256], xt, ["xdbg"], (), "st_dbg")
            A.release()
        S.emit()
    return nc


def _vec(v):
    v = np.asarray(v, np.float32)
    lead = v.shape[:-1]
    c = v.shape[-1] // 128
    return np.moveaxis(v.reshape(lead + (c, 128)), -1, 0)


def kernel(x_prompt, x_sample, c_prompt, c_sample, state_conv, state_mconv, state_C, state_n, state_m,
           norm_g, ada_w, ada_b, cv_w_in, cv_w_dw, cv_b_dw, cv_ln_g, cv_ln_b, cv_w_out,
           ml_w_in, ml_w_conv, ml_b_conv, ml_w_q, ml_w_k, ml_w_v, ml_w_gate, ml_b_gate,
           ml_gn_g, ml_skip, ml_w_out, final_g, _nlayers=DEPTH):
    f = lambda a: np.ascontiguousarray(np.asarray(a, np.float32))
    smallp = np.zeros((128, NSMALL), np.float32)

    def put(name, arr):
        arr = np.asarray(arr, np.float32).reshape(128, -1)
        smallp[:, _SM[name]:_SM[name] + arr.shape[1]] = arr

    put("norm_g", _vec(norm_g))
    put("ada_b", _vec(ada_b))
    put("cv_w_dw", np.transpose(_vec(cv_w_dw), (0, 1, 3, 2)))
    put("cv_b_dw", _vec(cv_b_dw))
    put("cv_ln_g", _vec(cv_ln_g))
    put("cv_ln_b", _vec(cv_ln_b))
    put("ml_w_conv", np.transpose(_vec(ml_w_conv), (0, 1, 3, 2)))
    put("ml_b_conv", _vec(ml_b_conv))
    put("ml_gn_g", _vec(ml_gn_g))
    put("ml_skip", _vec(ml_skip))
    put("final_g", _vec(final_g))
    bg = np.zeros((128, 2), np.float32)
    bg[0:8, :] = np.asarray(ml_b_gate, np.float32).T
    put("b_gate", bg)
    constp = np.zeros((128, NCONST), np.float32)
    constp[:, C_ID:C_ID + 128] = np.eye(128, dtype=np.float32)
    s_idx = np.arange(128)[:, None]
    t_idx = np.arange(128)[None, :]
    constp[:, C_MASK:C_MASK + 128] = np.where(s_idx <= t_idx, 0.0, NEG)
    for h in range(4):
        constp[h, C_SELROW + h * 128:C_SELROW + (h + 1) * 128] = 1.0
        constp[4 + h, C_SELF + h] = 1.0
    constp[:, C_ONES:C_ONES + 128] = 1.0
    wg = np.asarray(ml_w_gate, np.float32).reshape(2, 3, 16, 128, 8)
    wgate = np.ascontiguousarray(np.transpose(wg, (0, 3, 1, 2, 4)).reshape(2, 128, 384))

    x_prompt = np.asarray(x_prompt, np.float32)
    x_sample = np.asarray(x_sample, np.float32)
    shared = dict(smallp=smallp, constp=constp, ada_w=f(ada_w), cv_w_in=f(cv_w_in), cv_w_out=f(cv_w_out),
                  ml_w_in=f(ml_w_in), ml_w_q=f(ml_w_q), ml_w_k=f(ml_w_k), ml_w_v=f(ml_w_v), wgate=wgate,
                  ml_w_out=f(ml_w_out))
    in_maps = []
    for c in range(8):
        sl = slice(4 * c, 4 * c + 4)
        xs = np.concatenate([x_prompt[c], x_sample[sl].reshape(NS * LS, D)], axis=0)
        cc = np.concatenate([np.asarray(c_prompt, np.float32)[c:c + 1], np.asarray(c_sample, np.float32)[sl]], axis=0)
        m = dict(shared)
        m["xT"] = np.ascontiguousarray(xs.T)
        m["cT"] = np.ascontiguousarray(cc.T)
        m["sconvT"] = np.ascontiguousarray(np.transpose(np.asarray(state_conv, np.float32)[:, sl], (0, 1, 3, 2)))
        m["smconvT"] = np.ascontiguousarray(np.transpose(np.asarray(state_mconv, np.float32)[:, sl], (0, 1, 3, 2)))
        m["sCT"] = np.ascontiguousarray(np.transpose(np.asarray(state_C, np.float32)[:, sl], (0, 1, 2, 4, 3)))
        sn = np.asarray(state_n, np.float32)[:, sl].reshape(2, NS, H, 4, 128)
        m["snT"] = np.ascontiguousarray(np.transpose(sn, (0, 4, 1, 2, 3)).reshape(2, 128, NS * H * 4))
        m["smT"] = np.ascontiguousarray(np.transpose(np.asarray(state_m, np.float32)[:, sl], (0, 2, 1)))
        in_maps.append(m)
    if _nlayers == "prep":
        return in_maps
    nc = build(_nlayers)
    res = run_bass_kernel_spmd(nc, in_maps, core_ids=list(range(8)))
    return assemble(res.results)


def assemble(R, cores=range(8)):
    y_prompt = np.empty((8, TP, D), np.float32)
    y_sample = np.empty((32, LS, D), np.float32)
    p_conv = np.empty((2, 8, CK - 1, D), np.float32)
    s_conv = np.empty((2, 32, CK - 1, D), np.float32)
    p_mconv = np.empty((2, 8, MK - 1, MW), np.float32)
    s_mconv = np.empty((2, 32, MK - 1, MW), np.float32)
    p_C = np.empty((2, 8, H, DH, DH), np.float32)
    s_C = np.empty((2, 32, H, DH, DH), np.float32)
    p_n = np.empty((2, 8, H, DH), np.float32)
    s_n = np.empty((2, 32, H, DH), np.float32)
    p_m = np.empty((2, 8, H), np.float32)
    s_m = np.empty((2, 32, H), np.float32)
    for c in cores:
        r = R[c]
        sl = slice(4 * c, 4 * c + 4)
        yt = np.asarray(r["yT"]).T
        y_prompt[c] = yt[:TP]
        y_sample[sl] = yt[TP:].reshape(NS, LS, D)
        oc = np.transpose(np.asarray(r["oconvT"]), (0, 1, 3, 2))
        p_conv[:, c] = oc[:, 0]
        s_conv[:, sl] = oc[:, 1:]
        om = np.transpose(np.asarray(r["omconvT"]), (0, 1, 3, 2))
        p_mconv[:, c] = om[:, 0]
        s_mconv[:, sl] = om[:, 1:]
        oC = np.transpose(np.asarray(r["oCT"]), (0, 1, 2, 4, 3))
        p_C[:, c] = oC[:, 0]
        s_C[:, sl] = oC[:, 1:]
        on = np.transpose(np.asarray(r["onT"]).reshape(2, 128, NSEQ, H, 4), (0, 2, 3, 4, 1)).reshape(2, NSEQ, H, DH)
        p_n[:, c] = on[:, 0]
        s_n[:, sl] = on[:, 1:]
        omm = np.transpose(np.asarray(r["omT"]), (0, 2, 1))
        p_m[:, c] = omm[:, 0]
        s_m[:, sl] = omm[:, 1:]
    return (y_prompt, y_sample, p_conv, p_mconv, p_C, p_n, p_m, s_conv, s_mconv, s_C, s_n, s_m)
```
